# Optimizing a Trainium2 kernel written in Bass

```python
import jax, jax.numpy as jnp
from jax import lax
import numpy as np

D_MODEL = 1024
BATCH = 2
SEQ = 8192
DEPTH = 4

N_MIXERS = 2
N_HGRN_LAYERS = (DEPTH + 1) // 2
N_MLA_LAYERS = DEPTH // 2
RMS_EPS = 1e-6

HGRN_EXPAND = 128
HGRN_HEADS = D_MODEL // HGRN_EXPAND
HGRN_VDIM = D_MODEL // HGRN_HEADS
HGRN_CHUNK = 64

MLA_HEADS = D_MODEL // 128
MLA_NOPE = 128
MLA_ROPE = 64
MLA_QK_HEAD = MLA_NOPE + MLA_ROPE
MLA_V_HEAD = 128
MLA_Q_LORA = D_MODEL // 4
MLA_KV_LORA = D_MODEL // 4
ROPE_THETA = 10000.0
Q_BLOCK = 128

D_FF = 2816
FFN_CONV = 3

kernel_name = "hybrid_hgrn2_mla_convffn_trunk"


def _rmsnorm(x, gain):
    xf = x.astype(jnp.float32)
    y = xf * lax.rsqrt(jnp.mean(xf * xf, axis=-1, keepdims=True) + RMS_EPS)
    return (y * gain.astype(jnp.float32)).astype(x.dtype)


def _chunk_gated_recurrence(q, k, v, log_f):
    B, S, H, Dk = q.shape
    Dv = v.shape[-1]
    C = HGRN_CHUNK
    N = S // C

    def to_chunks(t):
        return t.astype(jnp.float32).reshape(B, N, C, H, t.shape[-1]).transpose(1, 0, 3, 2, 4)

    causal = jnp.tril(jnp.ones((C, C), dtype=bool))[:, :, None]

    def step(state, inp):
        qc, kc, vc, gc = inp
        G = jnp.cumsum(gc, axis=2)
        o_inter = jnp.einsum("bhtd,bhde->bhte", qc * jnp.exp(G), state)
        diff = G[:, :, :, None, :] - G[:, :, None, :, :]
        decay = jnp.where(causal, jnp.exp(jnp.where(causal, diff, 0.0)), 0.0)
        scores = jnp.einsum("bhtd,bhsd,bhtsd->bhts", qc, kc, decay)
        o_intra = jnp.einsum("bhts,bhse->bhte", scores, vc)
        G_last = G[:, :, -1:, :]
        new_state = state * jnp.exp(G_last[:, :, 0, :, None]) + jnp.einsum(
            "bhsd,bhse->bhde", kc * jnp.exp(G_last - G), vc)
        return new_state, o_inter + o_intra

    state0 = jnp.zeros((B, H, Dk, Dv), jnp.float32)
    _, o = lax.scan(step, state0, (to_chunks(q), to_chunks(k), to_chunks(v), to_chunks(log_f)))
    return o.transpose(1, 0, 3, 2, 4).reshape(B, S, H, Dv)


def _hgrn2_mixer(h, w_in, lower_bound, out_norm, w_out):
    B, S, _ = h.shape
    q, f, i, g = jnp.split(h @ w_in, 4, axis=-1)

    def heads(t):
        return t.reshape(B, S, HGRN_HEADS, -1)

    lb = lower_bound.astype(jnp.float32)
    forget = lb + (1.0 - lb) * jax.nn.sigmoid(f.astype(jnp.float32))
    key_in = 1.0 - forget
    log_f = jnp.log(forget)
    o = _chunk_gated_recurrence(heads(jax.nn.silu(q)), heads(key_in), heads(i), heads(log_f))
    o = _rmsnorm(o, out_norm) * jax.nn.silu(heads(g.astype(jnp.float32)))
    return o.reshape(B, S, D_MODEL).astype(h.dtype) @ w_out


def _rope_tail(x, cos, sin):
    x_nope, x1, x2 = jnp.split(x, [MLA_NOPE, MLA_NOPE + MLA_ROPE // 2], axis=-1)
    xf1 = x1.astype(jnp.float32)
    xf2 = x2.astype(jnp.float32)
    rot = jnp.concatenate([xf1 * cos - xf2 * sin, xf2 * cos + xf1 * sin], axis=-1).astype(x.dtype)
    return jnp.concatenate([x_nope, rot], axis=-1)


def _causal_block_attention(q, k, v):
    B, S, H, Dqk = q.shape
    nb = S // Q_BLOCK
    scale = Dqk ** -0.5
    q_blocks = q.reshape(B, nb, Q_BLOCK, H, Dqk).transpose(1, 0, 3, 2, 4)
    k_t = k.transpose(0, 2, 1, 3)
    v_t = v.transpose(0, 2, 1, 3)
    key_pos = jnp.arange(S)

    def attend(args):
        q_blk, blk = args
        s = jnp.einsum("bhqd,bhkd->bhqk", q_blk, k_t).astype(jnp.float32) * scale
        q_pos = blk * Q_BLOCK + jnp.arange(Q_BLOCK)
        s = jnp.where(key_pos[None, :] <= q_pos[:, None], s, -jnp.inf)
        p = jax.nn.softmax(s, axis=-1).astype(v_t.dtype)
        return jnp.einsum("bhqk,bhkv->bhqv", p, v_t)

    o = lax.map(attend, (q_blocks, jnp.arange(nb)))
    return o.transpose(1, 0, 3, 2, 4).reshape(B, S, H, -1)


def _mla_mixer(h, cos, sin, w_in, q_a_norm, w_q_up, kv_a_norm, w_kv_up, q_norm, k_norm, w_out):
    B, S, _ = h.shape
    c_q, c_kv, k_rope = jnp.split(h @ w_in, [MLA_Q_LORA, MLA_Q_LORA + MLA_KV_LORA], axis=-1)
    q = (_rmsnorm(c_q, q_a_norm) @ w_q_up).reshape(B, S, MLA_HEADS, MLA_QK_HEAD)
    kv = (_rmsnorm(c_kv, kv_a_norm) @ w_kv_up).reshape(B, S, MLA_HEADS, MLA_NOPE + MLA_V_HEAD)
    k_nope, v = jnp.split(kv, [MLA_NOPE], axis=-1)
    k = jnp.concatenate(
        [k_nope, jnp.broadcast_to(k_rope[:, :, None, :], (B, S, MLA_HEADS, MLA_ROPE))], axis=-1)
    q = _rope_tail(_rmsnorm(q, q_norm), cos, sin)
    k = _rope_tail(_rmsnorm(k, k_norm), cos, sin)
    o = _causal_block_attention(q, k, v)
    return o.reshape(B, S, MLA_HEADS * MLA_V_HEAD) @ w_out


def _conv_ffn(h, w_up, conv_w, conv_b, w_down):
    S = h.shape[1]
    u = h @ w_up
    u_pad = jnp.pad(u, ((0, 0), (FFN_CONV - 1, 0), (0, 0)))
    y = conv_b.astype(u.dtype)
    for j in range(FFN_CONV):
        y = y + u_pad[:, j:j + S, :] * conv_w[j]
    gate, up = jnp.split(y, 2, axis=-1)
    return (jax.nn.silu(gate) * up) @ w_down


def setup_inputs(seed: int = 0) -> dict:
    key = jax.random.key(seed)
    ks = iter(jax.random.split(key, 32))

    def nrm(shape, scale):
        return jax.random.normal(next(ks), shape, jnp.float32) * scale

    def gain(shape):
        return 1.0 + 0.02 * jax.random.normal(next(ks), shape, jnp.float32)

    D = D_MODEL
    x = jax.random.normal(next(ks), (BATCH, SEQ, D), jnp.float32)
    offsets = jax.random.randint(next(ks), (BATCH, 1), 0, 4096, dtype=jnp.int32)
    positions = offsets + jnp.arange(SEQ, dtype=jnp.int32)[None, :]
    return {
        "x": x,
        "positions": positions,
        "norm_mix": gain((DEPTH, D)),
        "norm_ffn": gain((DEPTH, D)),
        "hgrn_w_in": nrm((N_HGRN_LAYERS, D, 4 * D), D ** -0.5),
        "hgrn_lower_bounds": nrm((N_HGRN_LAYERS, HGRN_HEADS * HGRN_EXPAND), 0.1),
        "hgrn_out_norm": gain((N_HGRN_LAYERS, HGRN_VDIM)),
        "hgrn_w_out": nrm((N_HGRN_LAYERS, D, D), D ** -0.5),
        "mla_w_in": nrm((N_MLA_LAYERS, D, MLA_Q_LORA + MLA_KV_LORA + MLA_ROPE), D ** -0.5),
        "mla_q_a_norm": gain((N_MLA_LAYERS, MLA_Q_LORA)),
        "mla_w_q_up": nrm((N_MLA_LAYERS, MLA_Q_LORA, MLA_HEADS * MLA_QK_HEAD), MLA_Q_LORA ** -0.5),
        "mla_kv_a_norm": gain((N_MLA_LAYERS, MLA_KV_LORA)),
        "mla_w_kv_up": nrm((N_MLA_LAYERS, MLA_KV_LORA, MLA_HEADS * (MLA_NOPE + MLA_V_HEAD)), MLA_KV_LORA ** -0.5),
        "mla_q_norm": gain((N_MLA_LAYERS, MLA_QK_HEAD)),
        "mla_k_norm": gain((N_MLA_LAYERS, MLA_QK_HEAD)),
        "mla_w_out": nrm((N_MLA_LAYERS, MLA_HEADS * MLA_V_HEAD, D), (MLA_HEADS * MLA_V_HEAD) ** -0.5),
        "ffn_w_up": nrm((DEPTH, D, 2 * D_FF), D ** -0.5),
        "ffn_conv_w": nrm((DEPTH, FFN_CONV, 2 * D_FF), FFN_CONV ** -0.5),
        "ffn_conv_b": nrm((DEPTH, 2 * D_FF), 0.01),
        "ffn_w_down": nrm((DEPTH, D_FF, D), D_FF ** -0.5),
    }


def reference(x, positions, norm_mix, norm_ffn, hgrn_w_in, hgrn_lower_bounds, hgrn_out_norm, hgrn_w_out,
              mla_w_in, mla_q_a_norm, mla_w_q_up, mla_kv_a_norm, mla_w_kv_up, mla_q_norm, mla_k_norm,
              mla_w_out, ffn_w_up, ffn_conv_w, ffn_conv_b, ffn_w_down):
    lb_soft = jax.nn.softmax(hgrn_lower_bounds.astype(jnp.float32), axis=0)
    lower_bounds = jnp.cumsum(lb_soft, axis=0) - lb_soft[0:1]
    inv_freq = ROPE_THETA ** (-jnp.arange(0, MLA_ROPE, 2, dtype=jnp.float32) / MLA_ROPE)
    ang = positions.astype(jnp.float32)[..., None] * inv_freq
    cos = jnp.cos(ang)[:, :, None, :]
    sin = jnp.sin(ang)[:, :, None, :]

    for layer in range(DEPTH):
        h = _rmsnorm(x, norm_mix[layer])
        j = layer // N_MIXERS
        if layer % N_MIXERS == 0:
            y = _hgrn2_mixer(h, hgrn_w_in[j], lower_bounds[j], hgrn_out_norm[j], hgrn_w_out[j])
        else:
            y = _mla_mixer(h, cos, sin, mla_w_in[j], mla_q_a_norm[j], mla_w_q_up[j], mla_kv_a_norm[j],
                           mla_w_kv_up[j], mla_q_norm[j], mla_k_norm[j], mla_w_out[j])
        x = x + y
        h = _rmsnorm(x, norm_ffn[layer])
        x = x + _conv_ffn(h, ffn_w_up[layer], ffn_conv_w[layer], ffn_conv_b[layer], ffn_w_down[layer])
    return x
```

```python
import math
from concourse.bass_utils import run_bass_kernel_spmd


import numpy as np
import os
from contextlib import ExitStack
import concourse.bass as bass
import concourse.mybir as mybir

F32 = mybir.dt.float32
BF16 = mybir.dt.bfloat16
I32 = mybir.dt.int32
AF = mybir.ActivationFunctionType
ALU = mybir.AluOpType
AX = mybir.AxisListType

ENGS = ("pe", "act", "dve", "pool", "sp")
BLK = {"pe": "tensor", "act": "scalar", "dve": "vector", "pool": "gpsimd", "sp": "sync"}
GEN = 16000
KDMA = 8


class Op:
    __slots__ = ("eng", "fn", "dma", "seq", "deps", "signal", "ordinal", "didx", "dsem", "dval")

    def __init__(self, eng, fn, dma):
        self.eng = eng
        self.fn = fn
        self.dma = dma
        self.deps = []
        self.signal = False
        self.ordinal = 0
        self.didx = -1


class Prog:
    def __init__(self, nc, same_sync=True):
        self.nc = nc
        self.same_sync = same_sync
        self.ops = {e: [] for e in ENGS}
        self.last_w = {}
        self.rd_eng = {}
        self.rd_dma = {}
        self.ndma = {e: 0 for e in ENGS}
        self.stack = ExitStack()
        self._nm = 0

    def sb(self, shape, dt, name=None):
        self._nm += 1
        return self.stack.enter_context(self.nc.sbuf_tensor(name or f"sb{self._nm}", list(shape), dt))

    def ps(self, shape=(128, 512), dt=F32, name=None):
        self._nm += 1
        return self.stack.enter_context(self.nc.psum_tensor(name or f"ps{self._nm}", list(shape), dt))

    def add(self, eng, fn, reads=(), writes=(), dma=False, force=False):
        self.nadd = getattr(self, "nadd", 0) + 1
        lim = int(os.environ.get("FW_LIMIT", "0"))
        if lim and self.nadd > lim and not force:
            return None
        excl = getattr(self, "excl", None)
        if excl:
            mv = [b for b in reads if b in excl]
            if mv:
                reads = [b for b in reads if b not in excl]
                writes = list(writes) + mv
        op = Op(eng, fn, dma)
        op.seq = len(self.ops[eng])
        if dma:
            op.didx = self.ndma[eng]
            self.ndma[eng] += 1
        deps = {}
        for b in reads:
            lw = self.last_w.get(b)
            if lw is not None:
                deps[id(lw)] = lw
        for b in writes:
            lw = self.last_w.get(b)
            if lw is not None:
                deps[id(lw)] = lw
            for r in self.rd_eng.get(b, {}).values():
                deps[id(r)] = r
            for r in self.rd_dma.get(b, ()):
                deps[id(r)] = r
        for d in deps.values():
            if d is op:
                continue
            if d.dma:
                op.deps.append(d)
            elif d.eng == eng:
                if eng == "pe" and not dma:
                    continue
                if self.same_sync or dma:
                    d.signal = True
                    op.deps.append(d)
            else:
                d.signal = True
                op.deps.append(d)
        for b in reads:
            if dma:
                self.rd_dma.setdefault(b, []).append(op)
            else:
                self.rd_eng.setdefault(b, {})[eng] = op
        for b in writes:
            self.last_w[b] = op
            self.rd_eng[b] = {}
            self.rd_dma[b] = []
        self.ops[eng].append(op)
        return op

    def dma(self, eng, out, in_, reads=(), writes=(), force=False):
        return self.add(eng, lambda e: e.dma_start(out=out, in_=in_), reads, writes, dma=True, force=force)

    def finish(self, ops, eng="sp"):
        op = Op(eng, None, False)
        op.seq = len(self.ops[eng])
        for d in ops:
            if d is None:
                continue
            if not d.dma:
                d.signal = True
            op.deps.append(d)
        self.ops[eng].append(op)

    def emit(self):
        nc = self.nc
        st = self.stack
        csem = {}
        for e in ENGS:
            n = 0
            for op in self.ops[e]:
                if (not op.dma) and op.signal:
                    n += 1
                    op.ordinal = n
            ngen = (n + GEN - 1) // GEN
            csem[e] = [st.enter_context(nc.semaphore(f"c_{e}_{g}")) for g in range(ngen)]
        dsem = {}
        for e in ENGS:
            k = min(KDMA, self.ndma[e])
            dsem[e] = [st.enter_context(nc.semaphore(f"d_{e}_{i}")) for i in range(k)]
            dl = [op for op in self.ops[e] if op.dma]
            for op in dl:
                op.dsem = dsem[e][op.didx % KDMA]
                op.dval = 16 * (op.didx // KDMA + 1)
        self._dl = {e: [op for op in self.ops[e] if op.dma] for e in ENGS}

        def emit_engine(e, E):
            waited = {f: -1 for f in ENGS}
            wdma = set()
            for op in self.ops[e]:
                for d in op.deps:
                    if d.dma:
                        if id(d) not in wdma:
                            E.wait_ge(d.dsem, d.dval)
                            wdma.add(id(d))
                    else:
                        if d.seq > waited[d.eng]:
                            g = (d.ordinal - 1) // GEN
                            E.wait_ge(csem[d.eng][g], (d.ordinal - 1) % GEN + 1)
                            waited[d.eng] = d.seq
                if op.fn is None:
                    continue
                if op.dma:
                    if op.didx >= KDMA:
                        prev = self._dl[e][op.didx - KDMA]
                        if id(prev) not in wdma:
                            E.wait_ge(prev.dsem, prev.dval)
                            wdma.add(id(prev))
                    ins = op.fn(E)
                    ins.then_inc(op.dsem, 16)
                else:
                    ins = op.fn(E)
                    if op.signal:
                        g = (op.ordinal - 1) // GEN
                        ins.then_inc(csem[e][g], 1)

        with nc.Block() as block:
            for e in ENGS:
                if not self.ops[e]:
                    continue
                getattr(block, BLK[e])(lambda E, e=e: emit_engine(e, E))
        self.stack.close()
        return nc


P_TW = 1024
P_HALO = 2
P_TC = P_TW + P_HALO
P_NT = 2
P_COLS = P_NT * P_TW + P_HALO
NTILES3 = [(0, 2), (2, 514), (514, 1026)]
NV = 8 + 44 * 3 + 44 + 1


def build_post(eps=1e-6):
    nc = bass.Bass("TRN2", target_bir_lowering=False)
    xT = nc.dram_tensor("xT", [1024, P_COLS], F32, kind="ExternalInput").ap()
    oT = nc.dram_tensor("oT", [1024, P_COLS], BF16, kind="ExternalInput").ap()
    w_out = nc.dram_tensor("w_out", [8, 128, 1024], F32, kind="ExternalInput").ap()
    w_up = nc.dram_tensor("w_up", [22, 128, 2048], F32, kind="ExternalInput").ap()
    w_down = nc.dram_tensor("w_down", [8, 128, 2816], F32, kind="ExternalInput").ap()
    vec = nc.dram_tensor("vec", [128, NV], F32, kind="ExternalInput").ap()
    yT = nc.dram_tensor("yT", [1024, P_NT * P_TW], F32, kind="ExternalOutput").ap()

    p = Prog(nc)
    xt = p.sb([128, 8, P_TC], F32, "xt")
    ot = p.sb([128, 8, P_TC], BF16, "ot")
    ht = p.sb([128, 8, P_TC], BF16, "ht")
    at = p.sb([128, 22, P_TW], BF16, "at")
    sq = [p.sb([128, 512], BF16, f"sq{i}") for i in range(2)]
    rs = p.sb([128, P_TC], F32, "rs")
    ub = [[p.sb([128, P_TC], F32, f"u{i}{h}") for h in range(2)] for i in range(2)]
    yb = [p.sb([128, P_TW], F32, f"y{h}") for h in range(2)]
    wst = [p.sb([128, 2816], F32, f"wst{i}") for i in range(2)]
    wbf = [p.sb([128, 2816], BF16, f"wbf{i}") for i in range(2)]
    vt = p.sb([128, NV], F32, "vt")
    ones = p.sb([128, 128], BF16, "ones")
    epst = p.sb([128, 1], F32, "epst")
    acc = [p.ps(name=f"acc{i}") for i in range(2)]
    ups = [p.ps(name=f"ups{i}") for i in range(4)]

    xT_v = xT.rearrange("(k p) c -> p k c", p=128)
    oT_v = oT.rearrange("(k p) c -> p k c", p=128)
    yT_v = yT.rearrange("(k p) c -> p k c", p=128)

    p.dma("sp", vt[:, :], vec[:, :], writes=["vt"])
    p.add("pool", lambda e: e.memset(ones[:, :], 1.0), writes=["ones"])
    p.add("pool", lambda e: e.memset(epst[:, :], eps), writes=["eps"])

    G0 = 0
    CW0 = 8
    CB0 = 8 + 132
    FL0 = 8 + 132 + 44

    cnt = {"w": 0, "acc": 0, "ups": 0, "sq": 0, "u": 0}
    stores = []

    def load_w(src_ap, ncols):
        b = cnt["w"] % 2
        cnt["w"] += 1
        p.dma("sp", wst[b][:, 0:ncols], src_ap, writes=[("wst", b)])
        p.add("pool", lambda e: e.tensor_copy(wbf[b][:, 0:ncols], wst[b][:, 0:ncols]),
              reads=[("wst", b)], writes=[("wbf", b)])
        return b

    for t in range(P_NT):
        c0 = t * P_TW
        for k in range(8):
            p.dma("sp", xt[:, k, :], xT_v[:, k, c0:c0 + P_TC], writes=[("xt", k)])
        for k in range(8):
            p.dma("sp", ot[:, k, :], oT_v[:, k, c0:c0 + P_TC], writes=[("ot", k)])
        for m in range(8):
            b = load_w(w_out[m], 1024)
            for (a0, a1) in NTILES3:
                n = a1 - a0
                pb = cnt["acc"] % 2
                cnt["acc"] += 1
                for k in range(8):
                    p.add("pe", lambda e, pb=pb, n=n, b=b, k=k, a0=a0, a1=a1: e.matmul(
                        acc[pb][:, 0:n], wbf[b][:, k * 128:(k + 1) * 128], ot[:, k, a0:a1],
                        start=(k == 0), stop=(k == 7)),
                        reads=[("wbf", b), ("ot", k)], writes=[("acc", pb)])
                p.add("dve", lambda e, pb=pb, n=n, m=m, a0=a0, a1=a1: e.tensor_tensor(
                    xt[:, m, a0:a1], xt[:, m, a0:a1], acc[pb][:, 0:n], ALU.add),
                    reads=[("acc", pb), ("xt", m)], writes=[("xt", m)])
        for ti, (a0, a1) in enumerate(NTILES3):
            n = a1 - a0
            pb = cnt["acc"] % 2
            cnt["acc"] += 1
            for k in range(8):
                s = cnt["sq"] % 2
                cnt["sq"] += 1
                p.add("act", lambda e, s=s, n=n, k=k, a0=a0, a1=a1: e.activation(
                    sq[s][:, 0:n], xt[:, k, a0:a1], AF.Square),
                    reads=[("xt", k)], writes=[("sq", s)])
                p.add("pe", lambda e, pb=pb, n=n, s=s, k=k: e.matmul(
                    acc[pb][:, 0:n], ones[:, :], sq[s][:, 0:n], start=(k == 0), stop=(k == 7)),
                    reads=[("sq", s), "ones"], writes=[("acc", pb)])
            p.add("act", lambda e, pb=pb, n=n, a0=a0, a1=a1: e.activation(
                rs[:, a0:a1], acc[pb][:, 0:n], AF.Sqrt, bias=epst[:, 0:1], scale=1.0 / 1024.0),
                reads=[("acc", pb), "eps"], writes=[("rs", ti)])
            p.add("dve", lambda e, a0=a0, a1=a1: e.reciprocal(rs[:, a0:a1], rs[:, a0:a1]),
                  reads=[("rs", ti)], writes=[("rs", ti)])
        for k in range(8):
            p.add("dve", lambda e, k=k: e.scalar_tensor_tensor(
                ht[:, k, :], xt[:, k, :], vt[:, G0 + k:G0 + k + 1], rs[:, :], ALU.mult, ALU.mult),
                reads=[("xt", k), "vt", ("rs", 0), ("rs", 1), ("rs", 2)], writes=[("ht", k)])
            if t == 0:
                p.add("dve", lambda e, k=k: e.tensor_scalar(
                    ht[:, k, 0:P_HALO], ht[:, k, 0:P_HALO], vt[:, FL0:FL0 + 1], None, ALU.mult),
                    reads=[("ht", k), "vt"], writes=[("ht", k)])
        for j in range(22):
            b = load_w(w_up[j], 2048)
            ui = cnt["u"] % 2
            cnt["u"] += 1
            for half in range(2):
                ch = j + 22 * half
                for (a0, a1) in NTILES3:
                    n = a1 - a0
                    r = cnt["ups"] % 4
                    cnt["ups"] += 1
                    for k in range(8):
                        p.add("pe", lambda e, r=r, n=n, b=b, k=k, half=half, a0=a0, a1=a1: e.matmul(
                            ups[r][:, 0:n], wbf[b][:, k * 256 + half * 128:k * 256 + half * 128 + 128],
                            ht[:, k, a0:a1], start=(k == 0), stop=(k == 7)),
                            reads=[("wbf", b), ("ht", k)], writes=[("ups", r)])
                    p.add("act", lambda e, r=r, n=n, ui=ui, half=half, a0=a0, a1=a1: e.activation(
                        ub[ui][half][:, a0:a1], ups[r][:, 0:n], AF.Copy),
                        reads=[("ups", r)], writes=[("u", ui, half)])
                u = ub[ui][half]
                y = yb[half]
                w0 = vt[:, CW0 + 0 * 44 + ch:CW0 + 0 * 44 + ch + 1]
                w1 = vt[:, CW0 + 1 * 44 + ch:CW0 + 1 * 44 + ch + 1]
                w2 = vt[:, CW0 + 2 * 44 + ch:CW0 + 2 * 44 + ch + 1]
                bb = vt[:, CB0 + ch:CB0 + ch + 1]
                p.add("dve", lambda e, u=u, y=y, w2=w2, bb=bb: e.tensor_scalar(
                    y[:, :], u[:, 2:P_TC], w2, bb, ALU.mult, ALU.add),
                    reads=[("u", ui, half), "vt"], writes=[("y", half)])
                p.add("dve", lambda e, u=u, y=y, w1=w1: e.scalar_tensor_tensor(
                    y[:, :], u[:, 1:P_TC - 1], w1, y[:, :], ALU.mult, ALU.add),
                    reads=[("u", ui, half), "vt", ("y", half)], writes=[("y", half)])
                p.add("dve", lambda e, u=u, y=y, w0=w0: e.scalar_tensor_tensor(
                    y[:, :], u[:, 0:P_TC - 2], w0, y[:, :], ALU.mult, ALU.add),
                    reads=[("u", ui, half), "vt", ("y", half)], writes=[("y", half)])
            p.add("act", lambda e: e.activation(yb[0][:, :], yb[0][:, :], AF.Silu),
                  reads=[("y", 0)], writes=[("y", 0)])
            p.add("pool", lambda e, j=j: e.tensor_tensor(at[:, j, :], yb[0][:, :], yb[1][:, :], ALU.mult),
                  reads=[("y", 0), ("y", 1)], writes=[("at", j)])
        for m in range(8):
            b = load_w(w_down[m], 2816)
            for h2 in range(2):
                pb = cnt["acc"] % 2
                cnt["acc"] += 1
                for j in range(22):
                    p.add("pe", lambda e, pb=pb, b=b, j=j, h2=h2: e.matmul(
                        acc[pb][:, :], wbf[b][:, j * 128:(j + 1) * 128], at[:, j, h2 * 512:(h2 + 1) * 512],
                        start=(j == 0), stop=(j == 21)),
                        reads=[("wbf", b), ("at", j)], writes=[("acc", pb)])
                x0 = P_HALO + h2 * 512
                p.add("dve", lambda e, pb=pb, m=m, x0=x0: e.tensor_tensor(
                    xt[:, m, x0:x0 + 512], xt[:, m, x0:x0 + 512], acc[pb][:, :], ALU.add),
                    reads=[("acc", pb), ("xt", m)], writes=[("xt", m)])
            stores.append(p.dma("sp", yT_v[:, m, c0:c0 + P_TW], xt[:, m, P_HALO:P_TC], reads=[("xt", m)]))
    p.finish(stores)
    return p.emit()


def post_layout_weights(w_out, w_up, w_down, g, conv_w, conv_b):
    wo = np.ascontiguousarray(w_out.reshape(8, 128, 8, 128).transpose(2, 1, 0, 3).reshape(8, 128, 1024))
    wu = w_up.reshape(8, 128, 2, 22, 128)
    wu = np.ascontiguousarray(wu.transpose(3, 1, 0, 2, 4).reshape(22, 128, 2048))
    wd = w_down.reshape(22, 128, 8, 128)
    wd = np.ascontiguousarray(wd.transpose(2, 1, 0, 3).reshape(8, 128, 2816))
    vec = np.zeros((128, NV), np.float32)
    vec[:, 0:8] = g.reshape(8, 128).T
    for tap in range(3):
        vec[:, 8 + tap * 44:8 + (tap + 1) * 44] = conv_w[tap].reshape(44, 128).T
    vec[:, 8 + 132:8 + 176] = conv_b.reshape(44, 128).T
    return wo, wu, wd, vec


PR_TW = 512
PR_NT = 2048 // PR_TW


def build_pre(mode, eps=1e-6):
    nc = bass.Bass("TRN2", target_bir_lowering=False)
    xT = nc.dram_tensor("xT", [1024, 2048], F32, kind="ExternalInput").ap()
    vec = nc.dram_tensor("vec", [128, 12], F32, kind="ExternalInput").ap()
    if mode == "m":
        w_in = nc.dram_tensor("w_in", [128, 8 * 640], F32, kind="ExternalInput").ap()
        outT = nc.dram_tensor("latT", [640, 2048], BF16, kind="ExternalOutput").ap()
    else:
        outT = nc.dram_tensor("hT", [1024, 2048], BF16, kind="ExternalOutput").ap()
    p = Prog(nc)
    xt = [p.sb([128, 8, PR_TW], F32, f"xt{i}") for i in range(2)]
    ht = [p.sb([128, 8, PR_TW], BF16, f"ht{i}") for i in range(2)]
    sq = [p.sb([128, PR_TW], BF16, f"sq{i}") for i in range(2)]
    rs = p.sb([128, PR_TW], F32, "rs")
    vt = p.sb([128, 12], F32, "vt")
    ones = p.sb([128, 128], BF16, "ones")
    epst = p.sb([128, 1], F32, "epst")
    acc = [p.ps(name=f"acc{i}") for i in range(2)]
    if mode == "m":
        wst = p.sb([128, 8 * 640], F32, "wst")
        wbf = p.sb([128, 8 * 640], BF16, "wbf")
        lat = p.sb([128, 5, PR_TW], F32, "lat")
        lo = [p.sb([128, 5, PR_TW], BF16, f"lo{i}") for i in range(2)]
        rs2 = p.sb([128, PR_TW], F32, "rs2")
        lps = [p.ps(name=f"lps{i}") for i in range(2)]
        p.dma("sp", wst[:, :], w_in[:, :], writes=["wst"])
        p.add("pool", lambda e: e.tensor_copy(wbf[:, :], wst[:, :]), reads=["wst"], writes=["wbf"])
    xT_v = xT.rearrange("(k p) c -> p k c", p=128)
    oT_v = outT.rearrange("(k p) c -> p k c", p=128)
    p.dma("sp", vt[:, :], vec[:, :], writes=["vt"])
    p.add("pool", lambda e: e.memset(ones[:, :], 1.0), writes=["ones"])
    p.add("pool", lambda e: e.memset(epst[:, :], eps), writes=["eps"])
    cnt = {"acc": 0, "sq": 0, "lps": 0}
    stores = []

    def rms_stat(srcs, skeys, dst, dkey, n_feat):
        pb = cnt["acc"] % 2
        cnt["acc"] += 1
        for i, (src, sk) in enumerate(zip(srcs, skeys)):
            s = cnt["sq"] % 2
            cnt["sq"] += 1
            p.add("act", lambda e, s=s, src=src: e.activation(sq[s][:, :], src, AF.Square), reads=[sk], writes=[("sq", s)])
            p.add("pe", lambda e, pb=pb, s=s, i=i: e.matmul(acc[pb][:, :], ones[:, :], sq[s][:, :], start=(i == 0), stop=(i == len(srcs) - 1)),
                  reads=[("sq", s), "ones"], writes=[("acc", pb)])
        p.add("act", lambda e, pb=pb: e.activation(dst[:, :], acc[pb][:, :], AF.Sqrt, bias=epst[:, 0:1], scale=1.0 / n_feat),
              reads=[("acc", pb), "eps"], writes=[dkey])
        p.add("dve", lambda e: e.reciprocal(dst[:, :], dst[:, :]), reads=[dkey], writes=[dkey])

    for t in range(PR_NT):
        c0 = t * PR_TW
        xb = xt[t % 2]
        hb = ht[t % 2]
        xk = ("xt", t % 2)
        hk = ("ht", t % 2)
        for k in range(8):
            p.dma("sp", xb[:, k, :], xT_v[:, k, c0:c0 + PR_TW], writes=[(xk, k)])
        rms_stat([xb[:, k, :] for k in range(8)], [(xk, k) for k in range(8)], rs, "rs", 1024.0)
        for k in range(8):
            p.add("dve", lambda e, k=k, xb=xb, hb=hb: e.scalar_tensor_tensor(
                hb[:, k, :], xb[:, k, :], vt[:, k:k + 1], rs[:, :], ALU.mult, ALU.mult),
                reads=[(xk, k), "vt", "rs"], writes=[(hk, k)])
            if mode == "h":
                stores.append(p.dma("sp", oT_v[:, k, c0:c0 + PR_TW], hb[:, k, :], reads=[(hk, k)]))
        if mode == "m":
            lob = lo[t % 2]
            lk = ("lo", t % 2)
            for oc in range(5):
                pb = cnt["lps"] % 2
                cnt["lps"] += 1
                for k in range(8):
                    p.add("pe", lambda e, pb=pb, k=k, oc=oc, hb=hb: e.matmul(
                        lps[pb][:, :], wbf[:, k * 640 + oc * 128:k * 640 + oc * 128 + 128], hb[:, k, :],
                        start=(k == 0), stop=(k == 7)),
                        reads=["wbf", (hk, k)], writes=[("lps", pb)])
                if oc < 4:
                    p.add("act", lambda e, pb=pb, oc=oc: e.activation(lat[:, oc, :], lps[pb][:, :], AF.Copy),
                          reads=[("lps", pb)], writes=[("lat", oc)])
                else:
                    p.add("act", lambda e, pb=pb, lob=lob: e.activation(lob[:, 4, :], lps[pb][:, :], AF.Copy),
                          reads=[("lps", pb)], writes=[(lk, 4)])
            for grp in range(2):
                rms_stat([lat[:, 2 * grp + i, :] for i in range(2)], [("lat", 2 * grp + i) for i in range(2)], rs2, "rs2", 256.0)
                for i in range(2):
                    oc = 2 * grp + i
                    p.add("dve", lambda e, oc=oc, lob=lob, grp=grp, i=i: e.scalar_tensor_tensor(
                        lob[:, oc, :], lat[:, oc, :], vt[:, 8 + 2 * grp + i:9 + 2 * grp + i], rs2[:, :], ALU.mult, ALU.mult),
                        reads=[("lat", oc), "vt", "rs2"], writes=[(lk, oc)])
            for oc in range(5):
                stores.append(p.dma("sp", oT_v[:, oc, c0:c0 + PR_TW], lob[:, oc, :], reads=[(lk, oc)]))
    p.finish(stores)
    return p.emit()


def pre_inputs(g_mix, q_a=None, kv_a=None, w_in=None):
    vec = np.zeros((128, 12), np.float32)
    vec[:, 0:8] = g_mix.reshape(8, 128).T
    out = {"vec": vec}
    if w_in is not None:
        vec[:, 8:10] = q_a.reshape(2, 128).T
        vec[:, 10:12] = kv_a.reshape(2, 128).T
        w = np.zeros((1024, 640), np.float32)
        w[:, :576] = w_in
        out["w_in"] = np.ascontiguousarray(w.reshape(8, 128, 640).transpose(1, 0, 2).reshape(128, 8 * 640))
    return out


H_TB = 1024
H_NB = 16384 // H_TB
H_SEQ_BLOCKS = 8192 // H_TB
HC_MASK = 0
HC_IDENT = 128
HC_RMASK = 256
HC_CMA = 256 + H_TB
HC_N = 256 + 2 * H_TB
HV_N = 5


def build_mixh(eps=1e-6, nblocks=H_NB, upto=9):
    nc = bass.Bass("TRN2", target_bir_lowering=False)
    hT = nc.dram_tensor("hT", [1024, 16384], BF16, kind="ExternalInput").ap()
    w_in = nc.dram_tensor("w_in", [128, 8 * 512], F32, kind="ExternalInput").ap()
    consts = nc.dram_tensor("consts", [128, HC_N], F32, kind="ExternalInput").ap()
    vec = nc.dram_tensor("vec", [128, HV_N], F32, kind="ExternalInput").ap()
    oT = nc.dram_tensor("oT", [128, 16384], BF16, kind="ExternalOutput").ap()

    p = Prog(nc)
    TB = H_TB
    NTL = TB // 128
    cst = p.sb([128, HC_N], F32, "cst")
    vt = p.sb([128, HV_N], F32, "vt")
    lbt = p.sb([128, 4], F32, "lbt")
    wst = p.sb([128, 4096], F32, "wst")
    wbf = p.sb([128, 4096], BF16, "wbf")
    ident = p.sb([128, 128], BF16, "ident")
    bmask = p.sb([128, 128], F32, "bmask")
    ones = p.sb([128, 128], BF16, "ones")
    epst = p.sb([128, 1], F32, "epst")
    ht = [p.sb([128, 8, TB], BF16, f"ht{i}") for i in range(2)]
    qs = p.sb([128, TB], F32, "qs")
    fg = p.sb([128, TB], F32, "fg")
    lf = p.sb([128, TB], F32, "lf")
    kk = p.sb([128, TB], F32, "kk")
    G = p.sb([128, TB], F32, "G")
    Gc = p.sb([128, TB], F32, "Gc")
    E0 = p.sb([128, TB], F32, "E0")
    E1 = p.sb([128, TB], F32, "E1")
    gs = p.sb([128, TB], F32, "gs")
    vT = p.sb([128, TB], BF16, "vT")
    qt = p.sb([128, TB], BF16, "qt")
    kt = p.sb([128, TB], BF16, "kt")
    kh = p.sb([128, TB], BF16, "kh")
    vtok = p.sb([128, NTL, 128], BF16, "vtok")
    khA = p.sb([128, NTL, 128], BF16, "khA")
    khB = p.sb([128, NTL, 128], BF16, "khB")
    qhA = p.sb([128, TB], BF16, "qhA")
    qhB = p.sb([128, TB], BF16, "qhB")
    scm = p.sb([128, NTL, 128], BF16, "scm")
    Sf = p.sb([128, 128], F32, "Sf")
    Sb = [p.sb([128, 128], BF16, f"Sb{i}") for i in range(2)]
    osb = p.sb([128, TB], F32, "osb")
    osq = [p.sb([128, 512], BF16, f"osq{i}") for i in range(2)]
    rs = p.sb([128, TB], F32, "rs")
    ofin = [p.sb([128, TB], BF16, f"ofin{i}") for i in range(2)]

    pin = [p.ps(name=f"pin{i}") for i in range(2)]
    ptr = p.ps([128, 1024], BF16, "ptr")
    psc = p.ps(name="psc")
    pst = [p.ps(name=f"pst{i}") for i in range(2)]
    po = [p.ps(name=f"po{i}") for i in range(2)]

    hT_v = hT.rearrange("(k p) t -> p k t", p=128)

    p.dma("sp", cst[:, :], consts[:, :], writes=["cst"])
    p.dma("sp", vt[:, :], vec[:, :], writes=["vt"])
    p.dma("sp", wst[:, :], w_in[:, :], writes=["wst"])
    p.add("pool", lambda e: e.tensor_copy(wbf[:, :], wst[:, :]), reads=["wst"], writes=["wbf"])
    p.add("dve", lambda e: e.tensor_copy(ident[:, :], cst[:, HC_IDENT:HC_IDENT + 128]), reads=["cst"], writes=["ident"])
    p.add("dve", lambda e: e.tensor_copy(bmask[:, :], cst[:, HC_MASK:HC_MASK + 128]), reads=["cst"], writes=["bmask"])
    p.add("pool", lambda e: e.memset(ones[:, :], 1.0), writes=["ones"])
    p.add("pool", lambda e: e.memset(epst[:, :], eps), writes=["eps"])
    p.add("pool", lambda e: e.memset(khA[:, :, :], 0.0), writes=["khA"])
    p.add("pool", lambda e: e.memset(khB[:, :, :], 0.0), writes=["khB"])
    if upto < 9:
        for i in range(2):
            p.add("pool", lambda e, i=i: e.memset(ofin[i][:, :], 0.0), writes=[("ofin", i)])
    p.add("act", lambda e: e.activation(lbt[:, 0:2], vt[:, 0:2], AF.Exp), reads=["vt"], writes=["lbt"])
    p.add("dve", lambda e: e.tensor_tensor(lbt[:, 2:3], lbt[:, 0:1], lbt[:, 1:2], ALU.add), reads=["lbt"], writes=["lbt"])
    p.add("dve", lambda e: e.reciprocal(lbt[:, 2:3], lbt[:, 2:3]), reads=["lbt"], writes=["lbt"])
    p.add("dve", lambda e: e.tensor_tensor(lbt[:, 0:2], lbt[:, 0:2], vt[:, 2:4], ALU.mult), reads=["lbt", "vt"], writes=["lbt"])
    p.add("dve", lambda e: e.tensor_tensor(lbt[:, 3:4], lbt[:, 0:1], lbt[:, 1:2], ALU.add), reads=["lbt"], writes=["lbt"])
    p.add("dve", lambda e: e.tensor_tensor(lbt[:, 0:1], lbt[:, 3:4], lbt[:, 2:3], ALU.mult), reads=["lbt"], writes=["lbt"])
    p.add("dve", lambda e: e.tensor_scalar(lbt[:, 1:2], lbt[:, 0:1], -1.0, 1.0, ALU.mult, ALU.add), reads=["lbt"], writes=["lbt"])
    LB = lbt[:, 0:1]
    OML = lbt[:, 1:2]
    ON = vt[:, 4:5]
    rmask = cst[:, HC_RMASK:HC_RMASK + TB]

    cnt = {"pin": 0, "tr": 0, "sc": 0, "st": 0, "scm": 0, "S": 0, "osq": 0}
    stores = []

    def bc_col(t, col):
        v = t[:, :].rearrange("p (n c) -> p n c", c=64)
        return v[:, :, col:col + 1].broadcast(2, 64) if hasattr(v, "broadcast") else None

    for blk in range(nblocks):
      for _once in (0,):
          hb = ht[blk % 2]
          hk = ("ht", blk % 2)
          t0 = blk * TB
          for k in range(8):
              p.dma("sp", hb[:, k, :], hT_v[:, k, t0:t0 + TB], writes=[(hk, k)])
          if blk % H_SEQ_BLOCKS == 0:
              p.add("pool", lambda e: e.memset(Sf[:, :], 0.0), writes=["Sf"])
              si = cnt["S"] % 2
              p.add("pool", lambda e, si=si: e.memset(Sb[si][:, :], 0.0), writes=[("Sb", si)])
          for grp in range(4):
              for nt in range(TB // 512):
                  pb = cnt["pin"] % 2
                  cnt["pin"] += 1
                  for k in range(8):
                      p.add("pe", lambda e, pb=pb, k=k, grp=grp, nt=nt, hb=hb: e.matmul(
                          pin[pb][:, :], wbf[:, k * 512 + grp * 128:k * 512 + grp * 128 + 128],
                          hb[:, k, nt * 512:(nt + 1) * 512], start=(k == 0), stop=(k == 7)),
                          reads=["wbf", (hk, k)], writes=[("pin", pb)])
                  sl = slice(nt * 512, (nt + 1) * 512)
                  if grp == 0:
                      p.add("act", lambda e, pb=pb, sl=sl: e.activation(qs[:, sl], pin[pb][:, :], AF.Silu),
                            reads=[("pin", pb)], writes=["qs"])
                  elif grp == 1:
                      p.add("act", lambda e, pb=pb, sl=sl: e.activation(fg[:, sl], pin[pb][:, :], AF.Sigmoid),
                            reads=[("pin", pb)], writes=["fg"])
                  elif grp == 2:
                      p.add("act", lambda e, pb=pb, sl=sl: e.activation(vT[:, sl], pin[pb][:, :], AF.Copy),
                            reads=[("pin", pb)], writes=["vT"])
                  else:
                      p.add("act", lambda e, pb=pb, sl=sl: e.activation(gs[:, sl], pin[pb][:, :], AF.Silu),
                            reads=[("pin", pb)], writes=["gs"])
          if upto < 2: break
          p.add("dve", lambda e: e.tensor_scalar(fg[:, :], fg[:, :], OML, LB, ALU.mult, ALU.add),
                reads=["fg", "lbt"], writes=["fg"])
          p.add("act", lambda e: e.activation(lf[:, :], fg[:, :], AF.Ln), reads=["fg"], writes=["lf"])
          p.add("pool", lambda e: e.tensor_scalar(kk[:, :], fg[:, :], -1.0, 1.0, ALU.mult, ALU.add),
                reads=["fg"], writes=["kk"])
          p.add("dve", lambda e: e.tensor_tensor_scan(G[:, :], rmask, lf[:, :], 0.0, ALU.mult, ALU.add),
                reads=["lf", "cst"], writes=["G"])
          G3 = G[:, :].rearrange("p (n c) -> p n c", c=64)
          Gc3 = Gc[:, :].rearrange("p (n c) -> p n c", c=64)
          nch = TB // 64
          p.add("dve", lambda e: e.tensor_tensor(Gc3, G3, G3[:, :, 31:32].to_broadcast([128, nch, 64]), ALU.subtract),
                reads=["G"], writes=["Gc"])
          p.add("act", lambda e: e.activation(E0[:, :], Gc[:, :], AF.Exp), reads=["Gc"], writes=["E0"])
          p.add("dve", lambda e: e.tensor_tensor(qt[:, :], qs[:, :], E0[:, :], ALU.mult), reads=["qs", "E0"], writes=["qt"])
          p.add("act", lambda e: e.activation(E1[:, :], Gc[:, :], AF.Exp, scale=-1.0), reads=["Gc"], writes=["E1"])
          p.add("pool", lambda e: e.tensor_tensor(kt[:, :], kk[:, :], E1[:, :], ALU.mult), reads=["kk", "E1"], writes=["kt"])
          p.add("dve", lambda e: e.tensor_tensor(Gc3, G3, G3[:, :, 63:64].to_broadcast([128, nch, 64]), ALU.subtract),
                reads=["G"], writes=["Gc"])
          p.add("act", lambda e: e.activation(E1[:, :], Gc[:, :], AF.Exp, scale=-1.0), reads=["Gc"], writes=["E1"])
          p.add("pool", lambda e: e.tensor_tensor(kh[:, :], kk[:, :], E1[:, :], ALU.mult), reads=["kk", "E1"], writes=["kh"])
          p.add("act", lambda e: e.activation(E0[:, :], G[:, :], AF.Exp), reads=["G"], writes=["E0"])
          p.add("dve", lambda e: e.tensor_tensor(E1[:, :], qs[:, :], E0[:, :], ALU.mult), reads=["qs", "E0", "E1"], writes=["E1"])
          p.add("pool", lambda e: e.tensor_tensor(qhA[:, :], E1[:, :], cst[:, HC_CMA:HC_CMA + TB], ALU.mult), reads=["E1", "cst"], writes=["qhA"])
          p.add("dve", lambda e: e.tensor_tensor(qhB[:, :], E1[:, :], qhA[:, :], ALU.subtract), reads=["E1", "qhA"], writes=["qhB"])
          if upto < 3: break
          for tl in range(NTL):
              p.add("pe", lambda e, tl=tl: e.transpose(ptr[:, tl * 128:(tl + 1) * 128], vT[:, tl * 128:(tl + 1) * 128], ident[:, :]),
                    reads=["vT", "ident"], writes=["ptr"])
          p.add("dve", lambda e: e.tensor_copy(vtok[:, :, :].rearrange("p a b -> p (a b)"), ptr[:, :]),
                reads=["ptr"], writes=["vtok"])
          for tl in range(NTL):
              p.add("pe", lambda e, tl=tl: e.transpose(ptr[:, tl * 128:(tl + 1) * 128], kh[:, tl * 128:(tl + 1) * 128], ident[:, :]),
                    reads=["kh", "ident"], writes=["ptr"])
          p.add("dve", lambda e: e.tensor_copy(khA[0:64, :, :].rearrange("p a b -> p (a b)"), ptr[0:64, :]),
                reads=["ptr"], writes=["khA"])
          p.add("dve", lambda e: e.tensor_copy(khB[64:128, :, :].rearrange("p a b -> p (a b)"), ptr[64:128, :]),
                reads=["ptr"], writes=["khB"])
          for g4 in range(NTL // 4):
              for q in range(4):
                  c0 = (g4 * 4 + q) * 128
                  p.add("pe", lambda e, q=q, c0=c0: e.matmul(psc[:, q * 128:(q + 1) * 128], kt[:, c0:c0 + 128], qt[:, c0:c0 + 128],
                                                             start=True, stop=True),
                        reads=["kt", "qt"], writes=["psc"])
              p.add("dve", lambda e, g4=g4: e.tensor_tensor(
                  scm[:, g4 * 4:(g4 + 1) * 4, :], psc[:, :].rearrange("p (a b) -> p a b", b=128),
                  bmask[:, :].unsqueeze(1).to_broadcast([128, 4, 128]), ALU.mult),
                  reads=["psc", "bmask"], writes=[("scm", g4)])
          if upto < 5: break
          for tl in range(NTL):
              c0 = tl * 128
              if tl % 2 == 0:
                  sb_ = cnt["st"] % 2
                  cnt["st"] += 1
                  for q in range(4):
                      tile = tl + q // 2
                      khx = khA if q % 2 == 0 else khB
                      p.add("pe", lambda e, sb_=sb_, q=q, tile=tile, khx=khx: e.matmul(
                          pst[sb_][:, q * 128:(q + 1) * 128], khx[:, tile, :], vtok[:, tile, :], start=True, stop=True),
                          reads=["khA", "khB", "vtok"], writes=[("pst", sb_)])
              ob = (tl // 4) % 2
              oc = (tl % 4) * 128
              okey = ("po", ob)
              p.add("pe", lambda e, ob=ob, oc=oc, tl=tl: e.matmul(po[ob][:, oc:oc + 128], vtok[:, tl, :], scm[:, tl, :],
                                                             start=True, stop=False),
                    reads=["vtok", ("scm", tl // 4)], writes=[okey])
              for half in range(2):
                  si = cnt["S"] % 2
                  hc = c0 + half * 64
                  qhx = qhA if half == 0 else qhB
                  p.add("pe", lambda e, ob=ob, oc=oc, half=half, si=si, c0=c0, qhx=qhx: e.matmul(
                      po[ob][:, oc:oc + 128], Sb[si][:, :], qhx[:, c0:c0 + 128], start=False, stop=(half == 1)),
                      reads=[("Sb", si), "qhA", "qhB"], writes=[okey])
                  q = (tl % 2) * 2 + half
                  dcol = hc + 63
                  p.add("dve", lambda e, sb_=sb_, q=q, dcol=dcol: e.scalar_tensor_tensor(
                      Sf[:, :], Sf[:, :], E0[:, dcol:dcol + 1], pst[sb_][:, q * 128:(q + 1) * 128], ALU.mult, ALU.add),
                      reads=["Sf", "E0", ("pst", sb_)], writes=["Sf"])
                  cnt["S"] += 1
                  sn = cnt["S"] % 2
                  p.add("act", lambda e, sn=sn: e.activation(Sb[sn][:, :], Sf[:, :], AF.Copy),
                        reads=["Sf"], writes=[("Sb", sn)])
              if tl % 4 == 3:
                  o0 = (tl // 4) * 512
                  p.add("act", lambda e, ob=ob, o0=o0: e.activation(osb[:, o0:o0 + 512], po[ob][:, :], AF.Copy),
                        reads=[okey], writes=["osb"])
          for nt in range(TB // 512):
              sl = slice(nt * 512, (nt + 1) * 512)
              s = cnt["osq"] % 2
              cnt["osq"] += 1
              pb = cnt["pin"] % 2
              cnt["pin"] += 1
              p.add("act", lambda e, s=s, sl=sl: e.activation(osq[s][:, :], osb[:, sl], AF.Square),
                    reads=["osb"], writes=[("osq", s)])
              p.add("pe", lambda e, pb=pb, s=s: e.matmul(pin[pb][:, :], ones[:, :], osq[s][:, :], start=True, stop=True),
                    reads=[("osq", s), "ones"], writes=[("pin", pb)])
              p.add("act", lambda e, pb=pb, sl=sl: e.activation(rs[:, sl], pin[pb][:, :], AF.Sqrt, bias=epst[:, 0:1], scale=1.0 / 128.0),
                    reads=[("pin", pb), "eps"], writes=["rs"])
          p.add("dve", lambda e: e.reciprocal(rs[:, :], rs[:, :]), reads=["rs"], writes=["rs"])
          p.add("dve", lambda e: e.tensor_tensor(rs[:, :], rs[:, :], gs[:, :], ALU.mult), reads=["rs", "gs"], writes=["rs"])
          of = ofin[blk % 2]
          p.add("dve", lambda e, of=of: e.scalar_tensor_tensor(of[:, :], osb[:, :], ON, rs[:, :], ALU.mult, ALU.mult),
                reads=["osb", "vt", "rs"], writes=[("ofin", blk % 2)])
      of = ofin[blk % 2]
      stores.append(p.dma("sp", oT[:, t0:t0 + TB], of[:, :], reads=[("ofin", blk % 2)], force=True))
    p.finish(stores)
    return p.emit()


def mixh_consts():
    c = np.zeros((128, HC_N), np.float32)
    s = np.arange(128)[:, None]
    t = np.arange(128)[None, :]
    c[:, HC_MASK:HC_MASK + 128] = ((s // 64 == t // 64) & (t >= s)).astype(np.float32)
    c[:, HC_IDENT:HC_IDENT + 128] = np.eye(128, dtype=np.float32)
    rm = np.ones(H_TB, np.float32)
    rm[0::64] = 0.0
    c[:, HC_RMASK:HC_RMASK + H_TB] = rm[None, :]
    c[:, HC_CMA:HC_CMA + H_TB] = (((np.arange(H_TB) // 64) % 2) == 0).astype(np.float32)[None, :]
    return c


def mixh_inputs(head, w_in, lbp, out_norm, j):
    cols = np.concatenate([np.arange(g * 1024 + head * 128, g * 1024 + head * 128 + 128) for g in range(4)])
    w = w_in[:, cols]
    w = np.ascontiguousarray(w.reshape(8, 128, 512).transpose(1, 0, 2).reshape(128, 4096))
    vec = np.zeros((128, HV_N), np.float32)
    vec[:, 0] = lbp[0, head * 128:(head + 1) * 128]
    vec[:, 1] = lbp[1, head * 128:(head + 1) * 128]
    vec[:, 2] = 1.0 if (0 >= 1 and 0 <= j) else 0.0
    vec[:, 3] = 1.0 if (1 >= 1 and 1 <= j) else 0.0
    vec[:, 4] = out_norm
    return w, vec


M_G = 4
M_TPB = 64
MG_N = 192 * 2 + 32
MC_MASK = 0
MC_ID = 2048
MC_N = 2048 + 128
MAGIC = 12582912.0
TWO_PI = 2.0 * math.pi
C1 = 6.28125
C2 = TWO_PI - C1
PI_CL = 3.1415925


def build_mixm(eps=1e-6, nbatch=2, nq=16, ng=None):
    nc = bass.Bass("TRN2", target_bir_lowering=False)
    latT = nc.dram_tensor("latT", [640, 16384], BF16, kind="ExternalInput").ap()
    posT = nc.dram_tensor("posT", [128, 128], I32, kind="ExternalInput").ap()
    wq = nc.dram_tensor("wq", [128, 2 * 192], F32, kind="ExternalInput").ap()
    wkv = nc.dram_tensor("wkv", [128, 3 * 320], F32, kind="ExternalInput").ap()
    gvec = nc.dram_tensor("gvec", [128, MG_N], F32, kind="ExternalInput").ap()
    consts = nc.dram_tensor("consts", [128, MC_N], F32, kind="ExternalInput").ap()
    oT = nc.dram_tensor("oT", [128, 16384], BF16, kind="ExternalOutput").ap()

    p = Prog(nc)
    G = M_G
    cst = p.sb([128, MC_N], F32, "cst")
    gv = p.sb([128, MG_N], F32, "gv")
    wqs = p.sb([128, 384], F32, "wqs")
    wkvs = p.sb([128, 960], F32, "wkvs")
    wqb = p.sb([128, 384], BF16, "wqb")
    wkvb = p.sb([128, 960], BF16, "wkvb")
    masks = p.sb([128, 4, 512], BF16, "masks")
    ident = p.sb([128, 128], BF16, "ident")
    ones = p.sb([128, 128], BF16, "ones")
    epst = p.sb([128, 1], F32, "epst")
    hpi = p.sb([128, 1], F32, "hpi")
    posi = p.sb([128, 128], I32, "posi")
    posf = p.sb([128, 128], F32, "posf")
    cosT = p.sb([128, M_TPB, 32], F32, "cosT")
    sinT = p.sb([128, M_TPB, 32], F32, "sinT")
    tg1 = p.sb([128, M_TPB, 32], F32, "tg1")
    tg2 = p.sb([128, M_TPB, 32], F32, "tg2")
    latb = [p.sb([128, 5, 512], BF16, f"latb{i}") for i in range(2)]
    junk = p.sb([128, 192], F32, "junk")
    ss = p.sb([128, 8], F32, "ss")
    qn = p.sb([128, G, 192], F32, "qn")
    kn = p.sb([128, G, 192], F32, "kn")
    trq = [p.sb([128, G, 32], F32, f"trq{i}") for i in range(4)]
    trk = [p.sb([128, G, 32], F32, f"trk{i}") for i in range(4)]
    qf = p.sb([128, G, 128], BF16, "qf")
    kf = p.sb([128, G, 128], BF16, "kf")
    qfr = p.sb([128, G, 128], BF16, "qfr")
    kfr = p.sb([128, G, 128], BF16, "kfr")
    qTm = p.sb([128, 8192], BF16, "qTm")
    qTr = p.sb([128, 8192], BF16, "qTr")
    kTm = p.sb([128, 8192], BF16, "kTm")
    kTr = p.sb([128, 8192], BF16, "kTr")
    vtok = p.sb([128, M_TPB, 128], BF16, "vtok")
    PT = [p.sb([128, 512], BF16, f"PT{i}") for i in range(3)]
    rl = p.sb([128, 512], F32, "rl")
    of = [p.sb([128, 512], BF16, f"of{i}") for i in range(2)]

    B = [p.ps(name=f"B{i}") for i in range(6)]
    ptrQ = p.ps([128, 1024], BF16, "ptrQ")
    ptrK = p.ps([128, 1024], BF16, "ptrK")

    p.excl = set([("B", i) for i in range(6)] + ["ptrQ", "ptrK"])
    p.dma("sp", cst[:, :], consts[:, :], writes=["cst"])
    p.dma("sp", gv[:, :], gvec[:, :], writes=["gv"])
    p.dma("sp", wqs[:, :], wq[:, :], writes=["wqs"])
    p.dma("sp", wkvs[:, :], wkv[:, :], writes=["wkvs"])
    p.dma("sp", posi[:, :], posT[:, :], writes=["posi"])
    p.add("pool", lambda e: e.tensor_copy(wqb[:, :], wqs[:, :]), reads=["wqs"], writes=["wqb"])
    p.add("pool", lambda e: e.tensor_copy(wkvb[:, :], wkvs[:, :]), reads=["wkvs"], writes=["wkvb"])
    p.add("dve", lambda e: e.tensor_copy(masks[:, :, :].rearrange("p a b -> p (a b)"), cst[:, MC_MASK:MC_MASK + 2048]),
          reads=["cst"], writes=["masks"])
    p.add("dve", lambda e: e.tensor_copy(ident[:, :], cst[:, MC_ID:MC_ID + 128]), reads=["cst"], writes=["ident"])
    p.add("dve", lambda e: e.tensor_copy(posf[:, :], posi[:, :]), reads=["posi"], writes=["posf"])
    p.add("pool", lambda e: e.memset(ones[:, :], 1.0), writes=["ones"])
    p.add("pool", lambda e: e.memset(epst[:, :], eps), writes=["eps"])
    p.add("pool", lambda e: e.memset(hpi[:, :], math.pi / 2.0), writes=["hpi"])
    p.add("pool", lambda e: e.memset(qfr[:, :, :], 0.0), writes=["qfr"])
    p.add("pool", lambda e: e.memset(kfr[:, :, :], 0.0), writes=["kfr"])
    p.add("dve", lambda e: e.tensor_scalar(gv[:, 0:192], gv[:, 0:192], 192.0 ** -0.5, None, ALU.mult),
          reads=["gv"], writes=["gv"])
    GQ = gv[:, 0:192]
    GK = gv[:, 192:384]
    INVF = gv[:, 384:416]

    cnt = {"lat": 0, "pt": 0, "sc": 0, "of": 0}
    stores = []

    for b in range(nbatch):
        i0 = b * M_TPB
        p.add("dve", lambda e, i0=i0: e.tensor_tensor(
            tg1[:, :, :], posf[:, i0:i0 + M_TPB].unsqueeze(2).to_broadcast([128, M_TPB, 32]),
            INVF.unsqueeze(1).to_broadcast([128, M_TPB, 32]), ALU.mult),
            reads=["posf", "gv"], writes=["tg1"])
        p.add("dve", lambda e: e.tensor_scalar(tg2[:, :, :], tg1[:, :, :], 1.0 / TWO_PI, MAGIC, ALU.mult, ALU.add),
              reads=["tg1"], writes=["tg2"])
        p.add("dve", lambda e: e.tensor_scalar(tg2[:, :, :], tg2[:, :, :], -MAGIC, None, ALU.add),
              reads=["tg2"], writes=["tg2"])
        p.add("dve", lambda e: e.scalar_tensor_tensor(tg1[:, :, :], tg2[:, :, :], -C1, tg1[:, :, :], ALU.mult, ALU.add),
              reads=["tg1", "tg2"], writes=["tg1"])
        p.add("dve", lambda e: e.scalar_tensor_tensor(tg1[:, :, :], tg2[:, :, :], -C2, tg1[:, :, :], ALU.mult, ALU.add),
              reads=["tg1", "tg2"], writes=["tg1"])
        p.add("dve", lambda e: e.tensor_scalar(tg1[:, :, :], tg1[:, :, :], -PI_CL, PI_CL, ALU.max, ALU.min),
              reads=["tg1"], writes=["tg1"])
        p.add("act", lambda e: e.activation(sinT[:, :, :], tg1[:, :, :], AF.Sin), reads=["tg1"], writes=["sinT"])
        p.add("dve", lambda e: e.tensor_scalar(tg2[:, :, :], tg1[:, :, :], -1.0, None, ALU.mult),
              reads=["tg1", "tg2"], writes=["tg2"])
        p.add("dve", lambda e: e.tensor_tensor(tg2[:, :, :], tg2[:, :, :], tg1[:, :, :], ALU.max),
              reads=["tg1", "tg2"], writes=["tg2"])
        p.add("act", lambda e: e.activation(cosT[:, :, :], tg2[:, :, :], AF.Sin, bias=hpi[:, 0:1], scale=-1.0),
              reads=["tg2", "hpi"], writes=["cosT"])
        for g in range(M_TPB // G if ng is None else ng):
            tok0 = b * 8192 + g * 512
            lb = latb[cnt["lat"] % 2]
            lk = ("latb", cnt["lat"] % 2)
            cnt["lat"] += 1
            for k in range(5):
                p.dma("sp", lb[:, k, :], latT[k * 128:(k + 1) * 128, tok0:tok0 + 512], writes=[(lk, k)])
            for q in range(G):
                ts_ = slice(q * 128, (q + 1) * 128)
                for k in range(2):
                    p.add("pe", lambda e, q=q, k=k, lb=lb, ts_=ts_: e.matmul(
                        B[q // 2][:, (q % 2) * 192:(q % 2) * 192 + 192], lb[:, k, ts_], wqb[:, k * 192:(k + 1) * 192],
                        start=(k == 0), stop=(k == 1)),
                        reads=[(lk, k), "wqb"], writes=[("B", q // 2)])
                for k in range(3):
                    p.add("pe", lambda e, q=q, k=k, lb=lb, ts_=ts_: e.matmul(
                        B[2 + q][:, 0:320], lb[:, 2 + k, ts_], wkvb[:, k * 320:(k + 1) * 320],
                        start=(k == 0), stop=(k == 2)),
                        reads=[(lk, 2 + k), "wkvb"], writes=[("B", 2 + q)])
            for q in range(G):
                qsrc = B[q // 2][:, (q % 2) * 192:(q % 2) * 192 + 192]
                ksrc = B[2 + q][:, 0:192]
                p.add("act", lambda e, q=q, qsrc=qsrc: e.activation(junk[:, :], qsrc, AF.Square, accum_out=ss[:, q:q + 1]),
                      reads=[("B", q // 2)], writes=["junk", "ss"])
                p.add("act", lambda e, q=q, ksrc=ksrc: e.activation(junk[:, :], ksrc, AF.Square, accum_out=ss[:, 4 + q:5 + q]),
                      reads=[("B", 2 + q)], writes=["junk", "ss"])
            p.add("act", lambda e: e.activation(ss[:, :], ss[:, :], AF.Sqrt, bias=epst[:, 0:1], scale=1.0 / 192.0),
                  reads=["ss", "eps"], writes=["ss"])
            p.add("dve", lambda e: e.reciprocal(ss[:, :], ss[:, :]), reads=["ss"], writes=["ss"])
            for q in range(G):
                qsrc = B[q // 2][:, (q % 2) * 192:(q % 2) * 192 + 192]
                ksrc = B[2 + q][:, 0:192]
                p.add("dve", lambda e, q=q, qsrc=qsrc: e.scalar_tensor_tensor(qn[:, q, :], qsrc, ss[:, q:q + 1], GQ, ALU.mult, ALU.mult),
                      reads=[("B", q // 2), "ss", "gv"], writes=["qn"])
                p.add("dve", lambda e, q=q, ksrc=ksrc: e.scalar_tensor_tensor(kn[:, q, :], ksrc, ss[:, 4 + q:5 + q], GK, ALU.mult, ALU.mult),
                      reads=[("B", 2 + q), "ss", "gv"], writes=["kn"])
                tile = g * G + q
                p.add("act", lambda e, q=q, tile=tile: e.activation(vtok[:, tile, :], B[2 + q][:, 192:320], AF.Copy),
                      reads=[("B", 2 + q)], writes=["vtok"])
            ti0 = g * G
            cs = cosT[:, ti0:ti0 + G, :]
            sn = sinT[:, ti0:ti0 + G, :]
            for (src, skey, dst, dkey, dstr, drkey, eng, tr, tn) in ((qn, "qn", qf, "qf", qfr, "qfr", "dve", trq, "trq"), (kn, "kn", kf, "kf", kfr, "kfr", "pool", trk, "trk")):
                x1 = src[:, :, 128:160]
                x2 = src[:, :, 160:192]
                p.add("act", lambda e, src=src, dst=dst: e.activation(dst[:, :, :], src[:, :, 0:128], AF.Copy),
                      reads=[skey], writes=[dkey])
                p.add(eng, lambda e, x1=x1, tr=tr, cs=cs: e.tensor_tensor(tr[0][:, :, :], x1, cs, ALU.mult), reads=[skey, "cosT"], writes=[(tn, 0)])
                p.add(eng, lambda e, x2=x2, tr=tr, sn=sn: e.tensor_tensor(tr[1][:, :, :], x2, sn, ALU.mult), reads=[skey, "sinT"], writes=[(tn, 1)])
                p.add(eng, lambda e, dstr=dstr, tr=tr: e.tensor_tensor(dstr[:, :, 0:32], tr[0][:, :, :], tr[1][:, :, :], ALU.subtract),
                      reads=[(tn, 0), (tn, 1)], writes=[drkey])
                p.add(eng, lambda e, x2=x2, tr=tr, cs=cs: e.tensor_tensor(tr[2][:, :, :], x2, cs, ALU.mult), reads=[skey, "cosT"], writes=[(tn, 2)])
                p.add(eng, lambda e, x1=x1, tr=tr, sn=sn: e.tensor_tensor(tr[3][:, :, :], x1, sn, ALU.mult), reads=[skey, "sinT"], writes=[(tn, 3)])
                p.add(eng, lambda e, dstr=dstr, tr=tr: e.tensor_tensor(dstr[:, :, 32:64], tr[2][:, :, :], tr[3][:, :, :], ALU.add),
                      reads=[(tn, 2), (tn, 3)], writes=[drkey])
            lt0 = g * 512
            for (srcf, skey, srcr, srkey, ptr_, pkey, dm, dmk, dr, drk) in ((qf, "qf", qfr, "qfr", ptrQ, "ptrQ", qTm, "qTm", qTr, "qTr"),
                                                                              (kf, "kf", kfr, "kfr", ptrK, "ptrK", kTm, "kTm", kTr, "kTr")):
                for q in range(G):
                    p.add("pe", lambda e, q=q, srcf=srcf, ptr_=ptr_: e.transpose(
                        ptr_[:, q * 128:(q + 1) * 128], srcf[:, q, :], ident[:, :]),
                        reads=[skey, "ident"], writes=[pkey])
                for q in range(G):
                    p.add("pe", lambda e, q=q, srcr=srcr, ptr_=ptr_: e.transpose(
                        ptr_[:, 512 + q * 128:512 + (q + 1) * 128], srcr[:, q, :], ident[:, :]),
                        reads=[srkey, "ident"], writes=[pkey])
                p.add("dve", lambda e, ptr_=ptr_, dm=dm, lt0=lt0: e.tensor_copy(dm[:, lt0:lt0 + 512], ptr_[:, 0:512]),
                      reads=[pkey], writes=[dmk])
                p.add("act", lambda e, ptr_=ptr_, dr=dr, lt0=lt0: e.activation(dr[:, lt0:lt0 + 512], ptr_[:, 512:1024], AF.Copy),
                      reads=[pkey], writes=[drk])
        for i in range(nq):
            qs = slice(i * 512, (i + 1) * 512)
            nj = 4 * i + 4
            ob = 2 + (i % 2)
            lbk = 4 + (i % 2)
            for j in range(nj):
                sc = cnt["sc"] % 2
                cnt["sc"] += 1
                r = cnt["pt"] % 3
                cnt["pt"] += 1
                ks = slice(j * 128, (j + 1) * 128)
                p.add("pe", lambda e, sc=sc, ks=ks, qs=qs: e.matmul(B[sc][:, :], kTm[:, ks], qTm[:, qs], start=True, stop=False),
                      reads=["kTm", "qTm"], writes=[("B", sc)])
                p.add("pe", lambda e, sc=sc, ks=ks, qs=qs: e.matmul(B[sc][:, :], kTr[:, ks], qTr[:, qs], start=False, stop=True),
                      reads=["kTr", "qTr"], writes=[("B", sc)])
                p.add("act", lambda e, sc=sc, r=r: e.activation(PT[r][:, :], B[sc][:, :], AF.Exp),
                      reads=[("B", sc)], writes=[("PT", r)])
                if j >= 4 * i:
                    m = j - 4 * i
                    p.add("pool", lambda e, r=r, m=m: e.tensor_tensor(PT[r][:, :], PT[r][:, :], masks[:, m, :], ALU.mult),
                          reads=[("PT", r), "masks"], writes=[("PT", r)])
                p.add("pe", lambda e, ob=ob, j=j, r=r, nj=nj: e.matmul(B[ob][:, :], vtok[:, j, :], PT[r][:, :],
                                                                  start=(j == 0), stop=(j == nj - 1)),
                      reads=["vtok", ("PT", r)], writes=[("B", ob)])
                p.add("pe", lambda e, lbk=lbk, j=j, r=r, nj=nj: e.matmul(B[lbk][:, :], ones[:, :], PT[r][:, :],
                                                                    start=(j == 0), stop=(j == nj - 1)),
                      reads=["ones", ("PT", r)], writes=[("B", lbk)])
            p.add("dve", lambda e, lbk=lbk: e.reciprocal(rl[:, :], B[lbk][:, :]), reads=[("B", lbk)], writes=["rl"])
            oi = cnt["of"] % 2
            cnt["of"] += 1
            p.add("dve", lambda e, ob=ob, oi=oi: e.tensor_tensor(of[oi][:, :], B[ob][:, :], rl[:, :], ALU.mult),
                  reads=[("B", ob), "rl"], writes=[("of", oi)])
            t0 = b * 8192 + i * 512
            stores.append(p.dma("sp", oT[:, t0:t0 + 512], of[oi][:, :], reads=[("of", oi)], force=True))
    p.finish(stores)
    return p.emit()


def mixm_consts():
    c = np.zeros((128, MC_N), np.float32)
    k = np.arange(128)[:, None]
    q = np.arange(512)[None, :]
    for m in range(4):
        c[:, MC_MASK + m * 512:MC_MASK + (m + 1) * 512] = (q >= k + 128 * m).astype(np.float32)
    c[:, MC_ID:MC_ID + 128] = np.eye(128, dtype=np.float32)
    return c


def mixm_inputs(head, w_q_up, w_kv_up, q_norm, k_norm):
    wq = w_q_up[:, head * 192:(head + 1) * 192]
    wq = np.ascontiguousarray(wq.reshape(2, 128, 192).transpose(1, 0, 2).reshape(128, 384))
    wkv_h = w_kv_up[:, head * 256:(head + 1) * 256]
    ext = np.zeros((3, 128, 320), np.float32)
    for k in range(2):
        ext[k, :, 0:128] = wkv_h[k * 128:(k + 1) * 128, 0:128]
        ext[k, :, 192:320] = wkv_h[k * 128:(k + 1) * 128, 128:256]
    ext[2, np.arange(64), 128 + np.arange(64)] = 1.0
    wkv = np.ascontiguousarray(ext.transpose(1, 0, 2).reshape(128, 960))
    gvec = np.zeros((128, MG_N), np.float32)
    gvec[:, 0:192] = q_norm[None, :]
    gvec[:, 192:384] = k_norm[None, :]
    inv_freq = (10000.0 ** (-np.arange(0, 64, 2, dtype=np.float32) / 64)).astype(np.float32)
    gvec[:, 384:416] = inv_freq[None, :]
    return wq, wkv, gvec


import ml_dtypes

_PROGS = {}
N_CORES = 8


def _prog(name):
    if name not in _PROGS:
        if name == "pre_h":
            _PROGS[name] = build_pre("h")
        elif name == "pre_m":
            _PROGS[name] = build_pre("m")
        elif name == "mix_h":
            _PROGS[name] = build_mixh()
        elif name == "mix_m":
            _PROGS[name] = build_mixm()
        elif name == "post":
            _PROGS[name] = build_post()
    return _PROGS[name]


def _run(name, ins):
    res = run_bass_kernel_spmd(_prog(name), ins, core_ids=list(range(N_CORES)))
    return res.results


def kernel(x, positions, norm_mix, norm_ffn, hgrn_w_in, hgrn_lower_bounds, hgrn_out_norm, hgrn_w_out,
           mla_w_in, mla_q_a_norm, mla_w_q_up, mla_kv_a_norm, mla_w_kv_up, mla_q_norm, mla_k_norm,
           mla_w_out, ffn_w_up, ffn_conv_w, ffn_conv_b, ffn_w_down):
    f32 = np.float32
    x = np.asarray(x, f32)
    B, S, D = x.shape
    T = B * S
    TPC = T // N_CORES
    xT = np.ascontiguousarray(x.reshape(T, D).T)
    pos = np.asarray(positions).reshape(-1).astype(np.int32)
    posT = np.ascontiguousarray(pos.reshape(T // 128, 128).T)
    bf16 = ml_dtypes.bfloat16
    for l in range(4):
        j = l // 2
        if l % 2 == 0:
            base = pre_inputs(np.asarray(norm_mix[l], f32))
            ins = [dict(base, xT=np.ascontiguousarray(xT[:, c * TPC:(c + 1) * TPC])) for c in range(N_CORES)]
            res = _run("pre_h", ins)
            hT = np.ascontiguousarray(np.concatenate([r["hT"] for r in res], axis=1))
            cst = mixh_consts()
            ins = []
            for c in range(N_CORES):
                w, vec = mixh_inputs(c, np.asarray(hgrn_w_in[j], f32), np.asarray(hgrn_lower_bounds, f32),
                                     np.asarray(hgrn_out_norm[j], f32), j)
                ins.append({"hT": hT, "w_in": w, "consts": cst, "vec": vec})
            res = _run("mix_h", ins)
            w_out = np.asarray(hgrn_w_out[j], f32)
        else:
            base = pre_inputs(np.asarray(norm_mix[l], f32), np.asarray(mla_q_a_norm[j], f32),
                              np.asarray(mla_kv_a_norm[j], f32), np.asarray(mla_w_in[j], f32))
            ins = [dict(base, xT=np.ascontiguousarray(xT[:, c * TPC:(c + 1) * TPC])) for c in range(N_CORES)]
            res = _run("pre_m", ins)
            latT = np.ascontiguousarray(np.concatenate([r["latT"] for r in res], axis=1))
            cst = mixm_consts()
            ins = []
            for c in range(N_CORES):
                wq, wkv, gvec = mixm_inputs(c, np.asarray(mla_w_q_up[j], f32), np.asarray(mla_w_kv_up[j], f32),
                                            np.asarray(mla_q_norm[j], f32), np.asarray(mla_k_norm[j], f32))
                ins.append({"latT": latT, "posT": posT, "wq": wq, "wkv": wkv, "gvec": gvec, "consts": cst})
            res = _run("mix_m", ins)
            w_out = np.asarray(mla_w_out[j], f32)
        oT = np.concatenate([r["oT"] for r in res], axis=0)
        wo, wu, wd, vec = post_layout_weights(w_out, np.asarray(ffn_w_up[l], f32), np.asarray(ffn_w_down[l], f32),
                                              np.asarray(norm_ffn[l], f32), np.asarray(ffn_conv_w[l], f32),
                                              np.asarray(ffn_conv_b[l], f32))
        ins = []
        for c in range(N_CORES):
            v = vec.copy()
            a = c * TPC
            if a % S == 0:
                xs = np.concatenate([np.zeros((D, P_HALO), f32), xT[:, a:a + TPC]], axis=1)
                os_ = np.concatenate([np.zeros((D, P_HALO), bf16), oT[:, a:a + TPC]], axis=1)
                v[:, 8 + 176] = 0.0
            else:
                xs = xT[:, a - P_HALO:a + TPC]
                os_ = oT[:, a - P_HALO:a + TPC]
                v[:, 8 + 176] = 1.0
            ins.append({"xT": np.ascontiguousarray(xs), "oT": np.ascontiguousarray(os_), "w_out": wo, "w_up": wu,
                        "w_down": wd, "vec": v})
        res = _run("post", ins)
        xT = np.ascontiguousarray(np.concatenate([r["yT"] for r in res], axis=1))
    return np.ascontiguousarray(xT.T).reshape(B, S, D).astype(f32)
```

```python
import math
from concourse.bass_utils import run_bass_kernel_spmd


import numpy as np
import os
from contextlib import ExitStack
import concourse.bass as bass
import concourse.mybir as mybir

F32 = mybir.dt.float32
BF16 = mybir.dt.bfloat16
I32 = mybir.dt.int32
AF = mybir.ActivationFunctionType
ALU = mybir.AluOpType
AX = mybir.AxisListType

ENGS = ("pe", "act", "dve", "pool", "sp")
BLK = {"pe": "tensor", "act": "scalar", "dve": "vector", "pool": "gpsimd", "sp": "sync"}
GEN = 16000
KDMA = 8


class Op:
    __slots__ = ("eng", "fn", "dma", "seq", "deps", "signal", "ordinal", "didx", "dsem", "dval")

    def __init__(self, eng, fn, dma):
        self.eng = eng
        self.fn = fn
        self.dma = dma
        self.deps = []
        self.signal = False
        self.ordinal = 0
        self.didx = -1


class Prog:
    def __init__(self, nc, same_sync=True):
        self.nc = nc
        self.same_sync = same_sync
        self.ops = {e: [] for e in ENGS}
        self.last_w = {}
        self.rd_eng = {}
        self.rd_dma = {}
        self.ndma = {e: 0 for e in ENGS}
        self.stack = ExitStack()
        self._nm = 0

    def sb(self, shape, dt, name=None):
        self._nm += 1
        return self.stack.enter_context(self.nc.sbuf_tensor(name or f"sb{self._nm}", list(shape), dt))

    def ps(self, shape=(128, 512), dt=F32, name=None):
        self._nm += 1
        return self.stack.enter_context(self.nc.psum_tensor(name or f"ps{self._nm}", list(shape), dt))

    def add(self, eng, fn, reads=(), writes=(), dma=False, force=False):
        self.nadd = getattr(self, "nadd", 0) + 1
        lim = int(os.environ.get("FW_LIMIT", "0"))
        if lim and self.nadd > lim and not force:
            return None
        excl = getattr(self, "excl", None)
        if excl:
            mv = [b for b in reads if b in excl]
            if mv:
                reads = [b for b in reads if b not in excl]
                writes = list(writes) + mv
        op = Op(eng, fn, dma)
        op.seq = len(self.ops[eng])
        if dma:
            op.didx = self.ndma[eng]
            self.ndma[eng] += 1
        deps = {}
        for b in reads:
            lw = self.last_w.get(b)
            if lw is not None:
                deps[id(lw)] = lw
        for b in writes:
            lw = self.last_w.get(b)
            if lw is not None:
                deps[id(lw)] = lw
            for r in self.rd_eng.get(b, {}).values():
                deps[id(r)] = r
            for r in self.rd_dma.get(b, ()):
                deps[id(r)] = r
        for d in deps.values():
            if d is op:
                continue
            if d.dma:
                op.deps.append(d)
            elif d.eng == eng:
                if eng == "pe" and not dma:
                    continue
                if self.same_sync or dma:
                    d.signal = True
                    op.deps.append(d)
            else:
                d.signal = True
                op.deps.append(d)
        for b in reads:
            if dma:
                self.rd_dma.setdefault(b, []).append(op)
            else:
                self.rd_eng.setdefault(b, {})[eng] = op
        for b in writes:
            self.last_w[b] = op
            self.rd_eng[b] = {}
            self.rd_dma[b] = []
        self.ops[eng].append(op)
        return op

    def dma(self, eng, out, in_, reads=(), writes=(), force=False):
        return self.add(eng, lambda e: e.dma_start(out=out, in_=in_), reads, writes, dma=True, force=force)

    def finish(self, ops, eng="sp"):
        op = Op(eng, None, False)
        op.seq = len(self.ops[eng])
        for d in ops:
            if d is None:
                continue
            if not d.dma:
                d.signal = True
            op.deps.append(d)
        self.ops[eng].append(op)

    def emit(self):
        nc = self.nc
        st = self.stack
        csem = {}
        for e in ENGS:
            n = 0
            for op in self.ops[e]:
                if (not op.dma) and op.signal:
                    n += 1
                    op.ordinal = n
            ngen = (n + GEN - 1) // GEN
            csem[e] = [st.enter_context(nc.semaphore(f"c_{e}_{g}")) for g in range(ngen)]
        dsem = {}
        for e in ENGS:
            k = min(KDMA, self.ndma[e])
            dsem[e] = [st.enter_context(nc.semaphore(f"d_{e}_{i}")) for i in range(k)]
            dl = [op for op in self.ops[e] if op.dma]
            for op in dl:
                op.dsem = dsem[e][op.didx % KDMA]
                op.dval = 16 * (op.didx // KDMA + 1)
        self._dl = {e: [op for op in self.ops[e] if op.dma] for e in ENGS}

        def emit_engine(e, E):
            waited = {f: -1 for f in ENGS}
            wdma = set()
            for op in self.ops[e]:
                for d in op.deps:
                    if d.dma:
                        if id(d) not in wdma:
                            E.wait_ge(d.dsem, d.dval)
                            wdma.add(id(d))
                    else:
                        if d.seq > waited[d.eng]:
                            g = (d.ordinal - 1) // GEN
                            E.wait_ge(csem[d.eng][g], (d.ordinal - 1) % GEN + 1)
                            waited[d.eng] = d.seq
                if op.fn is None:
                    continue
                if op.dma:
                    if op.didx >= KDMA:
                        prev = self._dl[e][op.didx - KDMA]
                        if id(prev) not in wdma:
                            E.wait_ge(prev.dsem, prev.dval)
                            wdma.add(id(prev))
                    ins = op.fn(E)
                    ins.then_inc(op.dsem, 16)
                else:
                    ins = op.fn(E)
                    if op.signal:
                        g = (op.ordinal - 1) // GEN
                        ins.then_inc(csem[e][g], 1)

        with nc.Block() as block:
            for e in ENGS:
                if not self.ops[e]:
                    continue
                getattr(block, BLK[e])(lambda E, e=e: emit_engine(e, E))
        self.stack.close()
        return nc


P_TW = 1024
P_HALO = 2
P_TC = P_TW + P_HALO
P_NT = 2
P_COLS = P_NT * P_TW + P_HALO
NTILES3 = [(0, 342), (342, 684), (684, 1026)]
NV = 8 + 44 * 3 + 44 + 1


def build_post(eps=1e-6):
    nc = bass.Bass("TRN2", target_bir_lowering=False)
    xT = nc.dram_tensor("xT", [1024, P_COLS], F32, kind="ExternalInput").ap()
    oT = nc.dram_tensor("oT", [1024, P_COLS], BF16, kind="ExternalInput").ap()
    w_out = nc.dram_tensor("w_out", [8, 128, 1024], F32, kind="ExternalInput").ap()
    w_up = nc.dram_tensor("w_up", [22, 128, 2048], F32, kind="ExternalInput").ap()
    w_down = nc.dram_tensor("w_down", [8, 128, 2816], F32, kind="ExternalInput").ap()
    vec = nc.dram_tensor("vec", [128, NV], F32, kind="ExternalInput").ap()
    yT = nc.dram_tensor("yT", [1024, P_NT * P_TW], F32, kind="ExternalOutput").ap()

    p = Prog(nc)
    xt = p.sb([128, 8, P_TC], F32, "xt")
    ot = p.sb([128, 8, P_TC], BF16, "ot")
    ht = p.sb([128, 8, P_TC], BF16, "ht")
    at = p.sb([128, 22, P_TW], BF16, "at")
    sq = [p.sb([128, 512], BF16, f"sq{i}") for i in range(2)]
    rs = p.sb([128, P_TC], F32, "rs")
    ub = [[p.sb([128, P_TC], F32, f"u{i}{h}") for h in range(2)] for i in range(2)]
    yb = [p.sb([128, P_TW], F32, f"y{h}") for h in range(2)]
    NWB = 4
    wbf = [p.sb([128, 2816], BF16, f"wbf{i}") for i in range(NWB)]
    vt = p.sb([128, NV], F32, "vt")
    ones = p.sb([128, 128], BF16, "ones")
    epst = p.sb([128, 1], F32, "epst")
    acc = [p.ps(name=f"acc{i}") for i in range(2)]
    ups = [p.ps(name=f"ups{i}") for i in range(6)]

    xT_v = xT.rearrange("(k p) c -> p k c", p=128)
    oT_v = oT.rearrange("(k p) c -> p k c", p=128)
    yT_v = yT.rearrange("(k p) c -> p k c", p=128)

    p.dma("sp", vt[:, :], vec[:, :], writes=["vt"])
    p.add("pool", lambda e: e.memset(ones[:, :], 1.0), writes=["ones"])
    p.add("pool", lambda e: e.memset(epst[:, :], eps), writes=["eps"])

    G0 = 0
    CW0 = 8
    CB0 = 8 + 132
    FL0 = 8 + 132 + 44

    cnt = {"w": 0, "wl": 0, "acc": 0, "ups": 0, "sq": 0, "u": 0}
    stores = []

    wlist = []
    for _t in range(P_NT):
        wlist += [(w_out[m], 1024) for m in range(8)]
        wlist += [(w_up[j], 2048) for j in range(22)]
        wlist += [(w_down[m], 2816) for m in range(8)]
    PF = NWB - 1

    def load_w(src_ap, ncols):
        i = cnt["w"]
        cnt["w"] += 1
        while cnt["wl"] <= min(i + PF, len(wlist) - 1):
            li = cnt["wl"]
            cnt["wl"] += 1
            ap, nco = wlist[li]
            p.dma("pool", wbf[li % NWB][:, 0:nco], ap, writes=[("wbf", li % NWB)])
        return i % NWB

    for t in range(P_NT):
        c0 = t * P_TW
        for k in range(8):
            p.dma("sp", xt[:, k, :], xT_v[:, k, c0:c0 + P_TC], writes=[("xt", k)])
        for k in range(8):
            p.dma("sp", ot[:, k, :], oT_v[:, k, c0:c0 + P_TC], writes=[("ot", k)])
        for m in range(8):
            b = load_w(w_out[m], 1024)
            for (a0, a1) in NTILES3:
                n = a1 - a0
                pb = cnt["acc"] % 2
                cnt["acc"] += 1
                for k in range(8):
                    p.add("pe", lambda e, pb=pb, n=n, b=b, k=k, a0=a0, a1=a1: e.matmul(
                        acc[pb][:, 0:n], wbf[b][:, k * 128:(k + 1) * 128], ot[:, k, a0:a1],
                        start=(k == 0), stop=(k == 7)),
                        reads=[("wbf", b), ("ot", k)], writes=[("acc", pb)])
                p.add("dve", lambda e, pb=pb, n=n, m=m, a0=a0, a1=a1: e.tensor_tensor(
                    xt[:, m, a0:a1], xt[:, m, a0:a1], acc[pb][:, 0:n], ALU.add),
                    reads=[("acc", pb), ("xt", m)], writes=[("xt", m)])
        for ti, (a0, a1) in enumerate(NTILES3):
            n = a1 - a0
            pb = cnt["acc"] % 2
            cnt["acc"] += 1
            for k in range(8):
                s = cnt["sq"] % 2
                cnt["sq"] += 1
                p.add("act", lambda e, s=s, n=n, k=k, a0=a0, a1=a1: e.activation(
                    sq[s][:, 0:n], xt[:, k, a0:a1], AF.Square),
                    reads=[("xt", k)], writes=[("sq", s)])
                p.add("pe", lambda e, pb=pb, n=n, s=s, k=k: e.matmul(
                    acc[pb][:, 0:n], ones[:, :], sq[s][:, 0:n], start=(k == 0), stop=(k == 7)),
                    reads=[("sq", s), "ones"], writes=[("acc", pb)])
            p.add("act", lambda e, pb=pb, n=n, a0=a0, a1=a1: e.activation(
                rs[:, a0:a1], acc[pb][:, 0:n], AF.Sqrt, bias=epst[:, 0:1], scale=1.0 / 1024.0),
                reads=[("acc", pb), "eps"], writes=[("rs", ti)])
            p.add("dve", lambda e, a0=a0, a1=a1: e.reciprocal(rs[:, a0:a1], rs[:, a0:a1]),
                  reads=[("rs", ti)], writes=[("rs", ti)])
        for k in range(8):
            p.add("dve", lambda e, k=k: e.scalar_tensor_tensor(
                ht[:, k, :], xt[:, k, :], vt[:, G0 + k:G0 + k + 1], rs[:, :], ALU.mult, ALU.mult),
                reads=[("xt", k), "vt", ("rs", 0), ("rs", 1), ("rs", 2)], writes=[("ht", k)])
            if t == 0:
                p.add("dve", lambda e, k=k: e.tensor_scalar(
                    ht[:, k, 0:P_HALO], ht[:, k, 0:P_HALO], vt[:, FL0:FL0 + 1], None, ALU.mult),
                    reads=[("ht", k), "vt"], writes=[("ht", k)])
        for j in range(22):
            b = load_w(w_up[j], 2048)
            ui = cnt["u"] % 2
            cnt["u"] += 1
            for half in range(2):
                ch = j + 22 * half
                for (a0, a1) in NTILES3:
                    n = a1 - a0
                    r = cnt["ups"] % 6
                    cnt["ups"] += 1
                    for k in range(8):
                        p.add("pe", lambda e, r=r, n=n, b=b, k=k, half=half, a0=a0, a1=a1: e.matmul(
                            ups[r][:, 0:n], wbf[b][:, k * 256 + half * 128:k * 256 + half * 128 + 128],
                            ht[:, k, a0:a1], start=(k == 0), stop=(k == 7)),
                            reads=[("wbf", b), ("ht", k)], writes=[("ups", r)])
                    p.add("act", lambda e, r=r, n=n, ui=ui, half=half, a0=a0, a1=a1: e.activation(
                        ub[ui][half][:, a0:a1], ups[r][:, 0:n], AF.Copy),
                        reads=[("ups", r)], writes=[("u", ui, half)])
                u = ub[ui][half]
                y = yb[half]
                w0 = vt[:, CW0 + 0 * 44 + ch:CW0 + 0 * 44 + ch + 1]
                w1 = vt[:, CW0 + 1 * 44 + ch:CW0 + 1 * 44 + ch + 1]
                w2 = vt[:, CW0 + 2 * 44 + ch:CW0 + 2 * 44 + ch + 1]
                bb = vt[:, CB0 + ch:CB0 + ch + 1]
                p.add("dve", lambda e, u=u, y=y, w2=w2, bb=bb: e.tensor_scalar(
                    y[:, :], u[:, 2:P_TC], w2, bb, ALU.mult, ALU.add),
                    reads=[("u", ui, half), "vt"], writes=[("y", half)])
                p.add("dve", lambda e, u=u, y=y, w1=w1: e.scalar_tensor_tensor(
                    y[:, :], u[:, 1:P_TC - 1], w1, y[:, :], ALU.mult, ALU.add),
                    reads=[("u", ui, half), "vt", ("y", half)], writes=[("y", half)])
                p.add("dve", lambda e, u=u, y=y, w0=w0: e.scalar_tensor_tensor(
                    y[:, :], u[:, 0:P_TC - 2], w0, y[:, :], ALU.mult, ALU.add),
                    reads=[("u", ui, half), "vt", ("y", half)], writes=[("y", half)])
            p.add("act", lambda e: e.activation(yb[0][:, :], yb[0][:, :], AF.Silu),
                  reads=[("y", 0)], writes=[("y", 0)])
            p.add("pool", lambda e, j=j: e.tensor_tensor(at[:, j, :], yb[0][:, :], yb[1][:, :], ALU.mult),
                  reads=[("y", 0), ("y", 1)], writes=[("at", j)])
        for m in range(8):
            b = load_w(w_down[m], 2816)
            for h2 in range(2):
                pb = cnt["acc"] % 2
                cnt["acc"] += 1
                for j in range(22):
                    p.add("pe", lambda e, pb=pb, b=b, j=j, h2=h2: e.matmul(
                        acc[pb][:, :], wbf[b][:, j * 128:(j + 1) * 128], at[:, j, h2 * 512:(h2 + 1) * 512],
                        start=(j == 0), stop=(j == 21)),
                        reads=[("wbf", b), ("at", j)], writes=[("acc", pb)])
                x0 = P_HALO + h2 * 512
                p.add("dve", lambda e, pb=pb, m=m, x0=x0: e.tensor_tensor(
                    xt[:, m, x0:x0 + 512], xt[:, m, x0:x0 + 512], acc[pb][:, :], ALU.add),
                    reads=[("acc", pb), ("xt", m)], writes=[("xt", m)])
            stores.append(p.dma("sp", yT_v[:, m, c0:c0 + P_TW], xt[:, m, P_HALO:P_TC], reads=[("xt", m)]))
    p.finish(stores)
    return p.emit()


def post_layout_weights(w_out, w_up, w_down, g, conv_w, conv_b):
    wo = np.ascontiguousarray(w_out.reshape(8, 128, 8, 128).transpose(2, 1, 0, 3).reshape(8, 128, 1024))
    wu = w_up.reshape(8, 128, 2, 22, 128)
    wu = np.ascontiguousarray(wu.transpose(3, 1, 0, 2, 4).reshape(22, 128, 2048))
    wd = w_down.reshape(22, 128, 8, 128)
    wd = np.ascontiguousarray(wd.transpose(2, 1, 0, 3).reshape(8, 128, 2816))
    vec = np.zeros((128, NV), np.float32)
    vec[:, 0:8] = g.reshape(8, 128).T
    for tap in range(3):
        vec[:, 8 + tap * 44:8 + (tap + 1) * 44] = conv_w[tap].reshape(44, 128).T
    vec[:, 8 + 132:8 + 176] = conv_b.reshape(44, 128).T
    return wo, wu, wd, vec


PR_TW = 512
PR_NT = 2048 // PR_TW


def build_pre(mode, eps=1e-6):
    nc = bass.Bass("TRN2", target_bir_lowering=False)
    xT = nc.dram_tensor("xT", [1024, 2048], F32, kind="ExternalInput").ap()
    vec = nc.dram_tensor("vec", [128, 12], F32, kind="ExternalInput").ap()
    if mode == "m":
        w_in = nc.dram_tensor("w_in", [128, 8 * 640], F32, kind="ExternalInput").ap()
        outT = nc.dram_tensor("latT", [640, 2048], BF16, kind="ExternalOutput").ap()
    else:
        outT = nc.dram_tensor("hT", [1024, 2048], BF16, kind="ExternalOutput").ap()
    p = Prog(nc)
    xt = [p.sb([128, 8, PR_TW], F32, f"xt{i}") for i in range(2)]
    ht = [p.sb([128, 8, PR_TW], BF16, f"ht{i}") for i in range(2)]
    sq = [p.sb([128, PR_TW], BF16, f"sq{i}") for i in range(2)]
    rs = p.sb([128, PR_TW], F32, "rs")
    vt = p.sb([128, 12], F32, "vt")
    ones = p.sb([128, 128], BF16, "ones")
    epst = p.sb([128, 1], F32, "epst")
    acc = [p.ps(name=f"acc{i}") for i in range(2)]
    if mode == "m":
        wst = p.sb([128, 8 * 640], F32, "wst")
        wbf = p.sb([128, 8 * 640], BF16, "wbf")
        lat = p.sb([128, 5, PR_TW], F32, "lat")
        lo = [p.sb([128, 5, PR_TW], BF16, f"lo{i}") for i in range(2)]
        rs2 = p.sb([128, PR_TW], F32, "rs2")
        lps = [p.ps(name=f"lps{i}") for i in range(2)]
        p.dma("sp", wst[:, :], w_in[:, :], writes=["wst"])
        p.add("pool", lambda e: e.tensor_copy(wbf[:, :], wst[:, :]), reads=["wst"], writes=["wbf"])
    xT_v = xT.rearrange("(k p) c -> p k c", p=128)
    oT_v = outT.rearrange("(k p) c -> p k c", p=128)
    p.dma("sp", vt[:, :], vec[:, :], writes=["vt"])
    p.add("pool", lambda e: e.memset(ones[:, :], 1.0), writes=["ones"])
    p.add("pool", lambda e: e.memset(epst[:, :], eps), writes=["eps"])
    cnt = {"acc": 0, "sq": 0, "lps": 0}
    stores = []

    def rms_stat(srcs, skeys, dst, dkey, n_feat):
        pb = cnt["acc"] % 2
        cnt["acc"] += 1
        for i, (src, sk) in enumerate(zip(srcs, skeys)):
            s = cnt["sq"] % 2
            cnt["sq"] += 1
            p.add("act", lambda e, s=s, src=src: e.activation(sq[s][:, :], src, AF.Square), reads=[sk], writes=[("sq", s)])
            p.add("pe", lambda e, pb=pb, s=s, i=i: e.matmul(acc[pb][:, :], ones[:, :], sq[s][:, :], start=(i == 0), stop=(i == len(srcs) - 1)),
                  reads=[("sq", s), "ones"], writes=[("acc", pb)])
        p.add("act", lambda e, pb=pb: e.activation(dst[:, :], acc[pb][:, :], AF.Sqrt, bias=epst[:, 0:1], scale=1.0 / n_feat),
              reads=[("acc", pb), "eps"], writes=[dkey])
        p.add("dve", lambda e: e.reciprocal(dst[:, :], dst[:, :]), reads=[dkey], writes=[dkey])

    for t in range(PR_NT):
        c0 = t * PR_TW
        xb = xt[t % 2]
        hb = ht[t % 2]
        xk = ("xt", t % 2)
        hk = ("ht", t % 2)
        for k in range(8):
            p.dma("sp", xb[:, k, :], xT_v[:, k, c0:c0 + PR_TW], writes=[(xk, k)])
        rms_stat([xb[:, k, :] for k in range(8)], [(xk, k) for k in range(8)], rs, "rs", 1024.0)
        for k in range(8):
            p.add("dve", lambda e, k=k, xb=xb, hb=hb: e.scalar_tensor_tensor(
                hb[:, k, :], xb[:, k, :], vt[:, k:k + 1], rs[:, :], ALU.mult, ALU.mult),
                reads=[(xk, k), "vt", "rs"], writes=[(hk, k)])
            if mode == "h":
                stores.append(p.dma("sp", oT_v[:, k, c0:c0 + PR_TW], hb[:, k, :], reads=[(hk, k)]))
        if mode == "m":
            lob = lo[t % 2]
            lk = ("lo", t % 2)
            for oc in range(5):
                pb = cnt["lps"] % 2
                cnt["lps"] += 1
                for k in range(8):
                    p.add("pe", lambda e, pb=pb, k=k, oc=oc, hb=hb: e.matmul(
                        lps[pb][:, :], wbf[:, k * 640 + oc * 128:k * 640 + oc * 128 + 128], hb[:, k, :],
                        start=(k == 0), stop=(k == 7)),
                        reads=["wbf", (hk, k)], writes=[("lps", pb)])
                if oc < 4:
                    p.add("act", lambda e, pb=pb, oc=oc: e.activation(lat[:, oc, :], lps[pb][:, :], AF.Copy),
                          reads=[("lps", pb)], writes=[("lat", oc)])
                else:
                    p.add("act", lambda e, pb=pb, lob=lob: e.activation(lob[:, 4, :], lps[pb][:, :], AF.Copy),
                          reads=[("lps", pb)], writes=[(lk, 4)])
            for grp in range(2):
                rms_stat([lat[:, 2 * grp + i, :] for i in range(2)], [("lat", 2 * grp + i) for i in range(2)], rs2, "rs2", 256.0)
                for i in range(2):
                    oc = 2 * grp + i
                    p.add("dve", lambda e, oc=oc, lob=lob, grp=grp, i=i: e.scalar_tensor_tensor(
                        lob[:, oc, :], lat[:, oc, :], vt[:, 8 + 2 * grp + i:9 + 2 * grp + i], rs2[:, :], ALU.mult, ALU.mult),
                        reads=[("lat", oc), "vt", "rs2"], writes=[(lk, oc)])
            for oc in range(5):
                stores.append(p.dma("sp", oT_v[:, oc, c0:c0 + PR_TW], lob[:, oc, :], reads=[(lk, oc)]))
    p.finish(stores)
    return p.emit()


def pre_inputs(g_mix, q_a=None, kv_a=None, w_in=None):
    vec = np.zeros((128, 12), np.float32)
    vec[:, 0:8] = g_mix.reshape(8, 128).T
    out = {"vec": vec}
    if w_in is not None:
        vec[:, 8:10] = q_a.reshape(2, 128).T
        vec[:, 10:12] = kv_a.reshape(2, 128).T
        w = np.zeros((1024, 640), np.float32)
        w[:, :576] = w_in
        out["w_in"] = np.ascontiguousarray(w.reshape(8, 128, 640).transpose(1, 0, 2).reshape(128, 8 * 640))
    return out


H_TB = 1024
H_NB = 16384 // H_TB
H_SEQ_BLOCKS = 8192 // H_TB
HC_MASK = 0
HC_IDENT = 128
HC_RMASK = 256
HC_CMA = 256 + H_TB
HC_N = 256 + 2 * H_TB
HV_N = 5


def build_mixh(eps=1e-6, nblocks=H_NB, upto=9):
    nc = bass.Bass("TRN2", target_bir_lowering=False)
    hT = nc.dram_tensor("hT", [1024, 16384], BF16, kind="ExternalInput").ap()
    w_in = nc.dram_tensor("w_in", [128, 8 * 512], F32, kind="ExternalInput").ap()
    consts = nc.dram_tensor("consts", [128, HC_N], F32, kind="ExternalInput").ap()
    vec = nc.dram_tensor("vec", [128, HV_N], F32, kind="ExternalInput").ap()
    oT = nc.dram_tensor("oT", [128, 16384], BF16, kind="ExternalOutput").ap()

    p = Prog(nc)
    TB = H_TB
    NTL = TB // 128
    cst = p.sb([128, HC_N], F32, "cst")
    vt = p.sb([128, HV_N], F32, "vt")
    lbt = p.sb([128, 4], F32, "lbt")
    wst = p.sb([128, 4096], F32, "wst")
    wbf = p.sb([128, 4096], BF16, "wbf")
    ident = p.sb([128, 128], BF16, "ident")
    bmask = p.sb([128, 128], F32, "bmask")
    ones = p.sb([128, 128], BF16, "ones")
    epst = p.sb([128, 1], F32, "epst")
    ht = [p.sb([128, 8, TB], BF16, f"ht{i}") for i in range(2)]
    qs = p.sb([128, TB], F32, "qs")
    fg = p.sb([128, TB], F32, "fg")
    lf = p.sb([128, TB], F32, "lf")
    kk = p.sb([128, TB], F32, "kk")
    G = p.sb([128, TB], F32, "G")
    Gc = p.sb([128, TB], F32, "Gc")
    E0 = p.sb([128, TB], F32, "E0")
    E1 = p.sb([128, TB], F32, "E1")
    gs = p.sb([128, TB], F32, "gs")
    vT = p.sb([128, TB], BF16, "vT")
    qt = p.sb([128, TB], BF16, "qt")
    kt = p.sb([128, TB], BF16, "kt")
    kh = p.sb([128, TB], BF16, "kh")
    vtok = p.sb([128, NTL, 128], BF16, "vtok")
    khA = p.sb([128, NTL, 128], BF16, "khA")
    khB = p.sb([128, NTL, 128], BF16, "khB")
    qhA = p.sb([128, TB], BF16, "qhA")
    qhB = p.sb([128, TB], BF16, "qhB")
    scm = p.sb([128, NTL, 128], BF16, "scm")
    Sf = p.sb([128, 128], F32, "Sf")
    Sb = [p.sb([128, 128], BF16, f"Sb{i}") for i in range(2)]
    osb = p.sb([128, TB], F32, "osb")
    osq = [p.sb([128, 512], BF16, f"osq{i}") for i in range(2)]
    rs = p.sb([128, TB], F32, "rs")
    ofin = [p.sb([128, TB], BF16, f"ofin{i}") for i in range(2)]

    pin = [p.ps(name=f"pin{i}") for i in range(2)]
    ptr = p.ps([128, 1024], BF16, "ptr")
    psc = p.ps(name="psc")
    pst = [p.ps(name=f"pst{i}") for i in range(2)]
    po = [p.ps(name=f"po{i}") for i in range(2)]

    hT_v = hT.rearrange("(k p) t -> p k t", p=128)

    p.dma("sp", cst[:, :], consts[:, :], writes=["cst"])
    p.dma("sp", vt[:, :], vec[:, :], writes=["vt"])
    p.dma("sp", wst[:, :], w_in[:, :], writes=["wst"])
    p.add("pool", lambda e: e.tensor_copy(wbf[:, :], wst[:, :]), reads=["wst"], writes=["wbf"])
    p.add("dve", lambda e: e.tensor_copy(ident[:, :], cst[:, HC_IDENT:HC_IDENT + 128]), reads=["cst"], writes=["ident"])
    p.add("dve", lambda e: e.tensor_copy(bmask[:, :], cst[:, HC_MASK:HC_MASK + 128]), reads=["cst"], writes=["bmask"])
    p.add("pool", lambda e: e.memset(ones[:, :], 1.0), writes=["ones"])
    p.add("pool", lambda e: e.memset(epst[:, :], eps), writes=["eps"])
    p.add("pool", lambda e: e.memset(khA[:, :, :], 0.0), writes=["khA"])
    p.add("pool", lambda e: e.memset(khB[:, :, :], 0.0), writes=["khB"])
    if upto < 9:
        for i in range(2):
            p.add("pool", lambda e, i=i: e.memset(ofin[i][:, :], 0.0), writes=[("ofin", i)])
    p.add("act", lambda e: e.activation(lbt[:, 0:2], vt[:, 0:2], AF.Exp), reads=["vt"], writes=["lbt"])
    p.add("dve", lambda e: e.tensor_tensor(lbt[:, 2:3], lbt[:, 0:1], lbt[:, 1:2], ALU.add), reads=["lbt"], writes=["lbt"])
    p.add("dve", lambda e: e.reciprocal(lbt[:, 2:3], lbt[:, 2:3]), reads=["lbt"], writes=["lbt"])
    p.add("dve", lambda e: e.tensor_tensor(lbt[:, 0:2], lbt[:, 0:2], vt[:, 2:4], ALU.mult), reads=["lbt", "vt"], writes=["lbt"])
    p.add("dve", lambda e: e.tensor_tensor(lbt[:, 3:4], lbt[:, 0:1], lbt[:, 1:2], ALU.add), reads=["lbt"], writes=["lbt"])
    p.add("dve", lambda e: e.tensor_tensor(lbt[:, 0:1], lbt[:, 3:4], lbt[:, 2:3], ALU.mult), reads=["lbt"], writes=["lbt"])
    p.add("dve", lambda e: e.tensor_scalar(lbt[:, 1:2], lbt[:, 0:1], -1.0, 1.0, ALU.mult, ALU.add), reads=["lbt"], writes=["lbt"])
    LB = lbt[:, 0:1]
    OML = lbt[:, 1:2]
    ON = vt[:, 4:5]
    rmask = cst[:, HC_RMASK:HC_RMASK + TB]

    cnt = {"pin": 0, "tr": 0, "sc": 0, "st": 0, "scm": 0, "S": 0, "osq": 0}
    stores = []

    def bc_col(t, col):
        v = t[:, :].rearrange("p (n c) -> p n c", c=64)
        return v[:, :, col:col + 1].broadcast(2, 64) if hasattr(v, "broadcast") else None

    for blk in range(nblocks):
      for _once in (0,):
          hb = ht[blk % 2]
          hk = ("ht", blk % 2)
          t0 = blk * TB
          for k in range(8):
              p.dma("sp", hb[:, k, :], hT_v[:, k, t0:t0 + TB], writes=[(hk, k)])
          if blk % H_SEQ_BLOCKS == 0:
              p.add("pool", lambda e: e.memset(Sf[:, :], 0.0), writes=["Sf"])
              si = cnt["S"] % 2
              p.add("pool", lambda e, si=si: e.memset(Sb[si][:, :], 0.0), writes=[("Sb", si)])
          for grp in range(4):
              for nt in range(TB // 512):
                  pb = cnt["pin"] % 2
                  cnt["pin"] += 1
                  for k in range(8):
                      p.add("pe", lambda e, pb=pb, k=k, grp=grp, nt=nt, hb=hb: e.matmul(
                          pin[pb][:, :], wbf[:, k * 512 + grp * 128:k * 512 + grp * 128 + 128],
                          hb[:, k, nt * 512:(nt + 1) * 512], start=(k == 0), stop=(k == 7)),
                          reads=["wbf", (hk, k)], writes=[("pin", pb)])
                  sl = slice(nt * 512, (nt + 1) * 512)
                  if grp == 0:
                      p.add("act", lambda e, pb=pb, sl=sl: e.activation(qs[:, sl], pin[pb][:, :], AF.Silu),
                            reads=[("pin", pb)], writes=["qs"])
                  elif grp == 1:
                      p.add("act", lambda e, pb=pb, sl=sl: e.activation(fg[:, sl], pin[pb][:, :], AF.Sigmoid),
                            reads=[("pin", pb)], writes=["fg"])
                  elif grp == 2:
                      p.add("act", lambda e, pb=pb, sl=sl: e.activation(vT[:, sl], pin[pb][:, :], AF.Copy),
                            reads=[("pin", pb)], writes=["vT"])
                  else:
                      p.add("act", lambda e, pb=pb, sl=sl: e.activation(gs[:, sl], pin[pb][:, :], AF.Silu),
                            reads=[("pin", pb)], writes=["gs"])
          if upto < 2: break
          p.add("dve", lambda e: e.tensor_scalar(fg[:, :], fg[:, :], OML, LB, ALU.mult, ALU.add),
                reads=["fg", "lbt"], writes=["fg"])
          p.add("act", lambda e: e.activation(lf[:, :], fg[:, :], AF.Ln), reads=["fg"], writes=["lf"])
          p.add("pool", lambda e: e.tensor_scalar(kk[:, :], fg[:, :], -1.0, 1.0, ALU.mult, ALU.add),
                reads=["fg"], writes=["kk"])
          p.add("dve", lambda e: e.tensor_tensor_scan(G[:, :], rmask, lf[:, :], 0.0, ALU.mult, ALU.add),
                reads=["lf", "cst"], writes=["G"])
          G3 = G[:, :].rearrange("p (n c) -> p n c", c=64)
          Gc3 = Gc[:, :].rearrange("p (n c) -> p n c", c=64)
          nch = TB // 64
          p.add("dve", lambda e: e.tensor_tensor(Gc3, G3, G3[:, :, 31:32].to_broadcast([128, nch, 64]), ALU.subtract),
                reads=["G"], writes=["Gc"])
          p.add("act", lambda e: e.activation(E0[:, :], Gc[:, :], AF.Exp), reads=["Gc"], writes=["E0"])
          p.add("dve", lambda e: e.tensor_tensor(qt[:, :], qs[:, :], E0[:, :], ALU.mult), reads=["qs", "E0"], writes=["qt"])
          p.add("act", lambda e: e.activation(E1[:, :], Gc[:, :], AF.Exp, scale=-1.0), reads=["Gc"], writes=["E1"])
          p.add("pool", lambda e: e.tensor_tensor(kt[:, :], kk[:, :], E1[:, :], ALU.mult), reads=["kk", "E1"], writes=["kt"])
          p.add("dve", lambda e: e.tensor_tensor(Gc3, G3, G3[:, :, 63:64].to_broadcast([128, nch, 64]), ALU.subtract),
                reads=["G"], writes=["Gc"])
          p.add("act", lambda e: e.activation(E1[:, :], Gc[:, :], AF.Exp, scale=-1.0), reads=["Gc"], writes=["E1"])
          p.add("pool", lambda e: e.tensor_tensor(kh[:, :], kk[:, :], E1[:, :], ALU.mult), reads=["kk", "E1"], writes=["kh"])
          p.add("act", lambda e: e.activation(E0[:, :], G[:, :], AF.Exp), reads=["G"], writes=["E0"])
          p.add("dve", lambda e: e.tensor_tensor(E1[:, :], qs[:, :], E0[:, :], ALU.mult), reads=["qs", "E0", "E1"], writes=["E1"])
          p.add("pool", lambda e: e.tensor_tensor(qhA[:, :], E1[:, :], cst[:, HC_CMA:HC_CMA + TB], ALU.mult), reads=["E1", "cst"], writes=["qhA"])
          p.add("dve", lambda e: e.tensor_tensor(qhB[:, :], E1[:, :], qhA[:, :], ALU.subtract), reads=["E1", "qhA"], writes=["qhB"])
          if upto < 3: break
          for tl in range(NTL):
              p.add("pe", lambda e, tl=tl: e.transpose(ptr[:, tl * 128:(tl + 1) * 128], vT[:, tl * 128:(tl + 1) * 128], ident[:, :]),
                    reads=["vT", "ident"], writes=["ptr"])
          p.add("dve", lambda e: e.tensor_copy(vtok[:, :, :].rearrange("p a b -> p (a b)"), ptr[:, :]),
                reads=["ptr"], writes=["vtok"])
          for tl in range(NTL):
              p.add("pe", lambda e, tl=tl: e.transpose(ptr[:, tl * 128:(tl + 1) * 128], kh[:, tl * 128:(tl + 1) * 128], ident[:, :]),
                    reads=["kh", "ident"], writes=["ptr"])
          p.add("dve", lambda e: e.tensor_copy(khA[0:64, :, :].rearrange("p a b -> p (a b)"), ptr[0:64, :]),
                reads=["ptr"], writes=["khA"])
          p.add("dve", lambda e: e.tensor_copy(khB[64:128, :, :].rearrange("p a b -> p (a b)"), ptr[64:128, :]),
                reads=["ptr"], writes=["khB"])
          for g4 in range(NTL // 4):
              for q in range(4):
                  c0 = (g4 * 4 + q) * 128
                  p.add("pe", lambda e, q=q, c0=c0: e.matmul(psc[:, q * 128:(q + 1) * 128], kt[:, c0:c0 + 128], qt[:, c0:c0 + 128],
                                                             start=True, stop=True),
                        reads=["kt", "qt"], writes=["psc"])
              p.add("dve", lambda e, g4=g4: e.tensor_tensor(
                  scm[:, g4 * 4:(g4 + 1) * 4, :], psc[:, :].rearrange("p (a b) -> p a b", b=128),
                  bmask[:, :].unsqueeze(1).to_broadcast([128, 4, 128]), ALU.mult),
                  reads=["psc", "bmask"], writes=[("scm", g4)])
          if upto < 5: break
          for tl in range(NTL):
              c0 = tl * 128
              if tl % 2 == 0:
                  sb_ = cnt["st"] % 2
                  cnt["st"] += 1
                  for q in range(4):
                      tile = tl + q // 2
                      khx = khA if q % 2 == 0 else khB
                      p.add("pe", lambda e, sb_=sb_, q=q, tile=tile, khx=khx: e.matmul(
                          pst[sb_][:, q * 128:(q + 1) * 128], khx[:, tile, :], vtok[:, tile, :], start=True, stop=True),
                          reads=["khA", "khB", "vtok"], writes=[("pst", sb_)])
              ob = (tl // 4) % 2
              oc = (tl % 4) * 128
              okey = ("po", ob)
              p.add("pe", lambda e, ob=ob, oc=oc, tl=tl: e.matmul(po[ob][:, oc:oc + 128], vtok[:, tl, :], scm[:, tl, :],
                                                             start=True, stop=False),
                    reads=["vtok", ("scm", tl // 4)], writes=[okey])
              for half in range(2):
                  si = cnt["S"] % 2
                  hc = c0 + half * 64
                  qhx = qhA if half == 0 else qhB
                  p.add("pe", lambda e, ob=ob, oc=oc, half=half, si=si, c0=c0, qhx=qhx: e.matmul(
                      po[ob][:, oc:oc + 128], Sb[si][:, :], qhx[:, c0:c0 + 128], start=False, stop=(half == 1)),
                      reads=[("Sb", si), "qhA", "qhB"], writes=[okey])
                  q = (tl % 2) * 2 + half
                  dcol = hc + 63
                  p.add("dve", lambda e, sb_=sb_, q=q, dcol=dcol: e.scalar_tensor_tensor(
                      Sf[:, :], Sf[:, :], E0[:, dcol:dcol + 1], pst[sb_][:, q * 128:(q + 1) * 128], ALU.mult, ALU.add),
                      reads=["Sf", "E0", ("pst", sb_)], writes=["Sf"])
                  cnt["S"] += 1
                  sn = cnt["S"] % 2
                  p.add("act", lambda e, sn=sn: e.activation(Sb[sn][:, :], Sf[:, :], AF.Copy),
                        reads=["Sf"], writes=[("Sb", sn)])
              if tl % 4 == 3:
                  o0 = (tl // 4) * 512
                  p.add("act", lambda e, ob=ob, o0=o0: e.activation(osb[:, o0:o0 + 512], po[ob][:, :], AF.Copy),
                        reads=[okey], writes=["osb"])
          for nt in range(TB // 512):
              sl = slice(nt * 512, (nt + 1) * 512)
              s = cnt["osq"] % 2
              cnt["osq"] += 1
              pb = cnt["pin"] % 2
              cnt["pin"] += 1
              p.add("act", lambda e, s=s, sl=sl: e.activation(osq[s][:, :], osb[:, sl], AF.Square),
                    reads=["osb"], writes=[("osq", s)])
              p.add("pe", lambda e, pb=pb, s=s: e.matmul(pin[pb][:, :], ones[:, :], osq[s][:, :], start=True, stop=True),
                    reads=[("osq", s), "ones"], writes=[("pin", pb)])
              p.add("act", lambda e, pb=pb, sl=sl: e.activation(rs[:, sl], pin[pb][:, :], AF.Sqrt, bias=epst[:, 0:1], scale=1.0 / 128.0),
                    reads=[("pin", pb), "eps"], writes=["rs"])
          p.add("dve", lambda e: e.reciprocal(rs[:, :], rs[:, :]), reads=["rs"], writes=["rs"])
          p.add("dve", lambda e: e.tensor_tensor(rs[:, :], rs[:, :], gs[:, :], ALU.mult), reads=["rs", "gs"], writes=["rs"])
          of = ofin[blk % 2]
          p.add("dve", lambda e, of=of: e.scalar_tensor_tensor(of[:, :], osb[:, :], ON, rs[:, :], ALU.mult, ALU.mult),
                reads=["osb", "vt", "rs"], writes=[("ofin", blk % 2)])
      of = ofin[blk % 2]
      stores.append(p.dma("sp", oT[:, t0:t0 + TB], of[:, :], reads=[("ofin", blk % 2)], force=True))
    p.finish(stores)
    return p.emit()


def mixh_consts():
    c = np.zeros((128, HC_N), np.float32)
    s = np.arange(128)[:, None]
    t = np.arange(128)[None, :]
    c[:, HC_MASK:HC_MASK + 128] = ((s // 64 == t // 64) & (t >= s)).astype(np.float32)
    c[:, HC_IDENT:HC_IDENT + 128] = np.eye(128, dtype=np.float32)
    rm = np.ones(H_TB, np.float32)
    rm[0::64] = 0.0
    c[:, HC_RMASK:HC_RMASK + H_TB] = rm[None, :]
    c[:, HC_CMA:HC_CMA + H_TB] = (((np.arange(H_TB) // 64) % 2) == 0).astype(np.float32)[None, :]
    return c


def mixh_inputs(head, w_in, lbp, out_norm, j):
    cols = np.concatenate([np.arange(g * 1024 + head * 128, g * 1024 + head * 128 + 128) for g in range(4)])
    w = w_in[:, cols]
    w = np.ascontiguousarray(w.reshape(8, 128, 512).transpose(1, 0, 2).reshape(128, 4096))
    vec = np.zeros((128, HV_N), np.float32)
    vec[:, 0] = lbp[0, head * 128:(head + 1) * 128]
    vec[:, 1] = lbp[1, head * 128:(head + 1) * 128]
    vec[:, 2] = 1.0 if (0 >= 1 and 0 <= j) else 0.0
    vec[:, 3] = 1.0 if (1 >= 1 and 1 <= j) else 0.0
    vec[:, 4] = out_norm
    return w, vec


M_G = 4
M_TPB = 64
MG_N = 192 * 2 + 32
MC_MASK = 0
MC_ID = 2048
MC_N = 2048 + 128
MAGIC = 12582912.0
TWO_PI = 2.0 * math.pi
C1 = 6.28125
C2 = TWO_PI - C1
PI_CL = 3.1415925


def build_mixm(eps=1e-6, nbatch=2, nq=16, ng=None):
    nc = bass.Bass("TRN2", target_bir_lowering=False)
    latT = nc.dram_tensor("latT", [640, 16384], BF16, kind="ExternalInput").ap()
    posT = nc.dram_tensor("posT", [128, 128], I32, kind="ExternalInput").ap()
    wq = nc.dram_tensor("wq", [128, 2 * 192], F32, kind="ExternalInput").ap()
    wkv = nc.dram_tensor("wkv", [128, 3 * 320], F32, kind="ExternalInput").ap()
    gvec = nc.dram_tensor("gvec", [128, MG_N], F32, kind="ExternalInput").ap()
    consts = nc.dram_tensor("consts", [128, MC_N], F32, kind="ExternalInput").ap()
    oT = nc.dram_tensor("oT", [128, 16384], BF16, kind="ExternalOutput").ap()

    p = Prog(nc)
    G = M_G
    cst = p.sb([128, MC_N], F32, "cst")
    gv = p.sb([128, MG_N], F32, "gv")
    wqs = p.sb([128, 384], F32, "wqs")
    wkvs = p.sb([128, 960], F32, "wkvs")
    wqb = p.sb([128, 384], BF16, "wqb")
    wkvb = p.sb([128, 960], BF16, "wkvb")
    masks = p.sb([128, 4, 512], BF16, "masks")
    ident = p.sb([128, 128], BF16, "ident")
    ones = p.sb([128, 128], BF16, "ones")
    epst = p.sb([128, 1], F32, "epst")
    hpi = p.sb([128, 1], F32, "hpi")
    posi = p.sb([128, 128], I32, "posi")
    posf = p.sb([128, 128], F32, "posf")
    cosT = p.sb([128, M_TPB, 32], F32, "cosT")
    sinT = p.sb([128, M_TPB, 32], F32, "sinT")
    tg1 = p.sb([128, M_TPB, 32], F32, "tg1")
    tg2 = p.sb([128, M_TPB, 32], F32, "tg2")
    latb = [p.sb([128, 5, 512], BF16, f"latb{i}") for i in range(2)]
    junk = p.sb([128, 192], F32, "junk")
    ss = p.sb([128, 8], F32, "ss")
    qn = p.sb([128, G, 192], F32, "qn")
    kn = p.sb([128, G, 192], F32, "kn")
    trq = [p.sb([128, G, 32], F32, f"trq{i}") for i in range(4)]
    trk = [p.sb([128, G, 32], F32, f"trk{i}") for i in range(4)]
    qf = p.sb([128, G, 128], BF16, "qf")
    kf = p.sb([128, G, 128], BF16, "kf")
    qfr = p.sb([128, G, 128], BF16, "qfr")
    kfr = p.sb([128, G, 128], BF16, "kfr")
    qTm = p.sb([128, 8192], BF16, "qTm")
    qTr = p.sb([128, 8192], BF16, "qTr")
    kTm = p.sb([128, 8192], BF16, "kTm")
    kTr = p.sb([128, 8192], BF16, "kTr")
    vtok = p.sb([128, M_TPB, 128], BF16, "vtok")
    PT = [p.sb([128, 512], BF16, f"PT{i}") for i in range(3)]
    rl = p.sb([128, 512], F32, "rl")
    of = [p.sb([128, 512], BF16, f"of{i}") for i in range(2)]

    B = [p.ps(name=f"B{i}") for i in range(6)]
    ptrQ = p.ps([128, 1024], BF16, "ptrQ")
    ptrK = p.ps([128, 1024], BF16, "ptrK")

    p.excl = set([("B", i) for i in range(6)] + ["ptrQ", "ptrK"])
    p.dma("sp", cst[:, :], consts[:, :], writes=["cst"])
    p.dma("sp", gv[:, :], gvec[:, :], writes=["gv"])
    p.dma("sp", wqs[:, :], wq[:, :], writes=["wqs"])
    p.dma("sp", wkvs[:, :], wkv[:, :], writes=["wkvs"])
    p.dma("sp", posi[:, :], posT[:, :], writes=["posi"])
    p.add("pool", lambda e: e.tensor_copy(wqb[:, :], wqs[:, :]), reads=["wqs"], writes=["wqb"])
    p.add("pool", lambda e: e.tensor_copy(wkvb[:, :], wkvs[:, :]), reads=["wkvs"], writes=["wkvb"])
    p.add("dve", lambda e: e.tensor_copy(masks[:, :, :].rearrange("p a b -> p (a b)"), cst[:, MC_MASK:MC_MASK + 2048]),
          reads=["cst"], writes=["masks"])
    p.add("dve", lambda e: e.tensor_copy(ident[:, :], cst[:, MC_ID:MC_ID + 128]), reads=["cst"], writes=["ident"])
    p.add("dve", lambda e: e.tensor_copy(posf[:, :], posi[:, :]), reads=["posi"], writes=["posf"])
    p.add("pool", lambda e: e.memset(ones[:, :], 1.0), writes=["ones"])
    p.add("pool", lambda e: e.memset(epst[:, :], eps), writes=["eps"])
    p.add("pool", lambda e: e.memset(hpi[:, :], math.pi / 2.0), writes=["hpi"])
    p.add("pool", lambda e: e.memset(qfr[:, :, :], 0.0), writes=["qfr"])
    p.add("pool", lambda e: e.memset(kfr[:, :, :], 0.0), writes=["kfr"])
    p.add("dve", lambda e: e.tensor_scalar(gv[:, 0:192], gv[:, 0:192], 192.0 ** -0.5, None, ALU.mult),
          reads=["gv"], writes=["gv"])
    GQ = gv[:, 0:192]
    GK = gv[:, 192:384]
    INVF = gv[:, 384:416]

    cnt = {"lat": 0, "pt": 0, "sc": 0, "of": 0}
    stores = []

    for b in range(nbatch):
        i0 = b * M_TPB
        p.add("dve", lambda e, i0=i0: e.tensor_tensor(
            tg1[:, :, :], posf[:, i0:i0 + M_TPB].unsqueeze(2).to_broadcast([128, M_TPB, 32]),
            INVF.unsqueeze(1).to_broadcast([128, M_TPB, 32]), ALU.mult),
            reads=["posf", "gv"], writes=["tg1"])
        p.add("dve", lambda e: e.tensor_scalar(tg2[:, :, :], tg1[:, :, :], 1.0 / TWO_PI, MAGIC, ALU.mult, ALU.add),
              reads=["tg1"], writes=["tg2"])
        p.add("dve", lambda e: e.tensor_scalar(tg2[:, :, :], tg2[:, :, :], -MAGIC, None, ALU.add),
              reads=["tg2"], writes=["tg2"])
        p.add("dve", lambda e: e.scalar_tensor_tensor(tg1[:, :, :], tg2[:, :, :], -C1, tg1[:, :, :], ALU.mult, ALU.add),
              reads=["tg1", "tg2"], writes=["tg1"])
        p.add("dve", lambda e: e.scalar_tensor_tensor(tg1[:, :, :], tg2[:, :, :], -C2, tg1[:, :, :], ALU.mult, ALU.add),
              reads=["tg1", "tg2"], writes=["tg1"])
        p.add("dve", lambda e: e.tensor_scalar(tg1[:, :, :], tg1[:, :, :], -PI_CL, PI_CL, ALU.max, ALU.min),
              reads=["tg1"], writes=["tg1"])
        p.add("act", lambda e: e.activation(sinT[:, :, :], tg1[:, :, :], AF.Sin), reads=["tg1"], writes=["sinT"])
        p.add("dve", lambda e: e.tensor_scalar(tg2[:, :, :], tg1[:, :, :], -1.0, None, ALU.mult),
              reads=["tg1", "tg2"], writes=["tg2"])
        p.add("dve", lambda e: e.tensor_tensor(tg2[:, :, :], tg2[:, :, :], tg1[:, :, :], ALU.max),
              reads=["tg1", "tg2"], writes=["tg2"])
        p.add("act", lambda e: e.activation(cosT[:, :, :], tg2[:, :, :], AF.Sin, bias=hpi[:, 0:1], scale=-1.0),
              reads=["tg2", "hpi"], writes=["cosT"])
        for g in range(M_TPB // G if ng is None else ng):
            tok0 = b * 8192 + g * 512
            lb = latb[cnt["lat"] % 2]
            lk = ("latb", cnt["lat"] % 2)
            cnt["lat"] += 1
            for k in range(5):
                p.dma("sp", lb[:, k, :], latT[k * 128:(k + 1) * 128, tok0:tok0 + 512], writes=[(lk, k)])
            for q in range(G):
                ts_ = slice(q * 128, (q + 1) * 128)
                for k in range(2):
                    p.add("pe", lambda e, q=q, k=k, lb=lb, ts_=ts_: e.matmul(
                        B[q // 2][:, (q % 2) * 192:(q % 2) * 192 + 192], lb[:, k, ts_], wqb[:, k * 192:(k + 1) * 192],
                        start=(k == 0), stop=(k == 1)),
                        reads=[(lk, k), "wqb"], writes=[("B", q // 2)])
                for k in range(3):
                    p.add("pe", lambda e, q=q, k=k, lb=lb, ts_=ts_: e.matmul(
                        B[2 + q][:, 0:320], lb[:, 2 + k, ts_], wkvb[:, k * 320:(k + 1) * 320],
                        start=(k == 0), stop=(k == 2)),
                        reads=[(lk, 2 + k), "wkvb"], writes=[("B", 2 + q)])
            for q in range(G):
                qsrc = B[q // 2][:, (q % 2) * 192:(q % 2) * 192 + 192]
                ksrc = B[2 + q][:, 0:192]
                p.add("act", lambda e, q=q, qsrc=qsrc: e.activation(junk[:, :], qsrc, AF.Square, accum_out=ss[:, q:q + 1]),
                      reads=[("B", q // 2)], writes=["junk", "ss"])
                p.add("act", lambda e, q=q, ksrc=ksrc: e.activation(junk[:, :], ksrc, AF.Square, accum_out=ss[:, 4 + q:5 + q]),
                      reads=[("B", 2 + q)], writes=["junk", "ss"])
            p.add("act", lambda e: e.activation(ss[:, :], ss[:, :], AF.Sqrt, bias=epst[:, 0:1], scale=1.0 / 192.0),
                  reads=["ss", "eps"], writes=["ss"])
            p.add("dve", lambda e: e.reciprocal(ss[:, :], ss[:, :]), reads=["ss"], writes=["ss"])
            for q in range(G):
                qsrc = B[q // 2][:, (q % 2) * 192:(q % 2) * 192 + 192]
                ksrc = B[2 + q][:, 0:192]
                p.add("dve", lambda e, q=q, qsrc=qsrc: e.scalar_tensor_tensor(qn[:, q, :], qsrc, ss[:, q:q + 1], GQ, ALU.mult, ALU.mult),
                      reads=[("B", q // 2), "ss", "gv"], writes=["qn"])
                p.add("dve", lambda e, q=q, ksrc=ksrc: e.scalar_tensor_tensor(kn[:, q, :], ksrc, ss[:, 4 + q:5 + q], GK, ALU.mult, ALU.mult),
                      reads=[("B", 2 + q), "ss", "gv"], writes=["kn"])
                tile = g * G + q
                p.add("act", lambda e, q=q, tile=tile: e.activation(vtok[:, tile, :], B[2 + q][:, 192:320], AF.Copy),
                      reads=[("B", 2 + q)], writes=["vtok"])
            ti0 = g * G
            cs = cosT[:, ti0:ti0 + G, :]
            sn = sinT[:, ti0:ti0 + G, :]
            for (src, skey, dst, dkey, dstr, drkey, eng, tr, tn) in ((qn, "qn", qf, "qf", qfr, "qfr", "dve", trq, "trq"), (kn, "kn", kf, "kf", kfr, "kfr", "pool", trk, "trk")):
                x1 = src[:, :, 128:160]
                x2 = src[:, :, 160:192]
                p.add("act", lambda e, src=src, dst=dst: e.activation(dst[:, :, :], src[:, :, 0:128], AF.Copy),
                      reads=[skey], writes=[dkey])
                p.add(eng, lambda e, x1=x1, tr=tr, cs=cs: e.tensor_tensor(tr[0][:, :, :], x1, cs, ALU.mult), reads=[skey, "cosT"], writes=[(tn, 0)])
                p.add(eng, lambda e, x2=x2, tr=tr, sn=sn: e.tensor_tensor(tr[1][:, :, :], x2, sn, ALU.mult), reads=[skey, "sinT"], writes=[(tn, 1)])
                p.add(eng, lambda e, dstr=dstr, tr=tr: e.tensor_tensor(dstr[:, :, 0:32], tr[0][:, :, :], tr[1][:, :, :], ALU.subtract),
                      reads=[(tn, 0), (tn, 1)], writes=[drkey])
                p.add(eng, lambda e, x2=x2, tr=tr, cs=cs: e.tensor_tensor(tr[2][:, :, :], x2, cs, ALU.mult), reads=[skey, "cosT"], writes=[(tn, 2)])
                p.add(eng, lambda e, x1=x1, tr=tr, sn=sn: e.tensor_tensor(tr[3][:, :, :], x1, sn, ALU.mult), reads=[skey, "sinT"], writes=[(tn, 3)])
                p.add(eng, lambda e, dstr=dstr, tr=tr: e.tensor_tensor(dstr[:, :, 32:64], tr[2][:, :, :], tr[3][:, :, :], ALU.add),
                      reads=[(tn, 2), (tn, 3)], writes=[drkey])
            lt0 = g * 512
            for (srcf, skey, srcr, srkey, ptr_, pkey, dm, dmk, dr, drk) in ((qf, "qf", qfr, "qfr", ptrQ, "ptrQ", qTm, "qTm", qTr, "qTr"),
                                                                              (kf, "kf", kfr, "kfr", ptrK, "ptrK", kTm, "kTm", kTr, "kTr")):
                for q in range(G):
                    p.add("pe", lambda e, q=q, srcf=srcf, ptr_=ptr_: e.transpose(
                        ptr_[:, q * 128:(q + 1) * 128], srcf[:, q, :], ident[:, :]),
                        reads=[skey, "ident"], writes=[pkey])
                for q in range(G):
                    p.add("pe", lambda e, q=q, srcr=srcr, ptr_=ptr_: e.transpose(
                        ptr_[:, 512 + q * 128:512 + (q + 1) * 128], srcr[:, q, :], ident[:, :]),
                        reads=[srkey, "ident"], writes=[pkey])
                p.add("dve", lambda e, ptr_=ptr_, dm=dm, lt0=lt0: e.tensor_copy(dm[:, lt0:lt0 + 512], ptr_[:, 0:512]),
                      reads=[pkey], writes=[dmk])
                p.add("act", lambda e, ptr_=ptr_, dr=dr, lt0=lt0: e.activation(dr[:, lt0:lt0 + 512], ptr_[:, 512:1024], AF.Copy),
                      reads=[pkey], writes=[drk])
        for i in range(nq):
            qs = slice(i * 512, (i + 1) * 512)
            nj = 4 * i + 4
            ob = 2 + (i % 2)
            lbk = 4 + (i % 2)
            for j in range(nj):
                sc = cnt["sc"] % 2
                cnt["sc"] += 1
                r = cnt["pt"] % 3
                cnt["pt"] += 1
                ks = slice(j * 128, (j + 1) * 128)
                p.add("pe", lambda e, sc=sc, ks=ks, qs=qs: e.matmul(B[sc][:, :], kTm[:, ks], qTm[:, qs], start=True, stop=False),
                      reads=["kTm", "qTm"], writes=[("B", sc)])
                p.add("pe", lambda e, sc=sc, ks=ks, qs=qs: e.matmul(B[sc][:, :], kTr[:, ks], qTr[:, qs], start=False, stop=True),
                      reads=["kTr", "qTr"], writes=[("B", sc)])
                p.add("act", lambda e, sc=sc, r=r: e.activation(PT[r][:, :], B[sc][:, :], AF.Exp),
                      reads=[("B", sc)], writes=[("PT", r)])
                if j >= 4 * i:
                    m = j - 4 * i
                    p.add("pool", lambda e, r=r, m=m: e.tensor_tensor(PT[r][:, :], PT[r][:, :], masks[:, m, :], ALU.mult),
                          reads=[("PT", r), "masks"], writes=[("PT", r)])
                p.add("pe", lambda e, ob=ob, j=j, r=r, nj=nj: e.matmul(B[ob][:, :], vtok[:, j, :], PT[r][:, :],
                                                                  start=(j == 0), stop=(j == nj - 1)),
                      reads=["vtok", ("PT", r)], writes=[("B", ob)])
                p.add("pe", lambda e, lbk=lbk, j=j, r=r, nj=nj: e.matmul(B[lbk][:, :], ones[:, :], PT[r][:, :],
                                                                    start=(j == 0), stop=(j == nj - 1)),
                      reads=["ones", ("PT", r)], writes=[("B", lbk)])
            p.add("dve", lambda e, lbk=lbk: e.reciprocal(rl[:, :], B[lbk][:, :]), reads=[("B", lbk)], writes=["rl"])
            oi = cnt["of"] % 2
            cnt["of"] += 1
            p.add("dve", lambda e, ob=ob, oi=oi: e.tensor_tensor(of[oi][:, :], B[ob][:, :], rl[:, :], ALU.mult),
                  reads=[("B", ob), "rl"], writes=[("of", oi)])
            t0 = b * 8192 + i * 512
            stores.append(p.dma("sp", oT[:, t0:t0 + 512], of[oi][:, :], reads=[("of", oi)], force=True))
    p.finish(stores)
    return p.emit()


def mixm_consts():
    c = np.zeros((128, MC_N), np.float32)
    k = np.arange(128)[:, None]
    q = np.arange(512)[None, :]
    for m in range(4):
        c[:, MC_MASK + m * 512:MC_MASK + (m + 1) * 512] = (q >= k + 128 * m).astype(np.float32)
    c[:, MC_ID:MC_ID + 128] = np.eye(128, dtype=np.float32)
    return c


def mixm_inputs(head, w_q_up, w_kv_up, q_norm, k_norm):
    wq = w_q_up[:, head * 192:(head + 1) * 192]
    wq = np.ascontiguousarray(wq.reshape(2, 128, 192).transpose(1, 0, 2).reshape(128, 384))
    wkv_h = w_kv_up[:, head * 256:(head + 1) * 256]
    ext = np.zeros((3, 128, 320), np.float32)
    for k in range(2):
        ext[k, :, 0:128] = wkv_h[k * 128:(k + 1) * 128, 0:128]
        ext[k, :, 192:320] = wkv_h[k * 128:(k + 1) * 128, 128:256]
    ext[2, np.arange(64), 128 + np.arange(64)] = 1.0
    wkv = np.ascontiguousarray(ext.transpose(1, 0, 2).reshape(128, 960))
    gvec = np.zeros((128, MG_N), np.float32)
    gvec[:, 0:192] = q_norm[None, :]
    gvec[:, 192:384] = k_norm[None, :]
    inv_freq = (10000.0 ** (-np.arange(0, 64, 2, dtype=np.float32) / 64)).astype(np.float32)
    gvec[:, 384:416] = inv_freq[None, :]
    return wq, wkv, gvec


import ml_dtypes

_PROGS = {}
N_CORES = 8


def _prog(name):
    if name not in _PROGS:
        if name == "pre_h":
            _PROGS[name] = build_pre("h")
        elif name == "pre_m":
            _PROGS[name] = build_pre("m")
        elif name == "mix_h":
            _PROGS[name] = build_mixh()
        elif name == "mix_m":
            _PROGS[name] = build_mixm()
        elif name == "post":
            _PROGS[name] = build_post()
    return _PROGS[name]


def _run(name, ins):
    res = run_bass_kernel_spmd(_prog(name), ins, core_ids=list(range(N_CORES)))
    return res.results


def kernel(x, positions, norm_mix, norm_ffn, hgrn_w_in, hgrn_lower_bounds, hgrn_out_norm, hgrn_w_out,
           mla_w_in, mla_q_a_norm, mla_w_q_up, mla_kv_a_norm, mla_w_kv_up, mla_q_norm, mla_k_norm,
           mla_w_out, ffn_w_up, ffn_conv_w, ffn_conv_b, ffn_w_down):
    f32 = np.float32
    x = np.asarray(x, f32)
    B, S, D = x.shape
    T = B * S
    TPC = T // N_CORES
    xT = np.ascontiguousarray(x.reshape(T, D).T)
    pos = np.asarray(positions).reshape(-1).astype(np.int32)
    posT = np.ascontiguousarray(pos.reshape(T // 128, 128).T)
    bf16 = ml_dtypes.bfloat16
    for l in range(4):
        j = l // 2
        if l % 2 == 0:
            base = pre_inputs(np.asarray(norm_mix[l], f32))
            ins = [dict(base, xT=np.ascontiguousarray(xT[:, c * TPC:(c + 1) * TPC])) for c in range(N_CORES)]
            res = _run("pre_h", ins)
            hT = np.ascontiguousarray(np.concatenate([r["hT"] for r in res], axis=1))
            cst = mixh_consts()
            ins = []
            for c in range(N_CORES):
                w, vec = mixh_inputs(c, np.asarray(hgrn_w_in[j], f32), np.asarray(hgrn_lower_bounds, f32),
                                     np.asarray(hgrn_out_norm[j], f32), j)
                ins.append({"hT": hT, "w_in": w, "consts": cst, "vec": vec})
            res = _run("mix_h", ins)
            w_out = np.asarray(hgrn_w_out[j], f32)
        else:
            base = pre_inputs(np.asarray(norm_mix[l], f32), np.asarray(mla_q_a_norm[j], f32),
                              np.asarray(mla_kv_a_norm[j], f32), np.asarray(mla_w_in[j], f32))
            ins = [dict(base, xT=np.ascontiguousarray(xT[:, c * TPC:(c + 1) * TPC])) for c in range(N_CORES)]
            res = _run("pre_m", ins)
            latT = np.ascontiguousarray(np.concatenate([r["latT"] for r in res], axis=1))
            cst = mixm_consts()
            ins = []
            for c in range(N_CORES):
                wq, wkv, gvec = mixm_inputs(c, np.asarray(mla_w_q_up[j], f32), np.asarray(mla_w_kv_up[j], f32),
                                            np.asarray(mla_q_norm[j], f32), np.asarray(mla_k_norm[j], f32))
                ins.append({"latT": latT, "posT": posT, "wq": wq, "wkv": wkv, "gvec": gvec, "consts": cst})
            res = _run("mix_m", ins)
            w_out = np.asarray(mla_w_out[j], f32)
        oT = np.concatenate([r["oT"] for r in res], axis=0)
        wo, wu, wd, vec = post_layout_weights(w_out, np.asarray(ffn_w_up[l], f32), np.asarray(ffn_w_down[l], f32),
                                              np.asarray(norm_ffn[l], f32), np.asarray(ffn_conv_w[l], f32),
                                              np.asarray(ffn_conv_b[l], f32))
        ins = []
        for c in range(N_CORES):
            v = vec.copy()
            a = c * TPC
            if a % S == 0:
                xs = np.concatenate([np.zeros((D, P_HALO), f32), xT[:, a:a + TPC]], axis=1)
                os_ = np.concatenate([np.zeros((D, P_HALO), bf16), oT[:, a:a + TPC]], axis=1)
                v[:, 8 + 176] = 0.0
            else:
                xs = xT[:, a - P_HALO:a + TPC]
                os_ = oT[:, a - P_HALO:a + TPC]
                v[:, 8 + 176] = 1.0
            ins.append({"xT": np.ascontiguousarray(xs), "oT": np.ascontiguousarray(os_), "w_out": wo, "w_up": wu,
                        "w_down": wd, "vec": v})
        res = _run("post", ins)
        xT = np.ascontiguousarray(np.concatenate([r["yT"] for r in res], axis=1))
    return np.ascontiguousarray(xT.T).reshape(B, S, D).astype(f32)
```

```python
import math
from concourse.bass_utils import run_bass_kernel_spmd


import numpy as np
import os
from contextlib import ExitStack
import concourse.bass as bass
import concourse.mybir as mybir

F32 = mybir.dt.float32
BF16 = mybir.dt.bfloat16
I32 = mybir.dt.int32
AF = mybir.ActivationFunctionType
ALU = mybir.AluOpType
AX = mybir.AxisListType

ENGS = ("pe", "act", "dve", "pool", "sp")
BLK = {"pe": "tensor", "act": "scalar", "dve": "vector", "pool": "gpsimd", "sp": "sync"}
GEN = 16000
KDMA = 8


class Op:
    __slots__ = ("eng", "fn", "dma", "seq", "deps", "signal", "ordinal", "didx", "dsem", "dval")

    def __init__(self, eng, fn, dma):
        self.eng = eng
        self.fn = fn
        self.dma = dma
        self.deps = []
        self.signal = False
        self.ordinal = 0
        self.didx = -1


class Prog:
    def __init__(self, nc, same_sync=True):
        self.nc = nc
        self.same_sync = same_sync
        self.ops = {e: [] for e in ENGS}
        self.last_w = {}
        self.rd_eng = {}
        self.rd_dma = {}
        self.ndma = {e: 0 for e in ENGS}
        self.stack = ExitStack()
        self._nm = 0

    def sb(self, shape, dt, name=None):
        self._nm += 1
        return self.stack.enter_context(self.nc.sbuf_tensor(name or f"sb{self._nm}", list(shape), dt))

    def ps(self, shape=(128, 512), dt=F32, name=None):
        self._nm += 1
        return self.stack.enter_context(self.nc.psum_tensor(name or f"ps{self._nm}", list(shape), dt))

    def add(self, eng, fn, reads=(), writes=(), dma=False, force=False):
        self.nadd = getattr(self, "nadd", 0) + 1
        lim = int(os.environ.get("FW_LIMIT", "0"))
        if lim and self.nadd > lim and not force:
            return None
        excl = getattr(self, "excl", None)
        if excl:
            mv = [b for b in reads if b in excl]
            if mv:
                reads = [b for b in reads if b not in excl]
                writes = list(writes) + mv
        op = Op(eng, fn, dma)
        op.seq = len(self.ops[eng])
        if dma:
            op.didx = self.ndma[eng]
            self.ndma[eng] += 1
        deps = {}
        for b in reads:
            lw = self.last_w.get(b)
            if lw is not None:
                deps[id(lw)] = lw
        for b in writes:
            lw = self.last_w.get(b)
            if lw is not None:
                deps[id(lw)] = lw
            for r in self.rd_eng.get(b, {}).values():
                deps[id(r)] = r
            for r in self.rd_dma.get(b, ()):
                deps[id(r)] = r
        for d in deps.values():
            if d is op:
                continue
            if d.dma:
                op.deps.append(d)
            elif d.eng == eng:
                if eng == "pe" and not dma:
                    continue
                if self.same_sync or dma:
                    d.signal = True
                    op.deps.append(d)
            else:
                d.signal = True
                op.deps.append(d)
        for b in reads:
            if dma:
                self.rd_dma.setdefault(b, []).append(op)
            else:
                self.rd_eng.setdefault(b, {})[eng] = op
        for b in writes:
            self.last_w[b] = op
            self.rd_eng[b] = {}
            self.rd_dma[b] = []
        self.ops[eng].append(op)
        return op

    def dma(self, eng, out, in_, reads=(), writes=(), force=False):
        return self.add(eng, lambda e: e.dma_start(out=out, in_=in_), reads, writes, dma=True, force=force)

    def finish(self, ops, eng="sp"):
        op = Op(eng, None, False)
        op.seq = len(self.ops[eng])
        for d in ops:
            if d is None:
                continue
            if not d.dma:
                d.signal = True
            op.deps.append(d)
        self.ops[eng].append(op)

    def emit(self):
        nc = self.nc
        st = self.stack
        csem = {}
        for e in ENGS:
            n = 0
            for op in self.ops[e]:
                if (not op.dma) and op.signal:
                    n += 1
                    op.ordinal = n
            ngen = (n + GEN - 1) // GEN
            csem[e] = [st.enter_context(nc.semaphore(f"c_{e}_{g}")) for g in range(ngen)]
        dsem = {}
        for e in ENGS:
            k = min(KDMA, self.ndma[e])
            dsem[e] = [st.enter_context(nc.semaphore(f"d_{e}_{i}")) for i in range(k)]
            dl = [op for op in self.ops[e] if op.dma]
            for op in dl:
                op.dsem = dsem[e][op.didx % KDMA]
                op.dval = 16 * (op.didx // KDMA + 1)
        self._dl = {e: [op for op in self.ops[e] if op.dma] for e in ENGS}

        def emit_engine(e, E):
            waited = {f: -1 for f in ENGS}
            wdma = set()
            for op in self.ops[e]:
                for d in op.deps:
                    if d.dma:
                        if id(d) not in wdma:
                            E.wait_ge(d.dsem, d.dval)
                            wdma.add(id(d))
                    else:
                        if d.seq > waited[d.eng]:
                            g = (d.ordinal - 1) // GEN
                            E.wait_ge(csem[d.eng][g], (d.ordinal - 1) % GEN + 1)
                            waited[d.eng] = d.seq
                if op.fn is None:
                    continue
                if op.dma:
                    if op.didx >= KDMA:
                        prev = self._dl[e][op.didx - KDMA]
                        if id(prev) not in wdma:
                            E.wait_ge(prev.dsem, prev.dval)
                            wdma.add(id(prev))
                    ins = op.fn(E)
                    ins.then_inc(op.dsem, 16)
                else:
                    ins = op.fn(E)
                    if op.signal:
                        g = (op.ordinal - 1) // GEN
                        ins.then_inc(csem[e][g], 1)

        with nc.Block() as block:
            for e in ENGS:
                if not self.ops[e]:
                    continue
                getattr(block, BLK[e])(lambda E, e=e: emit_engine(e, E))
        self.stack.close()
        return nc


P_TW = 1024
P_HALO = 2
P_TC = P_TW + P_HALO
P_NT = 2
P_COLS = P_NT * P_TW + P_HALO
NTILES3 = [(0, 342), (342, 684), (684, 1026)]
NV = 8 + 44 * 3 + 44 + 1


def build_post(eps=1e-6):
    nc = bass.Bass("TRN2", target_bir_lowering=False)
    xT = nc.dram_tensor("xT", [1024, P_COLS], F32, kind="ExternalInput").ap()
    oT = nc.dram_tensor("oT", [1024, P_COLS], BF16, kind="ExternalInput").ap()
    w_out = nc.dram_tensor("w_out", [8, 128, 1024], F32, kind="ExternalInput").ap()
    w_up = nc.dram_tensor("w_up", [22, 128, 2048], F32, kind="ExternalInput").ap()
    w_down = nc.dram_tensor("w_down", [8, 128, 2816], F32, kind="ExternalInput").ap()
    vec = nc.dram_tensor("vec", [128, NV], F32, kind="ExternalInput").ap()
    yT = nc.dram_tensor("yT", [1024, P_NT * P_TW], F32, kind="ExternalOutput").ap()

    p = Prog(nc)
    xt = p.sb([128, 8, P_TC], F32, "xt")
    ot = p.sb([128, 8, P_TC], BF16, "ot")
    ht = p.sb([128, 8, P_TC], BF16, "ht")
    at = p.sb([128, 22, P_TW], BF16, "at")
    sq = [p.sb([128, 512], BF16, f"sq{i}") for i in range(2)]
    rs = p.sb([128, P_TC], F32, "rs")
    ub = [[p.sb([128, P_TC], F32, f"u{i}{h}") for h in range(2)] for i in range(2)]
    yb = [p.sb([128, P_TW], F32, f"y{h}") for h in range(2)]
    NWB = 4
    wbf = [p.sb([128, 2816], BF16, f"wbf{i}") for i in range(NWB)]
    vt = p.sb([128, NV], F32, "vt")
    ones = p.sb([128, 128], BF16, "ones")
    epst = p.sb([128, 1], F32, "epst")
    acc = [p.ps(name=f"acc{i}") for i in range(2)]
    ups = [p.ps(name=f"ups{i}") for i in range(6)]

    xT_v = xT.rearrange("(k p) c -> p k c", p=128)
    oT_v = oT.rearrange("(k p) c -> p k c", p=128)
    yT_v = yT.rearrange("(k p) c -> p k c", p=128)

    p.dma("sp", vt[:, :], vec[:, :], writes=["vt"])
    p.add("pool", lambda e: e.memset(ones[:, :], 1.0), writes=["ones"])
    p.add("pool", lambda e: e.memset(epst[:, :], eps), writes=["eps"])

    G0 = 0
    CW0 = 8
    CB0 = 8 + 132
    FL0 = 8 + 132 + 44

    cnt = {"w": 0, "wl": 0, "acc": 0, "ups": 0, "sq": 0, "u": 0}
    stores = []

    wlist = []
    for _t in range(P_NT):
        wlist += [(w_out[m], 1024) for m in range(8)]
        wlist += [(w_up[j], 2048) for j in range(22)]
        wlist += [(w_down[m], 2816) for m in range(8)]
    PF = NWB - 1

    def load_w(src_ap, ncols):
        i = cnt["w"]
        cnt["w"] += 1
        while cnt["wl"] <= min(i + PF, len(wlist) - 1):
            li = cnt["wl"]
            cnt["wl"] += 1
            ap, nco = wlist[li]
            p.dma("pool", wbf[li % NWB][:, 0:nco], ap, writes=[("wbf", li % NWB)])
        return i % NWB

    for t in range(P_NT):
        c0 = t * P_TW
        for k in range(8):
            p.dma("sp", xt[:, k, :], xT_v[:, k, c0:c0 + P_TC], writes=[("xt", k)])
        for k in range(8):
            p.dma("sp", ot[:, k, :], oT_v[:, k, c0:c0 + P_TC], writes=[("ot", k)])
        for m in range(8):
            b = load_w(w_out[m], 1024)
            for (a0, a1) in NTILES3:
                n = a1 - a0
                pb = cnt["acc"] % 2
                cnt["acc"] += 1
                for k in range(8):
                    p.add("pe", lambda e, pb=pb, n=n, b=b, k=k, a0=a0, a1=a1: e.matmul(
                        acc[pb][:, 0:n], wbf[b][:, k * 128:(k + 1) * 128], ot[:, k, a0:a1],
                        start=(k == 0), stop=(k == 7)),
                        reads=[("wbf", b), ("ot", k)], writes=[("acc", pb)])
                p.add("dve", lambda e, pb=pb, n=n, m=m, a0=a0, a1=a1: e.tensor_tensor(
                    xt[:, m, a0:a1], xt[:, m, a0:a1], acc[pb][:, 0:n], ALU.add),
                    reads=[("acc", pb), ("xt", m)], writes=[("xt", m)])
        for ti, (a0, a1) in enumerate(NTILES3):
            n = a1 - a0
            pb = cnt["acc"] % 2
            cnt["acc"] += 1
            for k in range(8):
                s = cnt["sq"] % 2
                cnt["sq"] += 1
                p.add("act", lambda e, s=s, n=n, k=k, a0=a0, a1=a1: e.activation(
                    sq[s][:, 0:n], xt[:, k, a0:a1], AF.Square),
                    reads=[("xt", k)], writes=[("sq", s)])
                p.add("pe", lambda e, pb=pb, n=n, s=s, k=k: e.matmul(
                    acc[pb][:, 0:n], ones[:, :], sq[s][:, 0:n], start=(k == 0), stop=(k == 7)),
                    reads=[("sq", s), "ones"], writes=[("acc", pb)])
            p.add("act", lambda e, pb=pb, n=n, a0=a0, a1=a1: e.activation(
                rs[:, a0:a1], acc[pb][:, 0:n], AF.Sqrt, bias=epst[:, 0:1], scale=1.0 / 1024.0),
                reads=[("acc", pb), "eps"], writes=[("rs", ti)])
            p.add("dve", lambda e, a0=a0, a1=a1: e.reciprocal(rs[:, a0:a1], rs[:, a0:a1]),
                  reads=[("rs", ti)], writes=[("rs", ti)])
        for k in range(8):
            p.add("dve", lambda e, k=k: e.scalar_tensor_tensor(
                ht[:, k, :], xt[:, k, :], vt[:, G0 + k:G0 + k + 1], rs[:, :], ALU.mult, ALU.mult),
                reads=[("xt", k), "vt", ("rs", 0), ("rs", 1), ("rs", 2)], writes=[("ht", k)])
            if t == 0:
                p.add("dve", lambda e, k=k: e.tensor_scalar(
                    ht[:, k, 0:P_HALO], ht[:, k, 0:P_HALO], vt[:, FL0:FL0 + 1], None, ALU.mult),
                    reads=[("ht", k), "vt"], writes=[("ht", k)])
        for j in range(22):
            b = load_w(w_up[j], 2048)
            ui = cnt["u"] % 2
            cnt["u"] += 1
            for half in range(2):
                ch = j + 22 * half
                for (a0, a1) in NTILES3:
                    n = a1 - a0
                    r = cnt["ups"] % 6
                    cnt["ups"] += 1
                    for k in range(8):
                        p.add("pe", lambda e, r=r, n=n, b=b, k=k, half=half, a0=a0, a1=a1: e.matmul(
                            ups[r][:, 0:n], wbf[b][:, k * 256 + half * 128:k * 256 + half * 128 + 128],
                            ht[:, k, a0:a1], start=(k == 0), stop=(k == 7)),
                            reads=[("wbf", b), ("ht", k)], writes=[("ups", r)])
                    p.add("act", lambda e, r=r, n=n, ui=ui, half=half, a0=a0, a1=a1: e.activation(
                        ub[ui][half][:, a0:a1], ups[r][:, 0:n], AF.Copy),
                        reads=[("ups", r)], writes=[("u", ui, half)])
                u = ub[ui][half]
                y = yb[half]
                w0 = vt[:, CW0 + 0 * 44 + ch:CW0 + 0 * 44 + ch + 1]
                w1 = vt[:, CW0 + 1 * 44 + ch:CW0 + 1 * 44 + ch + 1]
                w2 = vt[:, CW0 + 2 * 44 + ch:CW0 + 2 * 44 + ch + 1]
                bb = vt[:, CB0 + ch:CB0 + ch + 1]
                p.add("dve", lambda e, u=u, y=y, w2=w2, bb=bb: e.tensor_scalar(
                    y[:, :], u[:, 2:P_TC], w2, bb, ALU.mult, ALU.add),
                    reads=[("u", ui, half), "vt"], writes=[("y", half)])
                p.add("dve", lambda e, u=u, y=y, w1=w1: e.scalar_tensor_tensor(
                    y[:, :], u[:, 1:P_TC - 1], w1, y[:, :], ALU.mult, ALU.add),
                    reads=[("u", ui, half), "vt", ("y", half)], writes=[("y", half)])
                p.add("dve", lambda e, u=u, y=y, w0=w0: e.scalar_tensor_tensor(
                    y[:, :], u[:, 0:P_TC - 2], w0, y[:, :], ALU.mult, ALU.add),
                    reads=[("u", ui, half), "vt", ("y", half)], writes=[("y", half)])
            p.add("act", lambda e: e.activation(yb[0][:, :], yb[0][:, :], AF.Silu),
                  reads=[("y", 0)], writes=[("y", 0)])
            p.add("pool", lambda e, j=j: e.tensor_tensor(at[:, j, :], yb[0][:, :], yb[1][:, :], ALU.mult),
                  reads=[("y", 0), ("y", 1)], writes=[("at", j)])
        for m in range(8):
            b = load_w(w_down[m], 2816)
            for h2 in range(2):
                pb = cnt["acc"] % 2
                cnt["acc"] += 1
                for j in range(22):
                    p.add("pe", lambda e, pb=pb, b=b, j=j, h2=h2: e.matmul(
                        acc[pb][:, :], wbf[b][:, j * 128:(j + 1) * 128], at[:, j, h2 * 512:(h2 + 1) * 512],
                        start=(j == 0), stop=(j == 21)),
                        reads=[("wbf", b), ("at", j)], writes=[("acc", pb)])
                x0 = P_HALO + h2 * 512
                p.add("dve", lambda e, pb=pb, m=m, x0=x0: e.tensor_tensor(
                    xt[:, m, x0:x0 + 512], xt[:, m, x0:x0 + 512], acc[pb][:, :], ALU.add),
                    reads=[("acc", pb), ("xt", m)], writes=[("xt", m)])
            stores.append(p.dma("sp", yT_v[:, m, c0:c0 + P_TW], xt[:, m, P_HALO:P_TC], reads=[("xt", m)]))
    p.finish(stores)
    return p.emit()


def post_layout_weights(w_out, w_up, w_down, g, conv_w, conv_b):
    wo = np.ascontiguousarray(w_out.reshape(8, 128, 8, 128).transpose(2, 1, 0, 3).reshape(8, 128, 1024))
    wu = w_up.reshape(8, 128, 2, 22, 128)
    wu = np.ascontiguousarray(wu.transpose(3, 1, 0, 2, 4).reshape(22, 128, 2048))
    wd = w_down.reshape(22, 128, 8, 128)
    wd = np.ascontiguousarray(wd.transpose(2, 1, 0, 3).reshape(8, 128, 2816))
    vec = np.zeros((128, NV), np.float32)
    vec[:, 0:8] = g.reshape(8, 128).T
    for tap in range(3):
        vec[:, 8 + tap * 44:8 + (tap + 1) * 44] = conv_w[tap].reshape(44, 128).T
    vec[:, 8 + 132:8 + 176] = conv_b.reshape(44, 128).T
    return wo, wu, wd, vec


PR_TW = 512
PR_NT = 2048 // PR_TW


def build_pre(mode, eps=1e-6):
    nc = bass.Bass("TRN2", target_bir_lowering=False)
    xT = nc.dram_tensor("xT", [1024, 2048], F32, kind="ExternalInput").ap()
    vec = nc.dram_tensor("vec", [128, 12], F32, kind="ExternalInput").ap()
    if mode == "m":
        w_in = nc.dram_tensor("w_in", [128, 8 * 640], F32, kind="ExternalInput").ap()
        outT = nc.dram_tensor("latT", [640, 2048], BF16, kind="ExternalOutput").ap()
    else:
        outT = nc.dram_tensor("hT", [1024, 2048], BF16, kind="ExternalOutput").ap()
    p = Prog(nc)
    xt = [p.sb([128, 8, PR_TW], F32, f"xt{i}") for i in range(2)]
    ht = [p.sb([128, 8, PR_TW], BF16, f"ht{i}") for i in range(2)]
    sq = [p.sb([128, PR_TW], BF16, f"sq{i}") for i in range(2)]
    rs = p.sb([128, PR_TW], F32, "rs")
    vt = p.sb([128, 12], F32, "vt")
    ones = p.sb([128, 128], BF16, "ones")
    epst = p.sb([128, 1], F32, "epst")
    acc = [p.ps(name=f"acc{i}") for i in range(2)]
    if mode == "m":
        wst = p.sb([128, 8 * 640], F32, "wst")
        wbf = p.sb([128, 8 * 640], BF16, "wbf")
        lat = p.sb([128, 5, PR_TW], F32, "lat")
        lo = [p.sb([128, 5, PR_TW], BF16, f"lo{i}") for i in range(2)]
        rs2 = p.sb([128, PR_TW], F32, "rs2")
        lps = [p.ps(name=f"lps{i}") for i in range(2)]
        p.dma("sp", wst[:, :], w_in[:, :], writes=["wst"])
        p.add("pool", lambda e: e.tensor_copy(wbf[:, :], wst[:, :]), reads=["wst"], writes=["wbf"])
    xT_v = xT.rearrange("(k p) c -> p k c", p=128)
    oT_v = outT.rearrange("(k p) c -> p k c", p=128)
    p.dma("sp", vt[:, :], vec[:, :], writes=["vt"])
    p.add("pool", lambda e: e.memset(ones[:, :], 1.0), writes=["ones"])
    p.add("pool", lambda e: e.memset(epst[:, :], eps), writes=["eps"])
    cnt = {"acc": 0, "sq": 0, "lps": 0}
    stores = []

    def rms_stat(srcs, skeys, dst, dkey, n_feat):
        pb = cnt["acc"] % 2
        cnt["acc"] += 1
        for i, (src, sk) in enumerate(zip(srcs, skeys)):
            s = cnt["sq"] % 2
            cnt["sq"] += 1
            p.add("act", lambda e, s=s, src=src: e.activation(sq[s][:, :], src, AF.Square), reads=[sk], writes=[("sq", s)])
            p.add("pe", lambda e, pb=pb, s=s, i=i: e.matmul(acc[pb][:, :], ones[:, :], sq[s][:, :], start=(i == 0), stop=(i == len(srcs) - 1)),
                  reads=[("sq", s), "ones"], writes=[("acc", pb)])
        p.add("act", lambda e, pb=pb: e.activation(dst[:, :], acc[pb][:, :], AF.Sqrt, bias=epst[:, 0:1], scale=1.0 / n_feat),
              reads=[("acc", pb), "eps"], writes=[dkey])
        p.add("dve", lambda e: e.reciprocal(dst[:, :], dst[:, :]), reads=[dkey], writes=[dkey])

    for t in range(PR_NT):
        c0 = t * PR_TW
        xb = xt[t % 2]
        hb = ht[t % 2]
        xk = ("xt", t % 2)
        hk = ("ht", t % 2)
        for k in range(8):
            p.dma("sp", xb[:, k, :], xT_v[:, k, c0:c0 + PR_TW], writes=[(xk, k)])
        rms_stat([xb[:, k, :] for k in range(8)], [(xk, k) for k in range(8)], rs, "rs", 1024.0)
        for k in range(8):
            p.add("dve", lambda e, k=k, xb=xb, hb=hb: e.scalar_tensor_tensor(
                hb[:, k, :], xb[:, k, :], vt[:, k:k + 1], rs[:, :], ALU.mult, ALU.mult),
                reads=[(xk, k), "vt", "rs"], writes=[(hk, k)])
            if mode == "h":
                stores.append(p.dma("sp", oT_v[:, k, c0:c0 + PR_TW], hb[:, k, :], reads=[(hk, k)]))
        if mode == "m":
            lob = lo[t % 2]
            lk = ("lo", t % 2)
            for oc in range(5):
                pb = cnt["lps"] % 2
                cnt["lps"] += 1
                for k in range(8):
                    p.add("pe", lambda e, pb=pb, k=k, oc=oc, hb=hb: e.matmul(
                        lps[pb][:, :], wbf[:, k * 640 + oc * 128:k * 640 + oc * 128 + 128], hb[:, k, :],
                        start=(k == 0), stop=(k == 7)),
                        reads=["wbf", (hk, k)], writes=[("lps", pb)])
                if oc < 4:
                    p.add("act", lambda e, pb=pb, oc=oc: e.activation(lat[:, oc, :], lps[pb][:, :], AF.Copy),
                          reads=[("lps", pb)], writes=[("lat", oc)])
                else:
                    p.add("act", lambda e, pb=pb, lob=lob: e.activation(lob[:, 4, :], lps[pb][:, :], AF.Copy),
                          reads=[("lps", pb)], writes=[(lk, 4)])
            for grp in range(2):
                rms_stat([lat[:, 2 * grp + i, :] for i in range(2)], [("lat", 2 * grp + i) for i in range(2)], rs2, "rs2", 256.0)
                for i in range(2):
                    oc = 2 * grp + i
                    p.add("dve", lambda e, oc=oc, lob=lob, grp=grp, i=i: e.scalar_tensor_tensor(
                        lob[:, oc, :], lat[:, oc, :], vt[:, 8 + 2 * grp + i:9 + 2 * grp + i], rs2[:, :], ALU.mult, ALU.mult),
                        reads=[("lat", oc), "vt", "rs2"], writes=[(lk, oc)])
            for oc in range(5):
                stores.append(p.dma("sp", oT_v[:, oc, c0:c0 + PR_TW], lob[:, oc, :], reads=[(lk, oc)]))
    p.finish(stores)
    return p.emit()


def pre_inputs(g_mix, q_a=None, kv_a=None, w_in=None):
    vec = np.zeros((128, 12), np.float32)
    vec[:, 0:8] = g_mix.reshape(8, 128).T
    out = {"vec": vec}
    if w_in is not None:
        vec[:, 8:10] = q_a.reshape(2, 128).T
        vec[:, 10:12] = kv_a.reshape(2, 128).T
        w = np.zeros((1024, 640), np.float32)
        w[:, :576] = w_in
        out["w_in"] = np.ascontiguousarray(w.reshape(8, 128, 640).transpose(1, 0, 2).reshape(128, 8 * 640))
    return out


H_TB = 1024
H_NB = 16384 // H_TB
H_SEQ_BLOCKS = 8192 // H_TB
HC_MASK = 0
HC_IDENT = 128
HC_RMASK = 256
HC_CMA = 256 + H_TB
HC_N = 256 + 2 * H_TB
HV_N = 5


def build_mixh(eps=1e-6, nblocks=H_NB):
    nc = bass.Bass("TRN2", target_bir_lowering=False)
    hT = nc.dram_tensor("hT", [1024, 16384], BF16, kind="ExternalInput").ap()
    w_in = nc.dram_tensor("w_in", [128, 8 * 512], F32, kind="ExternalInput").ap()
    consts = nc.dram_tensor("consts", [128, HC_N], F32, kind="ExternalInput").ap()
    vec = nc.dram_tensor("vec", [128, HV_N], F32, kind="ExternalInput").ap()
    oT = nc.dram_tensor("oT", [128, 16384], BF16, kind="ExternalOutput").ap()

    p = Prog(nc)
    TB = H_TB
    NTL = TB // 128
    nch = TB // 64
    cst = p.sb([128, HC_N], F32, "cst")
    vt = p.sb([128, HV_N], F32, "vt")
    lbt = p.sb([128, 4], F32, "lbt")
    wbf = p.sb([128, 4096], BF16, "wbf")
    ident = p.sb([128, 128], BF16, "ident")
    bmask = p.sb([128, 128], F32, "bmask")
    ones = p.sb([128, 128], BF16, "ones")
    epst = p.sb([128, 1], F32, "epst")
    ht = [p.sb([128, 8, TB], BF16, f"ht{i}") for i in range(2)]
    qs = p.sb([128, TB], F32, "qs")
    fg = p.sb([128, TB], F32, "fg")
    lf = p.sb([128, TB], F32, "lf")
    kk = p.sb([128, TB], F32, "kk")
    G = p.sb([128, TB], F32, "G")
    Gc = p.sb([128, TB], F32, "Gc")
    E0 = p.sb([128, TB], F32, "E0")
    E1 = p.sb([128, TB], F32, "E1")
    vT = p.sb([128, TB], BF16, "vT")
    qt = p.sb([128, TB], BF16, "qt")
    kt = p.sb([128, TB], BF16, "kt")
    kh = p.sb([128, TB], BF16, "kh")
    gs2 = [p.sb([128, TB], F32, f"gs{i}") for i in range(2)]
    dec2 = [p.sb([128, nch], F32, f"dec{i}") for i in range(2)]
    vtok2 = [p.sb([128, NTL, 128], BF16, f"vtok{i}") for i in range(2)]
    khA2 = [p.sb([128, NTL, 128], BF16, f"khA{i}") for i in range(2)]
    khB2 = [p.sb([128, NTL, 128], BF16, f"khB{i}") for i in range(2)]
    qhA2 = [p.sb([128, TB], BF16, f"qhA{i}") for i in range(2)]
    qhB2 = [p.sb([128, TB], BF16, f"qhB{i}") for i in range(2)]
    scm2 = [p.sb([128, NTL, 128], BF16, f"scm{i}") for i in range(2)]
    Sf = p.sb([128, 128], F32, "Sf")
    Sb = [p.sb([128, 128], BF16, f"Sb{i}") for i in range(2)]
    osb = p.sb([128, TB], F32, "osb")
    osq = [p.sb([128, 512], BF16, f"osq{i}") for i in range(2)]
    rs = p.sb([128, TB], F32, "rs")
    ofin = [p.sb([128, TB], BF16, f"ofin{i}") for i in range(2)]

    pin = [p.ps(name=f"pin{i}") for i in range(2)]
    ptr = p.ps([128, 1024], BF16, "ptr")
    psc = p.ps(name="psc")
    pst = [p.ps(name=f"pst{i}") for i in range(2)]
    po = [p.ps(name=f"po{i}") for i in range(2)]
    p.excl = set([("pin", 0), ("pin", 1), "ptr", "psc", ("pst", 0), ("pst", 1), ("po", 0), ("po", 1)])

    hT_v = hT.rearrange("(k p) t -> p k t", p=128)

    p.dma("sp", cst[:, :], consts[:, :], writes=["cst"])
    p.dma("sp", vt[:, :], vec[:, :], writes=["vt"])
    p.dma("pool", wbf[:, :], w_in[:, :], writes=["wbf"])
    p.add("dve", lambda e: e.tensor_copy(ident[:, :], cst[:, HC_IDENT:HC_IDENT + 128]), reads=["cst"], writes=["ident"])
    p.add("dve", lambda e: e.tensor_copy(bmask[:, :], cst[:, HC_MASK:HC_MASK + 128]), reads=["cst"], writes=["bmask"])
    p.add("pool", lambda e: e.memset(ones[:, :], 1.0), writes=["ones"])
    p.add("pool", lambda e: e.memset(epst[:, :], eps), writes=["eps"])
    for i in range(2):
        p.add("pool", lambda e, i=i: e.memset(khA2[i][:, :, :], 0.0), writes=[("khA", i)])
        p.add("pool", lambda e, i=i: e.memset(khB2[i][:, :, :], 0.0), writes=[("khB", i)])
    p.add("act", lambda e: e.activation(lbt[:, 0:2], vt[:, 0:2], AF.Exp), reads=["vt"], writes=["lbt"])
    p.add("dve", lambda e: e.tensor_tensor(lbt[:, 2:3], lbt[:, 0:1], lbt[:, 1:2], ALU.add), reads=["lbt"], writes=["lbt"])
    p.add("dve", lambda e: e.reciprocal(lbt[:, 2:3], lbt[:, 2:3]), reads=["lbt"], writes=["lbt"])
    p.add("dve", lambda e: e.tensor_tensor(lbt[:, 0:2], lbt[:, 0:2], vt[:, 2:4], ALU.mult), reads=["lbt", "vt"], writes=["lbt"])
    p.add("dve", lambda e: e.tensor_tensor(lbt[:, 3:4], lbt[:, 0:1], lbt[:, 1:2], ALU.add), reads=["lbt"], writes=["lbt"])
    p.add("dve", lambda e: e.tensor_tensor(lbt[:, 0:1], lbt[:, 3:4], lbt[:, 2:3], ALU.mult), reads=["lbt"], writes=["lbt"])
    p.add("dve", lambda e: e.tensor_scalar(lbt[:, 1:2], lbt[:, 0:1], -1.0, 1.0, ALU.mult, ALU.add), reads=["lbt"], writes=["lbt"])
    LB = lbt[:, 0:1]
    OML = lbt[:, 1:2]
    ON = vt[:, 4:5]
    rmask = cst[:, HC_RMASK:HC_RMASK + TB]
    cmA = cst[:, HC_CMA:HC_CMA + TB]
    G3 = G[:, :].rearrange("p (n c) -> p n c", c=64)
    Gc3 = Gc[:, :].rearrange("p (n c) -> p n c", c=64)
    E03 = E0[:, :].rearrange("p (n c) -> p n c", c=64)

    cnt = {"pin": 0, "st": 0, "S": 0, "osq": 0}
    stores = []

    def prep(blk):
        pp = blk % 2
        hb = ht[pp]
        hk = ("ht", pp)
        t0 = blk * TB
        gs, dec, vtok, khA, khB, qhA, qhB, scm = gs2[pp], dec2[pp], vtok2[pp], khA2[pp], khB2[pp], qhA2[pp], qhB2[pp], scm2[pp]
        for k in range(8):
            p.dma("sp", hb[:, k, :], hT_v[:, k, t0:t0 + TB], writes=[(hk, k)])
        for grp in range(4):
            for nt in range(TB // 512):
                pb = cnt["pin"] % 2
                cnt["pin"] += 1
                for k in range(8):
                    p.add("pe", lambda e, pb=pb, k=k, grp=grp, nt=nt, hb=hb: e.matmul(
                        pin[pb][:, :], wbf[:, k * 512 + grp * 128:k * 512 + grp * 128 + 128],
                        hb[:, k, nt * 512:(nt + 1) * 512], start=(k == 0), stop=(k == 7)),
                        reads=["wbf", (hk, k)], writes=[("pin", pb)])
                sl = slice(nt * 512, (nt + 1) * 512)
                if grp == 0:
                    p.add("act", lambda e, pb=pb, sl=sl: e.activation(qs[:, sl], pin[pb][:, :], AF.Silu),
                          reads=[("pin", pb)], writes=["qs"])
                elif grp == 1:
                    p.add("act", lambda e, pb=pb, sl=sl: e.activation(fg[:, sl], pin[pb][:, :], AF.Sigmoid),
                          reads=[("pin", pb)], writes=["fg"])
                elif grp == 2:
                    p.add("act", lambda e, pb=pb, sl=sl: e.activation(vT[:, sl], pin[pb][:, :], AF.Copy),
                          reads=[("pin", pb)], writes=["vT"])
                else:
                    p.add("act", lambda e, pb=pb, sl=sl, gs=gs: e.activation(gs[:, sl], pin[pb][:, :], AF.Silu),
                          reads=[("pin", pb)], writes=[("gs", pp)])
                yield
        p.add("dve", lambda e: e.tensor_scalar(fg[:, :], fg[:, :], OML, LB, ALU.mult, ALU.add),
              reads=["fg", "lbt"], writes=["fg"])
        p.add("act", lambda e: e.activation(lf[:, :], fg[:, :], AF.Ln), reads=["fg"], writes=["lf"])
        p.add("pool", lambda e: e.tensor_scalar(kk[:, :], fg[:, :], -1.0, 1.0, ALU.mult, ALU.add),
              reads=["fg"], writes=["kk"])
        p.add("dve", lambda e: e.tensor_tensor_scan(G[:, :], rmask, lf[:, :], 0.0, ALU.mult, ALU.add),
              reads=["lf", "cst"], writes=["G"])
        yield
        p.add("dve", lambda e: e.tensor_tensor(Gc3, G3, G3[:, :, 31:32].to_broadcast([128, nch, 64]), ALU.subtract),
              reads=["G"], writes=["Gc"])
        p.add("act", lambda e: e.activation(E0[:, :], Gc[:, :], AF.Exp), reads=["Gc"], writes=["E0"])
        p.add("dve", lambda e: e.tensor_tensor(qt[:, :], qs[:, :], E0[:, :], ALU.mult), reads=["qs", "E0"], writes=["qt"])
        p.add("act", lambda e: e.activation(E1[:, :], Gc[:, :], AF.Exp, scale=-1.0), reads=["Gc"], writes=["E1"])
        p.add("pool", lambda e: e.tensor_tensor(kt[:, :], kk[:, :], E1[:, :], ALU.mult), reads=["kk", "E1"], writes=["kt"])
        yield
        p.add("dve", lambda e: e.tensor_tensor(Gc3, G3, G3[:, :, 63:64].to_broadcast([128, nch, 64]), ALU.subtract),
              reads=["G"], writes=["Gc"])
        p.add("act", lambda e: e.activation(E1[:, :], Gc[:, :], AF.Exp, scale=-1.0), reads=["Gc"], writes=["E1"])
        p.add("pool", lambda e: e.tensor_tensor(kh[:, :], kk[:, :], E1[:, :], ALU.mult), reads=["kk", "E1"], writes=["kh"])
        p.add("act", lambda e: e.activation(E0[:, :], G[:, :], AF.Exp), reads=["G"], writes=["E0"])
        p.add("act", lambda e, dec=dec: e.activation(dec[:, :], E03[:, :, 63], AF.Copy), reads=["E0"], writes=[("dec", pp)])
        yield
        p.add("dve", lambda e: e.tensor_tensor(E1[:, :], qs[:, :], E0[:, :], ALU.mult), reads=["qs", "E0", "E1"], writes=["E1"])
        p.add("pool", lambda e, qhA=qhA: e.tensor_tensor(qhA[:, :], E1[:, :], cmA, ALU.mult), reads=["E1", "cst"], writes=[("qhA", pp)])
        p.add("dve", lambda e, qhA=qhA, qhB=qhB: e.tensor_tensor(qhB[:, :], E1[:, :], qhA[:, :], ALU.subtract),
              reads=["E1", ("qhA", pp)], writes=[("qhB", pp)])
        yield
        for tl in range(NTL):
            p.add("pe", lambda e, tl=tl: e.transpose(ptr[:, tl * 128:(tl + 1) * 128], vT[:, tl * 128:(tl + 1) * 128], ident[:, :]),
                  reads=["vT", "ident"], writes=["ptr"])
        p.add("dve", lambda e, vtok=vtok: e.tensor_copy(vtok[:, :, :].rearrange("p a b -> p (a b)"), ptr[:, :]),
              reads=["ptr"], writes=[("vtok", pp)])
        yield
        for tl in range(NTL):
            p.add("pe", lambda e, tl=tl: e.transpose(ptr[:, tl * 128:(tl + 1) * 128], kh[:, tl * 128:(tl + 1) * 128], ident[:, :]),
                  reads=["kh", "ident"], writes=["ptr"])
        p.add("dve", lambda e, khA=khA: e.tensor_copy(khA[0:64, :, :].rearrange("p a b -> p (a b)"), ptr[0:64, :]),
              reads=["ptr"], writes=[("khA", pp)])
        p.add("dve", lambda e, khB=khB: e.tensor_copy(khB[64:128, :, :].rearrange("p a b -> p (a b)"), ptr[64:128, :]),
              reads=["ptr"], writes=[("khB", pp)])
        yield
        for g4 in range(NTL // 4):
            for q in range(4):
                c0 = (g4 * 4 + q) * 128
                p.add("pe", lambda e, q=q, c0=c0: e.matmul(psc[:, q * 128:(q + 1) * 128], kt[:, c0:c0 + 128], qt[:, c0:c0 + 128],
                                                           start=True, stop=True),
                      reads=["kt", "qt"], writes=["psc"])
            p.add("dve", lambda e, g4=g4, scm=scm: e.tensor_tensor(
                scm[:, g4 * 4:(g4 + 1) * 4, :], psc[:, :].rearrange("p (a b) -> p a b", b=128),
                bmask[:, :].unsqueeze(1).to_broadcast([128, 4, 128]), ALU.mult),
                reads=["psc", "bmask"], writes=[("scm", pp, g4)])
            yield

    def rec(blk):
        pp = blk % 2
        t0 = blk * TB
        gs, dec, vtok, khA, khB, qhA, qhB, scm = gs2[pp], dec2[pp], vtok2[pp], khA2[pp], khB2[pp], qhA2[pp], qhB2[pp], scm2[pp]
        if blk % H_SEQ_BLOCKS == 0:
            p.add("pool", lambda e: e.memset(Sf[:, :], 0.0), writes=["Sf"])
            si = cnt["S"] % 2
            p.add("pool", lambda e, si=si: e.memset(Sb[si][:, :], 0.0), writes=[("Sb", si)])
        sb_ = 0
        for tl in range(NTL):
            c0 = tl * 128
            if tl % 2 == 0:
                sb_ = cnt["st"] % 2
                cnt["st"] += 1
                for q in range(4):
                    tile = tl + q // 2
                    khx = khA if q % 2 == 0 else khB
                    p.add("pe", lambda e, sb_=sb_, q=q, tile=tile, khx=khx, vtok=vtok: e.matmul(
                        pst[sb_][:, q * 128:(q + 1) * 128], khx[:, tile, :], vtok[:, tile, :], start=True, stop=True),
                        reads=[("khA", pp), ("khB", pp), ("vtok", pp)], writes=[("pst", sb_)])
            ob = (tl // 4) % 2
            oc = (tl % 4) * 128
            okey = ("po", ob)
            p.add("pe", lambda e, ob=ob, oc=oc, tl=tl, vtok=vtok, scm=scm: e.matmul(
                po[ob][:, oc:oc + 128], vtok[:, tl, :], scm[:, tl, :], start=True, stop=False),
                reads=[("vtok", pp), ("scm", pp, tl // 4)], writes=[okey])
            for half in range(2):
                si = cnt["S"] % 2
                qhx = qhA if half == 0 else qhB
                p.add("pe", lambda e, ob=ob, oc=oc, half=half, si=si, c0=c0, qhx=qhx: e.matmul(
                    po[ob][:, oc:oc + 128], Sb[si][:, :], qhx[:, c0:c0 + 128], start=False, stop=(half == 1)),
                    reads=[("Sb", si), ("qhA", pp), ("qhB", pp)], writes=[okey])
                q = (tl % 2) * 2 + half
                ci = tl * 2 + half
                p.add("dve", lambda e, sb_=sb_, q=q, ci=ci, dec=dec: e.scalar_tensor_tensor(
                    Sf[:, :], Sf[:, :], dec[:, ci:ci + 1], pst[sb_][:, q * 128:(q + 1) * 128], ALU.mult, ALU.add),
                    reads=["Sf", ("dec", pp), ("pst", sb_)], writes=["Sf"])
                cnt["S"] += 1
                sn = cnt["S"] % 2
                p.add("act", lambda e, sn=sn: e.activation(Sb[sn][:, :], Sf[:, :], AF.Copy),
                      reads=["Sf"], writes=[("Sb", sn)])
            if tl % 4 == 3:
                o0 = (tl // 4) * 512
                p.add("act", lambda e, ob=ob, o0=o0: e.activation(osb[:, o0:o0 + 512], po[ob][:, :], AF.Copy),
                      reads=[okey], writes=["osb"])
            yield
        for nt in range(TB // 512):
            sl = slice(nt * 512, (nt + 1) * 512)
            s = cnt["osq"] % 2
            cnt["osq"] += 1
            pb = cnt["pin"] % 2
            cnt["pin"] += 1
            p.add("act", lambda e, s=s, sl=sl: e.activation(osq[s][:, :], osb[:, sl], AF.Square),
                  reads=["osb"], writes=[("osq", s)])
            p.add("pe", lambda e, pb=pb, s=s: e.matmul(pin[pb][:, :], ones[:, :], osq[s][:, :], start=True, stop=True),
                  reads=[("osq", s), "ones"], writes=[("pin", pb)])
            p.add("act", lambda e, pb=pb, sl=sl: e.activation(rs[:, sl], pin[pb][:, :], AF.Sqrt, bias=epst[:, 0:1], scale=1.0 / 128.0),
                  reads=[("pin", pb), "eps"], writes=["rs"])
        yield
        p.add("dve", lambda e: e.reciprocal(rs[:, :], rs[:, :]), reads=["rs"], writes=["rs"])
        p.add("dve", lambda e, gs=gs: e.tensor_tensor(rs[:, :], rs[:, :], gs[:, :], ALU.mult), reads=["rs", ("gs", pp)], writes=["rs"])
        of = ofin[pp]
        p.add("dve", lambda e, of=of: e.scalar_tensor_tensor(of[:, :], osb[:, :], ON, rs[:, :], ALU.mult, ALU.mult),
              reads=["osb", "vt", "rs"], writes=[("ofin", pp)])
        stores.append(p.dma("sp", oT[:, t0:t0 + TB], of[:, :], reads=[("ofin", pp)]))
        yield

    def drain(gen):
        for _ in gen:
            pass

    def interleave(ga, gb):
        da = db = False
        while not (da and db):
            if not da:
                try:
                    next(ga)
                except StopIteration:
                    da = True
            if not db:
                try:
                    next(gb)
                except StopIteration:
                    db = True

    drain(prep(0))
    for blk in range(nblocks):
        if blk + 1 < nblocks:
            interleave(rec(blk), prep(blk + 1))
        else:
            drain(rec(blk))
    p.finish(stores)
    return p.emit()


def mixh_consts():
    c = np.zeros((128, HC_N), np.float32)
    s = np.arange(128)[:, None]
    t = np.arange(128)[None, :]
    c[:, HC_MASK:HC_MASK + 128] = ((s // 64 == t // 64) & (t >= s)).astype(np.float32)
    c[:, HC_IDENT:HC_IDENT + 128] = np.eye(128, dtype=np.float32)
    rm = np.ones(H_TB, np.float32)
    rm[0::64] = 0.0
    c[:, HC_RMASK:HC_RMASK + H_TB] = rm[None, :]
    c[:, HC_CMA:HC_CMA + H_TB] = (((np.arange(H_TB) // 64) % 2) == 0).astype(np.float32)[None, :]
    return c


def mixh_inputs(head, w_in, lbp, out_norm, j):
    cols = np.concatenate([np.arange(g * 1024 + head * 128, g * 1024 + head * 128 + 128) for g in range(4)])
    w = w_in[:, cols]
    w = np.ascontiguousarray(w.reshape(8, 128, 512).transpose(1, 0, 2).reshape(128, 4096))
    vec = np.zeros((128, HV_N), np.float32)
    vec[:, 0] = lbp[0, head * 128:(head + 1) * 128]
    vec[:, 1] = lbp[1, head * 128:(head + 1) * 128]
    vec[:, 2] = 1.0 if (0 >= 1 and 0 <= j) else 0.0
    vec[:, 3] = 1.0 if (1 >= 1 and 1 <= j) else 0.0
    vec[:, 4] = out_norm
    return w, vec


M_G = 4
M_TPB = 64
MG_N = 192 * 2 + 32
MC_MASK = 0
MC_ID = 2048
MC_N = 2048 + 128
MAGIC = 12582912.0
TWO_PI = 2.0 * math.pi
C1 = 6.28125
C2 = TWO_PI - C1
PI_CL = 3.1415925


def build_mixm(eps=1e-6, nbatch=2, nq=16, ng=None):
    nc = bass.Bass("TRN2", target_bir_lowering=False)
    latT = nc.dram_tensor("latT", [640, 16384], BF16, kind="ExternalInput").ap()
    posT = nc.dram_tensor("posT", [128, 128], I32, kind="ExternalInput").ap()
    wq = nc.dram_tensor("wq", [128, 2 * 192], F32, kind="ExternalInput").ap()
    wkv = nc.dram_tensor("wkv", [128, 3 * 320], F32, kind="ExternalInput").ap()
    gvec = nc.dram_tensor("gvec", [128, MG_N], F32, kind="ExternalInput").ap()
    consts = nc.dram_tensor("consts", [128, MC_N], F32, kind="ExternalInput").ap()
    oT = nc.dram_tensor("oT", [128, 16384], BF16, kind="ExternalOutput").ap()

    p = Prog(nc)
    G = M_G
    cst = p.sb([128, MC_N], F32, "cst")
    gv = p.sb([128, MG_N], F32, "gv")
    wqs = p.sb([128, 384], F32, "wqs")
    wkvs = p.sb([128, 960], F32, "wkvs")
    wqb = p.sb([128, 384], BF16, "wqb")
    wkvb = p.sb([128, 960], BF16, "wkvb")
    masks = p.sb([128, 4, 512], BF16, "masks")
    ident = p.sb([128, 128], BF16, "ident")
    ones = p.sb([128, 128], BF16, "ones")
    epst = p.sb([128, 1], F32, "epst")
    hpi = p.sb([128, 1], F32, "hpi")
    posi = p.sb([128, 128], I32, "posi")
    posf = p.sb([128, 128], F32, "posf")
    cosT = p.sb([128, M_TPB, 32], F32, "cosT")
    sinT = p.sb([128, M_TPB, 32], F32, "sinT")
    tg1 = p.sb([128, M_TPB, 32], F32, "tg1")
    tg2 = p.sb([128, M_TPB, 32], F32, "tg2")
    latb = [p.sb([128, 5, 512], BF16, f"latb{i}") for i in range(2)]
    junk = p.sb([128, 192], F32, "junk")
    ss = p.sb([128, 8], F32, "ss")
    qn = p.sb([128, G, 192], F32, "qn")
    kn = p.sb([128, G, 192], F32, "kn")
    trq = [p.sb([128, G, 32], F32, f"trq{i}") for i in range(4)]
    trk = [p.sb([128, G, 32], F32, f"trk{i}") for i in range(4)]
    qf = p.sb([128, G, 128], BF16, "qf")
    kf = p.sb([128, G, 128], BF16, "kf")
    qfr = p.sb([128, G, 128], BF16, "qfr")
    kfr = p.sb([128, G, 128], BF16, "kfr")
    qTm = p.sb([128, 8192], BF16, "qTm")
    qTr = p.sb([128, 8192], BF16, "qTr")
    kTm = p.sb([128, 8192], BF16, "kTm")
    kTr = p.sb([128, 8192], BF16, "kTr")
    vtok = p.sb([128, M_TPB, 128], BF16, "vtok")
    PT = [p.sb([128, 512], BF16, f"PT{i}") for i in range(3)]
    rl = p.sb([128, 512], F32, "rl")
    accL = [p.sb([128, 512], F32, f"accL{i}") for i in range(2)]
    onesf = p.sb([128, 128], F32, "onesf")
    of = [p.sb([128, 512], BF16, f"of{i}") for i in range(2)]

    B = [p.ps(name=f"B{i}") for i in range(6)]
    ptrQ = p.ps([128, 1024], BF16, "ptrQ")
    ptrK = p.ps([128, 1024], BF16, "ptrK")

    p.excl = set([("B", i) for i in range(6)] + ["ptrQ", "ptrK"])
    p.dma("sp", cst[:, :], consts[:, :], writes=["cst"])
    p.dma("sp", gv[:, :], gvec[:, :], writes=["gv"])
    p.dma("sp", wqs[:, :], wq[:, :], writes=["wqs"])
    p.dma("sp", wkvs[:, :], wkv[:, :], writes=["wkvs"])
    p.dma("sp", posi[:, :], posT[:, :], writes=["posi"])
    p.add("pool", lambda e: e.tensor_copy(wqb[:, :], wqs[:, :]), reads=["wqs"], writes=["wqb"])
    p.add("pool", lambda e: e.tensor_copy(wkvb[:, :], wkvs[:, :]), reads=["wkvs"], writes=["wkvb"])
    p.add("dve", lambda e: e.tensor_copy(masks[:, :, :].rearrange("p a b -> p (a b)"), cst[:, MC_MASK:MC_MASK + 2048]),
          reads=["cst"], writes=["masks"])
    p.add("dve", lambda e: e.tensor_copy(ident[:, :], cst[:, MC_ID:MC_ID + 128]), reads=["cst"], writes=["ident"])
    p.add("dve", lambda e: e.tensor_copy(posf[:, :], posi[:, :]), reads=["posi"], writes=["posf"])
    p.add("pool", lambda e: e.memset(ones[:, :], 1.0), writes=["ones"])
    p.add("pool", lambda e: e.memset(onesf[:, :], 1.0), writes=["onesf"])
    p.add("pool", lambda e: e.memset(epst[:, :], eps), writes=["eps"])
    p.add("pool", lambda e: e.memset(hpi[:, :], math.pi / 2.0), writes=["hpi"])
    p.add("pool", lambda e: e.memset(qfr[:, :, :], 0.0), writes=["qfr"])
    p.add("pool", lambda e: e.memset(kfr[:, :, :], 0.0), writes=["kfr"])
    p.add("dve", lambda e: e.tensor_scalar(gv[:, 0:192], gv[:, 0:192], 192.0 ** -0.5, None, ALU.mult),
          reads=["gv"], writes=["gv"])
    GQ = gv[:, 0:192]
    GK = gv[:, 192:384]
    INVF = gv[:, 384:416]

    cnt = {"lat": 0, "pt": 0, "sc": 0, "of": 0}
    stores = []

    for b in range(nbatch):
        i0 = b * M_TPB
        p.add("dve", lambda e, i0=i0: e.tensor_tensor(
            tg1[:, :, :], posf[:, i0:i0 + M_TPB].unsqueeze(2).to_broadcast([128, M_TPB, 32]),
            INVF.unsqueeze(1).to_broadcast([128, M_TPB, 32]), ALU.mult),
            reads=["posf", "gv"], writes=["tg1"])
        p.add("dve", lambda e: e.tensor_scalar(tg2[:, :, :], tg1[:, :, :], 1.0 / TWO_PI, MAGIC, ALU.mult, ALU.add),
              reads=["tg1"], writes=["tg2"])
        p.add("dve", lambda e: e.tensor_scalar(tg2[:, :, :], tg2[:, :, :], -MAGIC, None, ALU.add),
              reads=["tg2"], writes=["tg2"])
        p.add("dve", lambda e: e.scalar_tensor_tensor(tg1[:, :, :], tg2[:, :, :], -C1, tg1[:, :, :], ALU.mult, ALU.add),
              reads=["tg1", "tg2"], writes=["tg1"])
        p.add("dve", lambda e: e.scalar_tensor_tensor(tg1[:, :, :], tg2[:, :, :], -C2, tg1[:, :, :], ALU.mult, ALU.add),
              reads=["tg1", "tg2"], writes=["tg1"])
        p.add("dve", lambda e: e.tensor_scalar(tg1[:, :, :], tg1[:, :, :], -PI_CL, PI_CL, ALU.max, ALU.min),
              reads=["tg1"], writes=["tg1"])
        p.add("act", lambda e: e.activation(sinT[:, :, :], tg1[:, :, :], AF.Sin), reads=["tg1"], writes=["sinT"])
        p.add("dve", lambda e: e.tensor_scalar(tg2[:, :, :], tg1[:, :, :], -1.0, None, ALU.mult),
              reads=["tg1", "tg2"], writes=["tg2"])
        p.add("dve", lambda e: e.tensor_tensor(tg2[:, :, :], tg2[:, :, :], tg1[:, :, :], ALU.max),
              reads=["tg1", "tg2"], writes=["tg2"])
        p.add("act", lambda e: e.activation(cosT[:, :, :], tg2[:, :, :], AF.Sin, bias=hpi[:, 0:1], scale=-1.0),
              reads=["tg2", "hpi"], writes=["cosT"])
        for g in range(M_TPB // G if ng is None else ng):
            tok0 = b * 8192 + g * 512
            lb = latb[cnt["lat"] % 2]
            lk = ("latb", cnt["lat"] % 2)
            cnt["lat"] += 1
            for k in range(5):
                p.dma("sp", lb[:, k, :], latT[k * 128:(k + 1) * 128, tok0:tok0 + 512], writes=[(lk, k)])
            for q in range(G):
                ts_ = slice(q * 128, (q + 1) * 128)
                for k in range(2):
                    p.add("pe", lambda e, q=q, k=k, lb=lb, ts_=ts_: e.matmul(
                        B[q // 2][:, (q % 2) * 192:(q % 2) * 192 + 192], lb[:, k, ts_], wqb[:, k * 192:(k + 1) * 192],
                        start=(k == 0), stop=(k == 1)),
                        reads=[(lk, k), "wqb"], writes=[("B", q // 2)])
                for k in range(3):
                    p.add("pe", lambda e, q=q, k=k, lb=lb, ts_=ts_: e.matmul(
                        B[2 + q][:, 0:320], lb[:, 2 + k, ts_], wkvb[:, k * 320:(k + 1) * 320],
                        start=(k == 0), stop=(k == 2)),
                        reads=[(lk, 2 + k), "wkvb"], writes=[("B", 2 + q)])
            for q in range(G):
                qsrc = B[q // 2][:, (q % 2) * 192:(q % 2) * 192 + 192]
                ksrc = B[2 + q][:, 0:192]
                p.add("act", lambda e, q=q, qsrc=qsrc: e.activation(junk[:, :], qsrc, AF.Square, accum_out=ss[:, q:q + 1]),
                      reads=[("B", q // 2)], writes=["junk", "ss"])
                p.add("act", lambda e, q=q, ksrc=ksrc: e.activation(junk[:, :], ksrc, AF.Square, accum_out=ss[:, 4 + q:5 + q]),
                      reads=[("B", 2 + q)], writes=["junk", "ss"])
            p.add("act", lambda e: e.activation(ss[:, :], ss[:, :], AF.Sqrt, bias=epst[:, 0:1], scale=1.0 / 192.0),
                  reads=["ss", "eps"], writes=["ss"])
            p.add("dve", lambda e: e.reciprocal(ss[:, :], ss[:, :]), reads=["ss"], writes=["ss"])
            for q in range(G):
                qsrc = B[q // 2][:, (q % 2) * 192:(q % 2) * 192 + 192]
                ksrc = B[2 + q][:, 0:192]
                p.add("dve", lambda e, q=q, qsrc=qsrc: e.scalar_tensor_tensor(qn[:, q, :], qsrc, ss[:, q:q + 1], GQ, ALU.mult, ALU.mult),
                      reads=[("B", q // 2), "ss", "gv"], writes=["qn"])
                p.add("dve", lambda e, q=q, ksrc=ksrc: e.scalar_tensor_tensor(kn[:, q, :], ksrc, ss[:, 4 + q:5 + q], GK, ALU.mult, ALU.mult),
                      reads=[("B", 2 + q), "ss", "gv"], writes=["kn"])
                tile = g * G + q
                p.add("act", lambda e, q=q, tile=tile: e.activation(vtok[:, tile, :], B[2 + q][:, 192:320], AF.Copy),
                      reads=[("B", 2 + q)], writes=["vtok"])
            ti0 = g * G
            cs = cosT[:, ti0:ti0 + G, :]
            sn = sinT[:, ti0:ti0 + G, :]
            for (src, skey, dst, dkey, dstr, drkey, eng, tr, tn) in ((qn, "qn", qf, "qf", qfr, "qfr", "dve", trq, "trq"), (kn, "kn", kf, "kf", kfr, "kfr", "pool", trk, "trk")):
                x1 = src[:, :, 128:160]
                x2 = src[:, :, 160:192]
                p.add("act", lambda e, src=src, dst=dst: e.activation(dst[:, :, :], src[:, :, 0:128], AF.Copy),
                      reads=[skey], writes=[dkey])
                p.add(eng, lambda e, x1=x1, tr=tr, cs=cs: e.tensor_tensor(tr[0][:, :, :], x1, cs, ALU.mult), reads=[skey, "cosT"], writes=[(tn, 0)])
                p.add(eng, lambda e, x2=x2, tr=tr, sn=sn: e.tensor_tensor(tr[1][:, :, :], x2, sn, ALU.mult), reads=[skey, "sinT"], writes=[(tn, 1)])
                p.add(eng, lambda e, dstr=dstr, tr=tr: e.tensor_tensor(dstr[:, :, 0:32], tr[0][:, :, :], tr[1][:, :, :], ALU.subtract),
                      reads=[(tn, 0), (tn, 1)], writes=[drkey])
                p.add(eng, lambda e, x2=x2, tr=tr, cs=cs: e.tensor_tensor(tr[2][:, :, :], x2, cs, ALU.mult), reads=[skey, "cosT"], writes=[(tn, 2)])
                p.add(eng, lambda e, x1=x1, tr=tr, sn=sn: e.tensor_tensor(tr[3][:, :, :], x1, sn, ALU.mult), reads=[skey, "sinT"], writes=[(tn, 3)])
                p.add(eng, lambda e, dstr=dstr, tr=tr: e.tensor_tensor(dstr[:, :, 32:64], tr[2][:, :, :], tr[3][:, :, :], ALU.add),
                      reads=[(tn, 2), (tn, 3)], writes=[drkey])
            lt0 = g * 512
            for (srcf, skey, srcr, srkey, ptr_, pkey, dm, dmk, dr, drk) in ((qf, "qf", qfr, "qfr", ptrQ, "ptrQ", qTm, "qTm", qTr, "qTr"),
                                                                              (kf, "kf", kfr, "kfr", ptrK, "ptrK", kTm, "kTm", kTr, "kTr")):
                for q in range(G):
                    p.add("pe", lambda e, q=q, srcf=srcf, ptr_=ptr_: e.transpose(
                        ptr_[:, q * 128:(q + 1) * 128], srcf[:, q, :], ident[:, :]),
                        reads=[skey, "ident"], writes=[pkey])
                for q in range(G):
                    p.add("pe", lambda e, q=q, srcr=srcr, ptr_=ptr_: e.transpose(
                        ptr_[:, 512 + q * 128:512 + (q + 1) * 128], srcr[:, q, :], ident[:, :]),
                        reads=[srkey, "ident"], writes=[pkey])
                p.add("dve", lambda e, ptr_=ptr_, dm=dm, lt0=lt0: e.tensor_copy(dm[:, lt0:lt0 + 512], ptr_[:, 0:512]),
                      reads=[pkey], writes=[dmk])
                p.add("act", lambda e, ptr_=ptr_, dr=dr, lt0=lt0: e.activation(dr[:, lt0:lt0 + 512], ptr_[:, 512:1024], AF.Copy),
                      reads=[pkey], writes=[drk])
        for i in range(nq):
            qs = slice(i * 512, (i + 1) * 512)
            nj = 4 * i + 4
            ob = 2 + (i % 2)
            lbk = 4 + (i % 2)
            al = accL[i % 2]
            alk = ("accL", i % 2)
            scs = {}

            def qk(j):
                sc = cnt["sc"] % 2
                cnt["sc"] += 1
                scs[j] = sc
                ks = slice(j * 128, (j + 1) * 128)
                p.add("pe", lambda e, sc=sc, ks=ks, qs=qs: e.matmul(B[sc][:, :], kTm[:, ks], qTm[:, qs], start=True, stop=False),
                      reads=["kTm", "qTm"], writes=[("B", sc)])
                p.add("pe", lambda e, sc=sc, ks=ks, qs=qs: e.matmul(B[sc][:, :], kTr[:, ks], qTr[:, qs], start=False, stop=True),
                      reads=["kTr", "qTr"], writes=[("B", sc)])

            qk(0)
            for j in range(nj):
                if j + 1 < nj:
                    qk(j + 1)
                sc = scs[j]
                r = cnt["pt"] % 3
                cnt["pt"] += 1
                p.add("act", lambda e, sc=sc, r=r: e.activation(PT[r][:, :], B[sc][:, :], AF.Exp),
                      reads=[("B", sc)], writes=[("PT", r)])
                if j >= 4 * i:
                    m = j - 4 * i
                    p.add("pool", lambda e, r=r, m=m: e.tensor_tensor(PT[r][:, :], PT[r][:, :], masks[:, m, :], ALU.mult),
                          reads=[("PT", r), "masks"], writes=[("PT", r)])
                p.add("pe", lambda e, ob=ob, j=j, r=r, nj=nj: e.matmul(B[ob][:, :], vtok[:, j, :], PT[r][:, :],
                                                                  start=(j == 0), stop=(j == nj - 1)),
                      reads=["vtok", ("PT", r)], writes=[("B", ob)])
                if j == 0:
                    p.add("dve", lambda e, al=al, r=r: e.tensor_copy(al[:, :], PT[r][:, :]), reads=[("PT", r)], writes=[alk])
                else:
                    p.add("dve", lambda e, al=al, r=r: e.tensor_tensor(al[:, :], al[:, :], PT[r][:, :], ALU.add),
                          reads=[("PT", r), alk], writes=[alk])
            p.add("pe", lambda e, lbk=lbk, al=al: e.matmul(B[lbk][:, :], onesf[:, :], al[:, :], start=True, stop=True),
                  reads=["onesf", alk], writes=[("B", lbk)])
            p.add("dve", lambda e, lbk=lbk: e.reciprocal(rl[:, :], B[lbk][:, :]), reads=[("B", lbk)], writes=["rl"])
            oi = cnt["of"] % 2
            cnt["of"] += 1
            p.add("dve", lambda e, ob=ob, oi=oi: e.tensor_tensor(of[oi][:, :], B[ob][:, :], rl[:, :], ALU.mult),
                  reads=[("B", ob), "rl"], writes=[("of", oi)])
            t0 = b * 8192 + i * 512
            stores.append(p.dma("sp", oT[:, t0:t0 + 512], of[oi][:, :], reads=[("of", oi)], force=True))
    p.finish(stores)
    return p.emit()


def mixm_consts():
    c = np.zeros((128, MC_N), np.float32)
    k = np.arange(128)[:, None]
    q = np.arange(512)[None, :]
    for m in range(4):
        c[:, MC_MASK + m * 512:MC_MASK + (m + 1) * 512] = (q >= k + 128 * m).astype(np.float32)
    c[:, MC_ID:MC_ID + 128] = np.eye(128, dtype=np.float32)
    return c


def mixm_inputs(head, w_q_up, w_kv_up, q_norm, k_norm):
    wq = w_q_up[:, head * 192:(head + 1) * 192]
    wq = np.ascontiguousarray(wq.reshape(2, 128, 192).transpose(1, 0, 2).reshape(128, 384))
    wkv_h = w_kv_up[:, head * 256:(head + 1) * 256]
    ext = np.zeros((3, 128, 320), np.float32)
    for k in range(2):
        ext[k, :, 0:128] = wkv_h[k * 128:(k + 1) * 128, 0:128]
        ext[k, :, 192:320] = wkv_h[k * 128:(k + 1) * 128, 128:256]
    ext[2, np.arange(64), 128 + np.arange(64)] = 1.0
    wkv = np.ascontiguousarray(ext.transpose(1, 0, 2).reshape(128, 960))
    gvec = np.zeros((128, MG_N), np.float32)
    gvec[:, 0:192] = q_norm[None, :]
    gvec[:, 192:384] = k_norm[None, :]
    inv_freq = (10000.0 ** (-np.arange(0, 64, 2, dtype=np.float32) / 64)).astype(np.float32)
    gvec[:, 384:416] = inv_freq[None, :]
    return wq, wkv, gvec


import ml_dtypes

_PROGS = {}
N_CORES = 8


def _prog(name):
    if name not in _PROGS:
        if name == "pre_h":
            _PROGS[name] = build_pre("h")
        elif name == "pre_m":
            _PROGS[name] = build_pre("m")
        elif name == "mix_h":
            _PROGS[name] = build_mixh()
        elif name == "mix_m":
            _PROGS[name] = build_mixm()
        elif name == "post":
            _PROGS[name] = build_post()
    return _PROGS[name]


def _run(name, ins):
    res = run_bass_kernel_spmd(_prog(name), ins, core_ids=list(range(N_CORES)))
    return res.results


def kernel(x, positions, norm_mix, norm_ffn, hgrn_w_in, hgrn_lower_bounds, hgrn_out_norm, hgrn_w_out,
           mla_w_in, mla_q_a_norm, mla_w_q_up, mla_kv_a_norm, mla_w_kv_up, mla_q_norm, mla_k_norm,
           mla_w_out, ffn_w_up, ffn_conv_w, ffn_conv_b, ffn_w_down):
    f32 = np.float32
    x = np.asarray(x, f32)
    B, S, D = x.shape
    T = B * S
    TPC = T // N_CORES
    xT = np.ascontiguousarray(x.reshape(T, D).T)
    pos = np.asarray(positions).reshape(-1).astype(np.int32)
    posT = np.ascontiguousarray(pos.reshape(T // 128, 128).T)
    bf16 = ml_dtypes.bfloat16
    for l in range(4):
        j = l // 2
        if l % 2 == 0:
            base = pre_inputs(np.asarray(norm_mix[l], f32))
            ins = [dict(base, xT=np.ascontiguousarray(xT[:, c * TPC:(c + 1) * TPC])) for c in range(N_CORES)]
            res = _run("pre_h", ins)
            hT = np.ascontiguousarray(np.concatenate([r["hT"] for r in res], axis=1))
            cst = mixh_consts()
            ins = []
            for c in range(N_CORES):
                w, vec = mixh_inputs(c, np.asarray(hgrn_w_in[j], f32), np.asarray(hgrn_lower_bounds, f32),
                                     np.asarray(hgrn_out_norm[j], f32), j)
                ins.append({"hT": hT, "w_in": w, "consts": cst, "vec": vec})
            res = _run("mix_h", ins)
            w_out = np.asarray(hgrn_w_out[j], f32)
        else:
            base = pre_inputs(np.asarray(norm_mix[l], f32), np.asarray(mla_q_a_norm[j], f32),
                              np.asarray(mla_kv_a_norm[j], f32), np.asarray(mla_w_in[j], f32))
            ins = [dict(base, xT=np.ascontiguousarray(xT[:, c * TPC:(c + 1) * TPC])) for c in range(N_CORES)]
            res = _run("pre_m", ins)
            latT = np.ascontiguousarray(np.concatenate([r["latT"] for r in res], axis=1))
            cst = mixm_consts()
            ins = []
            for c in range(N_CORES):
                wq, wkv, gvec = mixm_inputs(c, np.asarray(mla_w_q_up[j], f32), np.asarray(mla_w_kv_up[j], f32),
                                            np.asarray(mla_q_norm[j], f32), np.asarray(mla_k_norm[j], f32))
                ins.append({"latT": latT, "posT": posT, "wq": wq, "wkv": wkv, "gvec": gvec, "consts": cst})
            res = _run("mix_m", ins)
            w_out = np.asarray(mla_w_out[j], f32)
        oT = np.concatenate([r["oT"] for r in res], axis=0)
        wo, wu, wd, vec = post_layout_weights(w_out, np.asarray(ffn_w_up[l], f32), np.asarray(ffn_w_down[l], f32),
                                              np.asarray(norm_ffn[l], f32), np.asarray(ffn_conv_w[l], f32),
                                              np.asarray(ffn_conv_b[l], f32))
        ins = []
        for c in range(N_CORES):
            v = vec.copy()
            a = c * TPC
            if a % S == 0:
                xs = np.concatenate([np.zeros((D, P_HALO), f32), xT[:, a:a + TPC]], axis=1)
                os_ = np.concatenate([np.zeros((D, P_HALO), bf16), oT[:, a:a + TPC]], axis=1)
                v[:, 8 + 176] = 0.0
            else:
                xs = xT[:, a - P_HALO:a + TPC]
                os_ = oT[:, a - P_HALO:a + TPC]
                v[:, 8 + 176] = 1.0
            ins.append({"xT": np.ascontiguousarray(xs), "oT": np.ascontiguousarray(os_), "w_out": wo, "w_up": wu,
                        "w_down": wd, "vec": v})
        res = _run("post", ins)
        xT = np.ascontiguousarray(np.concatenate([r["yT"] for r in res], axis=1))
    return np.ascontiguousarray(xT.T).reshape(B, S, D).astype(f32)
```
